# Optimizing a Trainium2 kernel written in Bass

```python
import math
import jax, jax.numpy as jnp
from jax import lax
import numpy as np

D_MODEL = 1024
BATCH = 8
SEQ = 4096
DEPTH = 4

GDN_HEADS = 4
GDN_HEAD_DIM = 128
CONV_WIDTH = 4
GDN_CHUNK = 64
DIFF_HEADS = 4
DIFF_QK_DIM = 64
DIFF_V_DIM = 2 * DIFF_QK_DIM
FOX_HEADS = 8
FOX_HEAD_DIM = D_MODEL // FOX_HEADS
D_FF = 4 * D_MODEL
PLE_DIM = 256
ROPE_THETA = 10000.0
Q_BLOCK = 128
EPS = 1e-6
NEG_INF = -1e30

N_EVEN = (DEPTH + 1) // 2
N_ODD = DEPTH // 2

GDN_QK = GDN_HEADS * GDN_HEAD_DIM
GDN_V = GDN_HEADS * GDN_HEAD_DIM
DIFF_Q = DIFF_HEADS * 2 * DIFF_QK_DIM
DIFF_V = DIFF_HEADS * DIFF_V_DIM
EVEN_SPLITS = [3 * GDN_QK, GDN_V, GDN_HEADS, GDN_HEADS, DIFF_Q, DIFF_Q, DIFF_V]
EVEN_IN = sum(EVEN_SPLITS)
EVEN_MIX = GDN_V + DIFF_V
ODD_MIX = FOX_HEADS * FOX_HEAD_DIM
ODD_SPLITS = [ODD_MIX, ODD_MIX, ODD_MIX, ODD_MIX, FOX_HEADS]
ODD_IN = sum(ODD_SPLITS)

kernel_name = 'hybrid_gdn_diffattn_fox_trunk'


def split_cols(t, sizes):
    offs = np.concatenate([[0], np.cumsum(sizes)]).tolist()
    return [t[..., offs[i]:offs[i + 1]] for i in range(len(sizes))]


def rms_norm(x, w):
    xf = x.astype(jnp.float32)
    y = xf * lax.rsqrt(jnp.mean(xf * xf, axis=-1, keepdims=True) + EPS)
    return (y * w.astype(jnp.float32)).astype(x.dtype)


def l2_norm(x):
    return x * lax.rsqrt(jnp.sum(x * x, axis=-1, keepdims=True) + EPS)


def rope_tables(positions, dim):
    inv_freq = ROPE_THETA ** (-jnp.arange(0, dim, 2, dtype=jnp.float32) / dim)
    ang = positions.astype(jnp.float32)[..., None] * inv_freq
    return jnp.cos(ang), jnp.sin(ang)


def apply_rope(x, cos, sin):
    shape = cos.shape[:2] + (1,) * (x.ndim - 3) + cos.shape[2:]
    c, s = cos.reshape(shape), sin.reshape(shape)
    x1, x2 = jnp.split(x.astype(jnp.float32), 2, axis=-1)
    return jnp.concatenate([x1 * c - x2 * s, x2 * c + x1 * s], axis=-1)


def causal_conv(x, w):
    K = w.shape[0]
    T = x.shape[1]
    xp = jnp.pad(x, ((0, 0), (K - 1, 0), (0, 0)))
    return sum(xp[:, i:i + T] * w[i] for i in range(K))


def causal_mask(start, T):
    return (start + jnp.arange(Q_BLOCK))[:, None] >= jnp.arange(T)[None, :]


def block_starts(T):
    return jnp.arange(T // Q_BLOCK) * Q_BLOCK


def gated_delta_rule(q, k, v, g, beta):
    B, H, T, dk = q.shape
    dv = v.shape[-1]
    C = GDN_CHUNK
    N = T // C
    chunk = lambda t: t.reshape(B, H, N, C, *t.shape[3:])
    q = chunk(q * dk ** -0.5)
    k = chunk(k)
    v = chunk(v)
    beta = chunk(beta)
    gc = jnp.cumsum(chunk(g), axis=-1)
    incl = jnp.tril(jnp.ones((C, C), dtype=bool))
    strict = jnp.tril(jnp.ones((C, C), dtype=bool), -1)
    decay = jnp.where(incl, jnp.exp(jnp.where(incl, gc[..., :, None] - gc[..., None, :], 0.0)), 0.0)
    kb = k * beta[..., None]
    kk = jnp.einsum('bhncd,bhnsd->bhncs', kb, k) * decay
    a_mat = jnp.where(strict, kk, 0.0) + jnp.eye(C, dtype=kk.dtype)
    rhs = jnp.concatenate([v * beta[..., None], kb * jnp.exp(gc)[..., None]], axis=-1)
    sol = lax.linalg.triangular_solve(a_mat, rhs, left_side=True, lower=True, unit_diagonal=True)
    u, w = sol[..., :dv], sol[..., dv:]
    qk = jnp.where(incl, jnp.einsum('bhncd,bhnsd->bhncs', q, k) * decay, 0.0)
    q_dec = q * jnp.exp(gc)[..., None]
    k_dec = k * jnp.exp(gc[..., -1:] - gc)[..., None]
    g_last = jnp.exp(gc[..., -1])

    def step(S, xs):
        qd, kd, uc, wc, qkc, gl = xs
        v_new = uc - jnp.einsum('bhcd,bhde->bhce', wc, S)
        o = jnp.einsum('bhcd,bhde->bhce', qd, S) + jnp.einsum('bhcs,bhse->bhce', qkc, v_new)
        S = S * gl[..., None, None] + jnp.einsum('bhcd,bhce->bhde', kd, v_new)
        return S, o

    xs = tuple(jnp.moveaxis(t, 2, 0) for t in (q_dec, k_dec, u, w, qk, g_last))
    S0 = jnp.zeros((B, H, dk, dv), jnp.float32)
    _, o = lax.scan(step, S0, xs)
    return jnp.moveaxis(o, 0, 2).reshape(B, H, T, dv)


def diff_attention(q, k, v, lam):
    B, T, H, _, d = q.shape
    dv = v.shape[-1]
    qh = q.transpose(0, 2, 3, 1, 4).astype(jnp.float32) * d ** -0.5
    kh = k.transpose(0, 2, 3, 1, 4).astype(jnp.float32)
    vh = v.transpose(0, 2, 1, 3).astype(jnp.float32)

    def block(start):
        qb = lax.dynamic_slice_in_dim(qh, start, Q_BLOCK, axis=3)
        s = jnp.einsum('bhmqd,bhmkd->bhmqk', qb, kh)
        a = jax.nn.softmax(jnp.where(causal_mask(start, T), s, NEG_INF), axis=-1)
        a = a[:, :, 0] - lam * a[:, :, 1]
        return jnp.einsum('bhqk,bhkd->bhqd', a, vh)

    o = lax.map(block, block_starts(T))
    return o.transpose(1, 0, 3, 2, 4).reshape(B, T, H, dv)


def even_mixer(h, cos, sin, w_in, conv_w, a_log, dt_bias, gdn_norm_w,
               lam_q1, lam_k1, lam_q2, lam_k2, diff_norm_w, w_out, lambda_init):
    B, T, _ = h.shape
    qkv_a, z_a, b_a, a_a, q_b, k_b, v_b = split_cols(h @ w_in, EVEN_SPLITS)
    qkv_a = jax.nn.silu(causal_conv(qkv_a, conv_w))
    q_a, k_a, v_a = split_cols(qkv_a, [GDN_QK, GDN_QK, GDN_V])
    to_heads = lambda t: t.reshape(B, T, GDN_HEADS, GDN_HEAD_DIM).transpose(0, 2, 1, 3).astype(jnp.float32)
    q_a = l2_norm(to_heads(q_a))
    k_a = l2_norm(to_heads(k_a))
    v_a = to_heads(v_a)
    beta = jax.nn.sigmoid(b_a.astype(jnp.float32)).transpose(0, 2, 1)
    g = (-jnp.exp(a_log.astype(jnp.float32))
         * jax.nn.softplus(a_a.astype(jnp.float32) + dt_bias.astype(jnp.float32))).transpose(0, 2, 1)
    o_a = gated_delta_rule(q_a, k_a, v_a, g, beta).transpose(0, 2, 1, 3)
    z = z_a.reshape(B, T, GDN_HEADS, GDN_HEAD_DIM).astype(jnp.float32)
    o_a = rms_norm(o_a, gdn_norm_w) * jax.nn.silu(z)
    q_b = apply_rope(q_b.reshape(B, T, DIFF_HEADS, 2, DIFF_QK_DIM), cos, sin)
    k_b = apply_rope(k_b.reshape(B, T, DIFF_HEADS, 2, DIFF_QK_DIM), cos, sin)
    v_b = v_b.reshape(B, T, DIFF_HEADS, DIFF_V_DIM)
    lam = (jnp.exp(jnp.sum(lam_q1.astype(jnp.float32) * lam_k1.astype(jnp.float32)))
           - jnp.exp(jnp.sum(lam_q2.astype(jnp.float32) * lam_k2.astype(jnp.float32))) + lambda_init)
    o_b = diff_attention(q_b, k_b, v_b, lam)
    o_b = rms_norm(o_b, diff_norm_w) * (1.0 - lambda_init)
    o = jnp.concatenate([o_a.reshape(B, T, GDN_V), o_b.reshape(B, T, DIFF_V)], axis=-1)
    return o.astype(h.dtype) @ w_out


def fox_mixer(h, w_in, b_forget, w_out):
    B, T, _ = h.shape
    q, k, v, gate, f = split_cols(h @ w_in, ODD_SPLITS)
    heads = lambda t: t.reshape(B, T, FOX_HEADS, FOX_HEAD_DIM).transpose(0, 2, 1, 3).astype(jnp.float32)
    qh = heads(q) * FOX_HEAD_DIM ** -0.5
    kh = heads(k)
    vh = heads(v)
    log_f = jax.nn.log_sigmoid(f.astype(jnp.float32) + b_forget.astype(jnp.float32))
    cum = jnp.cumsum(log_f, axis=1).transpose(0, 2, 1)

    def block(start):
        qb = lax.dynamic_slice_in_dim(qh, start, Q_BLOCK, axis=2)
        cb = lax.dynamic_slice_in_dim(cum, start, Q_BLOCK, axis=2)
        s = jnp.einsum('bhqd,bhkd->bhqk', qb, kh) + cb[..., :, None] - cum[:, :, None, :]
        a = jax.nn.softmax(jnp.where(causal_mask(start, T), s, NEG_INF), axis=-1)
        return jnp.einsum('bhqk,bhkd->bhqd', a, vh)

    o = lax.map(block, block_starts(T))
    o = o.transpose(1, 0, 3, 2, 4).reshape(B, T, ODD_MIX)
    o = o * jax.nn.sigmoid(gate.astype(jnp.float32))
    return o.astype(h.dtype) @ w_out


def setup_inputs(seed: int = 0) -> dict:
    key = jax.random.key(seed)
    ks = jax.random.split(key, 32)
    nrm = lambda k, shape, scale: jax.random.normal(k, shape, jnp.float32) * scale
    gain = lambda k, shape: 1.0 + 0.02 * jax.random.normal(k, shape, jnp.float32)
    dt = jnp.exp(jax.random.uniform(ks[7], (N_EVEN, GDN_HEADS), jnp.float32,
                                    math.log(0.001), math.log(0.1)))
    res_scale = (2.0 * DEPTH) ** -0.5
    return {
        'x': nrm(ks[0], (BATCH, SEQ, D_MODEL), 1.0),
        'p': nrm(ks[1], (DEPTH, BATCH, SEQ, PLE_DIM), 1.0),
        'positions': jnp.broadcast_to(jnp.arange(SEQ, dtype=jnp.int32), (BATCH, SEQ)),
        'norm_mix': gain(ks[2], (DEPTH, D_MODEL)),
        'norm_mlp': gain(ks[3], (DEPTH, D_MODEL)),
        'norm_final': gain(ks[4], (D_MODEL,)),
        'w_in_even': nrm(ks[5], (N_EVEN, D_MODEL, EVEN_IN), D_MODEL ** -0.5),
        'conv_w': nrm(ks[6], (N_EVEN, CONV_WIDTH, 3 * GDN_QK), CONV_WIDTH ** -0.5),
        'a_log': jnp.log(jax.random.uniform(ks[8], (N_EVEN, GDN_HEADS), jnp.float32, 1.0, 16.0)),
        'dt_bias': dt + jnp.log(-jnp.expm1(-dt)),
        'gdn_norm': gain(ks[9], (N_EVEN, GDN_HEAD_DIM)),
        'lam_q1': nrm(ks[10], (N_EVEN, DIFF_QK_DIM), 0.1),
        'lam_k1': nrm(ks[11], (N_EVEN, DIFF_QK_DIM), 0.1),
        'lam_q2': nrm(ks[12], (N_EVEN, DIFF_QK_DIM), 0.1),
        'lam_k2': nrm(ks[13], (N_EVEN, DIFF_QK_DIM), 0.1),
        'diff_norm': gain(ks[14], (N_EVEN, DIFF_V_DIM)),
        'w_out_even': nrm(ks[15], (N_EVEN, EVEN_MIX, D_MODEL), EVEN_MIX ** -0.5 * res_scale),
        'w_in_odd': nrm(ks[16], (N_ODD, D_MODEL, ODD_IN), D_MODEL ** -0.5),
        'b_forget': jax.random.uniform(ks[17], (N_ODD, FOX_HEADS), jnp.float32, 1.0, 4.0),
        'w_out_odd': nrm(ks[18], (N_ODD, ODD_MIX, D_MODEL), ODD_MIX ** -0.5 * res_scale),
        'w_mlp_up': nrm(ks[19], (DEPTH, D_MODEL, D_FF), D_MODEL ** -0.5),
        'w_mlp_down': nrm(ks[20], (DEPTH, D_FF, D_MODEL), D_FF ** -0.5 * res_scale),
        'w_ple_proj': nrm(ks[21], (DEPTH, PLE_DIM, D_MODEL), PLE_DIM ** -0.5 * res_scale),
        'w_ple_gate': nrm(ks[22], (DEPTH, D_MODEL, D_MODEL), D_MODEL ** -0.5),
    }


def reference(x, p, positions, norm_mix, norm_mlp, norm_final, w_in_even, conv_w, a_log, dt_bias,
              gdn_norm, lam_q1, lam_k1, lam_q2, lam_k2, diff_norm, w_out_even, w_in_odd, b_forget,
              w_out_odd, w_mlp_up, w_mlp_down, w_ple_proj, w_ple_gate):
    cos, sin = rope_tables(positions, DIFF_QK_DIM)
    for i in range(DEPTH):
        j = i // 2
        h = rms_norm(x, norm_mix[i])
        if i % 2 == 0:
            lambda_init = 0.8 - 0.6 * math.exp(-0.3 * i)
            y = even_mixer(h, cos, sin, w_in_even[j], conv_w[j], a_log[j], dt_bias[j], gdn_norm[j],
                           lam_q1[j], lam_k1[j], lam_q2[j], lam_k2[j], diff_norm[j], w_out_even[j],
                           lambda_init)
        else:
            y = fox_mixer(h, w_in_odd[j], b_forget[j], w_out_odd[j])
        x = x + y
        h = rms_norm(x, norm_mlp[i])
        x = x + jnp.square(jax.nn.relu(h @ w_mlp_up[i])) @ w_mlp_down[i]
        x = x + (p[i] @ w_ple_proj[i]) * jax.nn.sigmoid(x @ w_ple_gate[i])
    return rms_norm(x, norm_final)
```

```python
import math
import os
from contextlib import ExitStack
import numpy as np
import concourse.bass as bass
import concourse.mybir as mybir
from concourse.bass_utils import run_bass_kernel_spmd

F32 = mybir.dt.float32
BF16 = mybir.dt.bfloat16
I32 = mybir.dt.int32
ALU = mybir.AluOpType
AF = mybir.ActivationFunctionType
AX = mybir.AxisListType

D = 1024
DFF = 4096
DEPTH = 4
EVEN_IN = 3592
ODD_IN = 4104
EPS = 1e-6
NEG = -30000.0


class Tok:
    __slots__ = ("name", "w", "r")

    def __init__(self, name=""):
        self.name = name
        self.w = None
        self.r = []


class Buf:
    def __init__(self, t, n=1, name=""):
        self.t = t
        self.toks = [Tok(f"{name}{i}") for i in range(n)]
        self.tok = self.toks[0]


class Sched:
    NSLOT = 10
    NSPARE = 76
    LIMIT = int(os.environ.get("K_LIMIT", "12000"))

    def __init__(self, nc):
        self.nc = nc
        self.sems = []
        self.eng = {}
        self._ctx = []
        names = ["pe", "dve", "act", "pool", "sp"]
        handles = [nc.tensor, nc.vector, nc.scalar, nc.gpsimd, nc.sync]
        for n, h in zip(names, handles):
            self.eng[n] = dict(h=h, sem=self._newsem(n), cnt=0, clock=None, dslots=[], dnext=0)
        for n in ["sp", "pool"]:
            e = self.eng[n]
            for i in range(self.NSLOT):
                e["dslots"].append(dict(sem=self._newsem(f"{n}d{i}"), cnt=0))
        self.spare = [self._newsem(f"sp{i}") for i in range(self.NSPARE)]
        self.final = {}
        ns = len(self.sems)
        for e in self.eng.values():
            e["clock"] = np.zeros(ns, dtype=np.int64)
            e["own"] = {e["sem"]}
        self.nwaits = 0
        self.ninst = 0
        self._war = []
        self.psn = 0

    def _newsem(self, name):
        cm = self.nc.semaphore(name)
        h = cm.__enter__()
        self._ctx.append(cm)
        self.sems.append(h)
        return len(self.sems) - 1

    def _deps(self, reads, writes):
        deps = []
        for t in reads:
            if t.w is not None:
                deps.append(t.w)
        self._war = []
        for t in writes:
            if t.w is not None:
                deps.append(t.w)
            self._war.extend(t.r)
        return deps

    def _wait_for(self, en, deps):
        e = self.eng[en]
        clock = e["clock"]
        own = e["own"]
        war = [d for d in self._war if d[0] not in own]
        self._war = []
        need = {}
        for (s, v, ck) in list(deps) + war:
            if en == "pe" and s in own:
                continue
            if clock[s] >= v:
                continue
            if need.get(s, (0, None))[0] < v:
                need[s] = (v, ck)
        for s, (v, ck) in need.items():
            if clock[s] >= v:
                continue
            e["h"].wait_ge(self.sems[s], int(v))
            self.nwaits += 1
            clock[s] = v
            if ck is not None:
                np.maximum(clock, ck, out=clock)

    def _record(self, ev, reads, writes):
        for t in reads:
            t.r = [d for d in t.r if d[0] != ev[0]]
            t.r.append(ev)
        for t in writes:
            t.w = ev
            t.r = []

    def op(self, en, fn, reads=(), writes=()):
        e = self.eng[en]
        if e["cnt"] >= self.LIMIT:
            self.final[e["sem"]] = e["cnt"]
            e["sem"] = self.spare.pop()
            e["own"].add(e["sem"])
            e["cnt"] = 0
        self._wait_for(en, self._deps(reads, writes))
        ins = fn(e["h"])
        e["cnt"] += 1
        ins.then_inc(self.sems[e["sem"]], 1)
        self.ninst += 1
        ev = (e["sem"], e["cnt"], e["clock"].copy())
        self._record(ev, reads, writes)
        return ev

    def dma(self, en, out, in_, reads=(), writes=(), **kw):
        e = self.eng[en]
        slot = e["dslots"][e["dnext"] % self.NSLOT]
        e["dnext"] += 1
        if slot["cnt"] * 16 >= self.LIMIT:
            self.final[slot["sem"]] = slot["cnt"] * 16
            slot["sem"] = self.spare.pop()
            slot["cnt"] = 0
        deps = self._deps(reads, writes)
        if slot["cnt"] > 0:
            deps.append((slot["sem"], slot["cnt"] * 16, None))
        self._wait_for(en, deps)
        ins = e["h"].dma_start(out=out, in_=in_, **kw)
        slot["cnt"] += 1
        ins.then_inc(self.sems[slot["sem"]], 16)
        self.ninst += 1
        ev = (slot["sem"], slot["cnt"] * 16, e["clock"].copy())
        self._record(ev, reads, writes)
        return ev

    def barrier(self):
        cur = np.zeros(len(self.sems), dtype=np.int64)
        for s_, v_ in self.final.items():
            cur[s_] = v_
        for e in self.eng.values():
            cur[e["sem"]] = e["cnt"]
            for sl in e["dslots"]:
                cur[sl["sem"]] = sl["cnt"] * 16
        for en, e in self.eng.items():
            clock = e["clock"]
            for s in range(len(self.sems)):
                if s in e["own"]:
                    continue
                if clock[s] < cur[s]:
                    e["h"].wait_ge(self.sems[s], int(cur[s]))
                    self.nwaits += 1
                    clock[s] = cur[s]

    def close(self):
        for cm in reversed(self._ctx):
            cm.__exit__(None, None, None)


def make_consts(T):
    import ml_dtypes
    c = {}
    c["ident_f"] = np.eye(128, dtype=np.float32)
    tri = np.where(np.arange(128)[:, None] <= np.arange(128)[None, :], 0.0, NEG).astype(np.float32)
    c["trimask"] = tri
    esel = np.zeros((128, 8, 128), dtype=np.float32)
    for h in range(8):
        esel[:, h, h] = 1.0
    c["esel"] = esel
    esel2 = np.zeros((128, 4, 128), dtype=np.float32)
    for h in range(4):
        for r in range(128):
            esel2[r, h, 2 * h + r // 64] = 1.0
    c["esel2"] = esel2
    r = np.arange(128)
    d = r % 64
    invf = (10000.0 ** (-(2.0 * (d % 32)) / 64.0)).astype(np.float32)
    sgn = np.where(d < 32, -1.0, 1.0).astype(np.float32)
    c["rope"] = np.stack([invf, sgn], axis=1).astype(np.float32)
    ii = np.arange(128)[:, None]
    jj = np.arange(128)[None, :]
    c["negup"] = np.where(jj >= ii, 0.0, NEG).astype(np.float32)
    c["poslow"] = np.where(jj < ii, 0.0, -NEG).astype(np.float32)
    cm = np.ones((4, T), dtype=np.float32)
    cm[:, ::128] = 0.0
    c["cmask"] = cm
    return c


import os
STOP = os.environ.get("K_STOP", "")


class StopBuild(Exception):
    pass


class LIdx:
    def __init__(self, ap, mapping):
        self.ap = ap
        self.m = mapping

    def __getitem__(self, l):
        return self.ap[self.m[l]]


def layer_maps(layers):
    mall = {l: i for i, l in enumerate(layers)}
    ev = [l // 2 for l in layers if l % 2 == 0]
    od = [l // 2 for l in layers if l % 2 == 1]
    mev = {j: i for i, j in enumerate(ev)}
    mod = {j: i for i, j in enumerate(od)}
    return mall, (ev or [0]), mev, (od or [0]), mod


class Builder:
    def __init__(self, T, layers, final_norm=True):
        self.T = T
        self.NT = T // 128
        self.NB = T // 512
        self.layers = layers
        self.final_norm = final_norm
        self.nc = bass.Bass("TRN2", target_bir_lowering=False)
        self.S = None

    def declare(self):
        nc, T = self.nc, self.T
        I = lambda n, s, d=F32: nc.dram_tensor(n, s, d, kind="ExternalInput").ap()
        self.x = I("x", [T, D])
        mall, ev, mev, od, mod = layer_maps(self.layers)
        NL, NE, NO = len(self.layers), len(ev), len(od)
        A = lambda n, s: LIdx(I(n, [NL] + s), mall)
        E = lambda n, s: LIdx(I(n, [NE] + s), mev)
        O = lambda n, s: LIdx(I(n, [NO] + s), mod)
        self.pT = A("pT", [256, T])
        self.pos = I("pos", [1, T], I32)
        self.norm_mix = A("norm_mix", [D])
        self.norm_mlp = A("norm_mlp", [D])
        self.norm_final = I("norm_final", [1, D])
        self.w_in_even = E("w_in_even", [D, EVEN_IN + 1024])
        self.convw = E("convw", [128, 12, 4])
        self.a_log = E("a_log", [4])
        self.dt_bias = E("dt_bias", [4])
        self.gdn_norm = E("gdn_norm", [128])
        self.lam = [E(n, [64]) for n in ("lam_q1", "lam_k1", "lam_q2", "lam_k2")]
        self.diff_norm = E("diff_norm", [128])
        self.w_out_even = E("w_out_even", [D, D])
        self.w_in_odd = O("w_in_odd", [D, ODD_IN])
        self.b_forget = O("b_forget", [8])
        self.w_out_odd = O("w_out_odd", [D, D])
        self.w_mlp_up = A("w_mlp_up", [D, DFF])
        self.w_mlp_down = A("w_mlp_down", [DFF, D])
        self.w_ple_proj = A("w_ple_proj", [256, D])
        self.w_ple_gate = A("w_ple_gate", [D, D])
        self.c_ident = I("ident_f", [128, 128])
        self.c_tri = I("trimask", [128, 128])
        self.c_esel = I("esel", [128, 8, 128])
        self.c_esel2 = I("esel2", [128, 4, 128])
        self.c_rope = I("rope", [128, 2])
        self.c_negup = I("negup", [128, 128])
        self.c_poslow = I("poslow", [128, 128])
        self.c_cmask = I("cmask", [4, T])
        self.y = nc.dram_tensor("y", [T, D], F32, kind="ExternalOutput").ap()
        Sc = lambda n, s, d=F32: nc.dram_tensor(n, s, d, kind="Internal").ap()
        self.xres = Sc("xres", [T, D])
        self.obuf = Sc("obuf", [T, D])
        self.gbuf = Sc("gbuf", [T, D], BF16)
        self.qTd = Sc("qTd", [8, 128, T], BF16)
        self.kTd = Sc("kTd", [8, 128, T], BF16)
        self.vaug = Sc("vaug", [T, 8 * 129], BF16)
        self.obuf2 = Sc("obuf2", [T, D])
        self.gq = Sc("gq", [4, 128, T], BF16)
        self.gk = Sc("gk", [4, 128, T], BF16)
        self.gv = Sc("gv", [4, 128, T], BF16)
        self.gcd = Sc("gcd", [4, T])
        self.egcd = Sc("egcd", [4, T])
        self.t_g = Tok("g")
        self.t_gcd = Tok("gcd")
        self.t_obuf2 = [Tok(f"obuf2{i}") for i in range(self.NT)]
        self.qbd = Sc("qbd", [8, 5, T], BF16)
        self.kbd = Sc("kbd", [8, 5, T], BF16)
        self.wb = {}
        for l in self.layers:
            j = l // 2
            if l % 2 == 0:
                self.wb[("in", l)] = (Sc(f"wbin{l}", [D, EVEN_IN + 1024], BF16), self.w_in_even[j])
                self.wb[("out", l)] = (Sc(f"wbout{l}", [D, D], BF16), self.w_out_even[j])
            else:
                self.wb[("in", l)] = (Sc(f"wbin{l}", [D, ODD_IN], BF16), self.w_in_odd[j])
                self.wb[("out", l)] = (Sc(f"wbout{l}", [D, D], BF16), self.w_out_odd[j])
            self.wb[("up", l)] = (Sc(f"wbup{l}", [D, DFF], BF16), self.w_mlp_up[l])
            self.wb[("down", l)] = (Sc(f"wbdn{l}", [DFF, D], BF16), self.w_mlp_down[l])
            self.wb[("ple", l)] = (Sc(f"wbple{l}", [256, D], BF16), self.w_ple_proj[l])
            self.wb[("gate", l)] = (Sc(f"wbgate{l}", [D, D], BF16), self.w_ple_gate[l])
        self.wtok = {k: Tok(str(k)) for k in self.wb}
        self.t_xres = [Tok(f"xres{i}") for i in range(self.NT)]
        self.t_obuf = [Tok(f"obuf{i}") for i in range(self.NT)]
        self.t_gbuf = [Tok(f"gbuf{i}") for i in range(self.NT)]
        self.t_qT = [Tok(f"qT{h}") for h in range(8)]
        self.t_kT = [Tok(f"kT{h}") for h in range(8)]
        self.t_vaug = Tok("vaug")
        self.t_qbd = Tok("qbd")
        self.t_kbd = Tok("kbd")
        self.t_y = Tok("y")

    def sb(self, es, name, shape, dt, n=1):
        self._uid = getattr(self, "_uid", 0) + 1
        name = f"{name}_u{self._uid}"
        t = es.enter_context(self.nc.sbuf_tensor(name, shape, dt))
        return Buf(t, n, name)

    def ring(self, es, name, shape, dt, k, n=1):
        return [self.sb(es, f"{name}{i}", shape, dt, n) for i in range(k)]

    def psbank(self):
        i = self.S.psn % 8
        self.S.psn += 1
        return i

    def PS(self, b):
        return self.ps[:, b, :]

    def PSB(self, b):
        return self.ps[:, b, :].bitcast(BF16).rearrange("p (c n) -> p c n", n=128)

    def transpose8(self, src, src_toks, dstT, dst_toks, evac="dve"):
        S = self.S
        b = self.psbank()
        pv = self.PSB(b)
        for c in range(8):
            S.op("pe", lambda p: p.transpose(out=pv[:, c, :], in_=src[:, c * 128:(c + 1) * 128], identity=self.identb.t[:]),
                 reads=list(src_toks) + [self.identb.tok], writes=[self.pst[b]])
        if evac == "dve":
            S.op("dve", lambda v: v.tensor_copy(out=dstT, in_=pv), reads=[self.pst[b]], writes=dst_toks)
        else:
            S.op("act", lambda a: a.copy(out=dstT, in_=pv), reads=[self.pst[b]], writes=dst_toks)

    def rmsnorm_h(self, xt, xtok, wtile, hb, ss, sd, junk):
        S = self.S
        S.op("act", lambda a: a.activation(out=junk.t[:], in_=xt, func=AF.Square, accum_out=ss.t[:]),
             reads=[xtok], writes=[junk.tok, ss.tok])
        S.op("act", lambda a: a.activation(out=sd.t[:], in_=ss.t[:], func=AF.Sqrt, scale=1.0 / D, bias=self.epsb.t[:]),
             reads=[ss.tok, self.epsb.tok], writes=[sd.tok])
        S.op("dve", lambda v: v.reciprocal(out=sd.t[:], in_=sd.t[:]), reads=[sd.tok], writes=[sd.tok])
        S.op("dve", lambda v: v.scalar_tensor_tensor(out=hb.t[:], in0=xt, scalar=sd.t[:], in1=wtile.t[:], op0=ALU.mult, op1=ALU.mult),
             reads=[xtok, sd.tok, wtile.tok], writes=[hb.tok])

    def build(self):
        nc = self.nc
        self.declare()
        self.S = S = Sched(nc)
        with ExitStack() as es0:
            self.ps = es0.enter_context(nc.psum_tensor("ps", [128, 8, 512], F32))
            self.pst = [Tok(f"ps{i}") for i in range(8)]
            self.identb = self.sb(es0, "identb", [128, 128], BF16)
            self.identf = self.sb(es0, "identf", [128, 128], F32)
            self.trib = self.sb(es0, "trib", [128, 128], BF16)
            self.eselb = self.sb(es0, "eselb", [128, 8, 128], BF16)
            self.esel2b = self.sb(es0, "esel2b", [128, 4, 128], BF16)
            S.dma("pool", self.esel2b.t[:], self.c_esel2, writes=[self.esel2b.tok])
            self.epsb = self.sb(es0, "epsb", [128, 1], F32)
            self.zerob = self.sb(es0, "zerob", [128, 512], BF16)
            S.dma("sp", self.identf.t[:], self.c_ident, writes=[self.identf.tok])
            S.dma("pool", self.identb.t[:], self.c_ident, writes=[self.identb.tok])
            S.dma("pool", self.trib.t[:], self.c_tri, writes=[self.trib.tok])
            S.dma("pool", self.eselb.t[:], self.c_esel, writes=[self.eselb.tok])
            S.op("dve", lambda v: v.memset(self.epsb.t[:], EPS), writes=[self.epsb.tok])
            S.op("dve", lambda v: v.memset(self.zerob.t[:], 0.0), writes=[self.zerob.tok])
            xv = self.x.rearrange("(n p) c -> p n c", p=128)
            xr = self.xres.rearrange("(n p) c -> p n c", p=128)
            for i in range(0, self.NT, 4):
                S.dma("sp", xr[:, i:i + 4, :], xv[:, i:i + 4, :], writes=self.t_xres[i:i + 4])
            for l in self.layers:
                for kind in ("in", "out", "up", "down", "ple", "gate"):
                    dst, src = self.wb[(kind, l)]
                    rows = dst.shape[0]
                    step = 512
                    for r0 in range(0, rows, step):
                        r1 = min(rows, r0 + step)
                        S.dma("pool", dst[r0:r1, :], src[r0:r1, :], writes=[self.wtok[(kind, l)]])
            for li, l in enumerate(self.layers):
                last = (li == len(self.layers) - 1)
                self.stopped = False
                if l % 2 == 1:
                    self.odd_layer(l)
                else:
                    self.even_layer(l)
                if STOP == "E3":
                    self.stopped = True
                if not self.stopped:
                    self.tail(l, last)
            S._wait_for("sp", [self.t_y.w] if self.t_y.w else [])
            S.barrier()
        S.close()
        return nc

    def odd_layer(self, l):
        nc, S, T, NT, NB = self.nc, self.S, self.T, self.NT, self.NB
        j = l // 2
        scale = 128 ** -0.5
        with ExitStack() as esL:
            fT = self.sb(esL, "fT", [8, T], F32)
            qsq = self.sb(esL, "qsq", [8, T], F32)
            ksq = self.sb(esL, "ksq", [8, T], F32)
            S.barrier()
            with ExitStack() as es:
                WP = ODD_IN + 120
                win = self.sb(es, "win", [128, 8, WP], BF16)
                wsrc = self.wb[("in", l)][0].rearrange("(c p) n -> p c n", p=128)
                S.op("pool", lambda g: g.memset(win.t[:, :, ODD_IN:WP], 0.0), writes=[win.tok])
                for kc in range(8):
                    S.dma("sp", win.t[:, kc, 0:ODD_IN], wsrc[:, kc, :], reads=[self.wtok[("in", l)]], writes=[win.tok])
                nw = self.sb(es, "nw", [128, D], F32)
                S.dma("sp", nw.t[:], self.norm_mix[l].partition_broadcast(128), writes=[nw.tok])
                xr = self.ring(es, "xr", [128, D], F32, 3)
                hb = self.ring(es, "hb", [128, D], BF16, 2)
                junk = self.sb(es, "junk", [128, D], BF16)
                ss = self.ring(es, "ss", [128, 1], F32, 2)
                sd = self.ring(es, "sd", [128, 1], F32, 2)
                hT = self.ring(es, "hT", [128, 8, 512], BF16, 2, n=4)
                qt = self.ring(es, "qt", [128, 512], BF16, 4)
                sq = self.ring(es, "sq", [128, 512], BF16, 3)
                vt = self.ring(es, "vt", [128, 8, 129], BF16, 2)
                gt = self.ring(es, "gt", [128, D], BF16, 2)
                for v_ in vt:
                    S.op("pool", lambda g: g.memset(v_.t[:], 1.0), writes=[v_.tok])
                nx = 0
                nq = 0
                nv = 0
                for blk in range(NB):
                    t0 = blk * 512
                    h_T = hT[blk % 2]
                    for sub in range(4):
                        ti = blk * 4 + sub
                        xb_ = xr[nx % 3]
                        hb_ = hb[nx % 2]
                        S.dma("sp", xb_.t[:], self.xres[ti * 128:(ti + 1) * 128, :], reads=[self.t_xres[ti]], writes=[xb_.tok])
                        self.rmsnorm_h(xb_.t[:], xb_.tok, nw, hb_, ss[nx % 2], sd[nx % 2], junk)
                        self.transpose8(hb_.t[:], [hb_.tok], h_T.t[:, :, sub * 128:(sub + 1) * 128], [h_T.toks[sub]],
                                        evac=("dve" if sub % 2 == 0 else "act"))
                        nx += 1
                    bq = self.psbank()
                    bk = self.psbank()
                    for which in range(2):
                        bstat = bq if which == 0 else bk
                        for h in range(8):
                            b = self.psbank()
                            while b in (bq, bk):
                                b = self.psbank()
                            c0 = which * 1024 + h * 128
                            for kc in range(8):
                                S.op("pe", lambda p: p.matmul(self.PS(b), lhsT=win.t[:, kc, c0:c0 + 128], rhs=h_T.t[:, kc, :], start=(kc == 0), stop=(kc == 7)),
                                     reads=[win.tok] + h_T.toks, writes=[self.pst[b]])
                            q_ = qt[nq % 4]
                            s_ = sq[nq % 3]
                            nq += 1
                            S.op("act", lambda a: a.activation(out=q_.t[:], in_=self.PS(b), func=AF.Copy, scale=(scale if which == 0 else 1.0)),
                                 reads=[self.pst[b]], writes=[q_.tok])
                            dst = (self.qTd if which == 0 else self.kTd)[h][:, t0:t0 + 512]
                            S.dma("pool", dst, q_.t[:], reads=[q_.tok], writes=[(self.t_qT if which == 0 else self.t_kT)[h]])
                            S.op("dve", lambda v: v.tensor_tensor(out=s_.t[:], in0=q_.t[:], in1=q_.t[:], op=ALU.mult), reads=[q_.tok], writes=[s_.tok])
                            S.op("pe", lambda p: p.matmul(self.PS(bstat), lhsT=self.eselb.t[:, h, :], rhs=s_.t[:], start=(h == 0), stop=(h == 7)),
                                 reads=[self.eselb.tok, s_.tok], writes=[self.pst[bstat]])
                        dstat = (qsq if which == 0 else ksq)
                        S.op("dve", lambda v: v.tensor_copy(out=dstat.t[:, t0:t0 + 512], in_=self.ps[0:8, bstat, :]), reads=[self.pst[bstat]], writes=[dstat.tok])
                    b = self.psbank()
                    for kc in range(8):
                        S.op("pe", lambda p: p.matmul(self.PS(b), lhsT=win.t[:, kc, 4096:4096 + 128], rhs=h_T.t[:, kc, :], start=(kc == 0), stop=(kc == 7)),
                             reads=[win.tok] + h_T.toks, writes=[self.pst[b]])
                    S.op("dve", lambda v: v.tensor_copy(out=fT.t[:, t0:t0 + 512], in_=self.ps[0:8, b, :]), reads=[self.pst[b]], writes=[fT.tok])
                    for sub in range(4):
                        ti = blk * 4 + sub
                        v_ = vt[nv % 2]
                        g_ = gt[nv % 2]
                        nv += 1
                        for nh in range(2):
                            b = self.psbank()
                            for kc in range(8):
                                S.op("pe", lambda p: p.matmul(self.PS(b), lhsT=h_T.t[:, kc, sub * 128:(sub + 1) * 128], rhs=win.t[:, kc, 2048 + nh * 512:2048 + (nh + 1) * 512], start=(kc == 0), stop=(kc == 7)),
                                     reads=[win.tok, h_T.toks[sub]], writes=[self.pst[b]])
                            S.op("dve", lambda v: v.tensor_copy(out=v_.t[:, nh * 4:(nh + 1) * 4, 0:128], in_=self.PS(b).rearrange("p (h c) -> p h c", c=128)),
                                 reads=[self.pst[b]], writes=[v_.tok])
                        S.dma("pool", self.vaug[ti * 128:(ti + 1) * 128, :], v_.t[:].rearrange("p h c -> p (h c)"), reads=[v_.tok], writes=[self.t_vaug])
                        for nh in range(2):
                            b = self.psbank()
                            for kc in range(8):
                                S.op("pe", lambda p: p.matmul(self.PS(b), lhsT=h_T.t[:, kc, sub * 128:(sub + 1) * 128], rhs=win.t[:, kc, 3072 + nh * 512:3072 + (nh + 1) * 512], start=(kc == 0), stop=(kc == 7)),
                                     reads=[win.tok, h_T.toks[sub]], writes=[self.pst[b]])
                            S.op("act", lambda a: a.activation(out=g_.t[:, nh * 512:(nh + 1) * 512], in_=self.PS(b), func=AF.Sigmoid),
                                 reads=[self.pst[b]], writes=[g_.tok])
                        S.dma("pool", self.gbuf[ti * 128:(ti + 1) * 128, :], g_.t[:], reads=[g_.tok], writes=[self.t_gbuf[ti]])
            S.barrier()
            with ExitStack() as es:
                bf = self.sb(es, "bf", [8, 1], F32)
                S.dma("sp", bf.t[:], self.b_forget[j].rearrange("(h o) -> h o", o=1), writes=[bf.tok])
                xa = self.sb(es, "xa", [8, T], F32)
                ya = self.sb(es, "ya", [8, T], F32)
                za = self.sb(es, "za", [8, T], F32)
                ones = self.sb(es, "ones", [8, T], F32)
                km = self.sb(es, "km", [8, 1], F32)
                b1 = self.sb(es, "b1", [8, T], BF16)
                b2 = self.sb(es, "b2", [8, T], BF16)
                b3 = self.sb(es, "b3", [8, T], BF16)
                onesb = self.sb(es, "onesb", [8, T], BF16)
                S.op("dve", lambda v: v.memset(ones.t[:], 1.0), writes=[ones.tok])
                S.op("pool", lambda g: g.memset(onesb.t[:], 1.0), writes=[onesb.tok])
                S.op("dve", lambda v: v.tensor_scalar(out=xa.t[:], in0=fT.t[:], scalar1=bf.t[:], scalar2=None, op0=ALU.add), reads=[fT.tok, bf.tok], writes=[xa.tok])
                S.op("dve", lambda v: v.scalar_tensor_tensor(out=ya.t[:], in0=xa.t[:], scalar=-1.0, in1=xa.t[:], op0=ALU.mult, op1=ALU.max), reads=[xa.tok], writes=[ya.tok])
                S.op("act", lambda a: a.activation(out=ya.t[:], in_=ya.t[:], func=AF.Exp, scale=-1.0), reads=[ya.tok], writes=[ya.tok])
                S.op("act", lambda a: a.activation(out=ya.t[:], in_=ya.t[:], func=AF.Ln, bias=1.0), reads=[ya.tok], writes=[ya.tok])
                S.op("dve", lambda v: v.scalar_tensor_tensor(out=za.t[:], in0=xa.t[:], scalar=0.0, in1=ya.t[:], op0=ALU.min, op1=ALU.subtract), reads=[xa.tok, ya.tok], writes=[za.tok])
                S.op("dve", lambda v: v.tensor_tensor_scan(out=xa.t[:], data0=ones.t[:], data1=za.t[:], initial=0.0, op0=ALU.mult, op1=ALU.add), reads=[ones.tok, za.tok, xa.tok], writes=[xa.tok])
                S.op("dve", lambda v: v.tensor_reduce(out=km.t[:], in_=ksq.t[:], axis=AX.X, op=ALU.max), reads=[ksq.tok], writes=[km.tok])
                S.op("dve", lambda v: v.tensor_scalar(out=ya.t[:], in0=qsq.t[:], scalar1=128.0, scalar2=km.t[:], op0=ALU.mult, op1=ALU.add), reads=[qsq.tok, km.tok], writes=[ya.tok])
                S.op("dve", lambda v: v.scalar_tensor_tensor(out=ya.t[:], in0=ya.t[:], scalar=-0.5 * scale, in1=xa.t[:], op0=ALU.mult, op1=ALU.add), reads=[ya.tok, xa.tok], writes=[ya.tok])
                S.op("dve", lambda v: v.tensor_copy(out=b1.t[:], in_=ya.t[:]), reads=[ya.tok], writes=[b1.tok])
                S.op("dve", lambda v: v.tensor_tensor(out=b2.t[:], in0=ya.t[:], in1=b1.t[:], op=ALU.subtract), reads=[ya.tok, b1.tok], writes=[b2.tok])
                S.dma("sp", self.qbd[:, 0, :], b1.t[:], reads=[b1.tok], writes=[self.t_qbd])
                S.dma("sp", self.qbd[:, 1, :], b2.t[:], reads=[b2.tok], writes=[self.t_qbd])
                for r in (2, 3, 4):
                    S.dma("sp", self.qbd[:, r, :], onesb.t[:], reads=[onesb.tok], writes=[self.t_qbd])
                for r in (0, 1):
                    S.dma("sp", self.kbd[:, r, :], onesb.t[:], reads=[onesb.tok], writes=[self.t_kbd])
                S.op("dve", lambda v: v.tensor_scalar(out=za.t[:], in0=xa.t[:], scalar1=-1.0, scalar2=None, op0=ALU.mult), reads=[xa.tok], writes=[za.tok])
                S.op("dve", lambda v: v.tensor_copy(out=b1.t[:], in_=za.t[:]), reads=[za.tok, b1.tok], writes=[b1.tok])
                S.op("dve", lambda v: v.tensor_tensor(out=za.t[:], in0=za.t[:], in1=b1.t[:], op=ALU.subtract), reads=[za.tok, b1.tok], writes=[za.tok])
                S.op("dve", lambda v: v.tensor_copy(out=b2.t[:], in_=za.t[:]), reads=[za.tok, b2.tok], writes=[b2.tok])
                S.op("dve", lambda v: v.tensor_tensor(out=b3.t[:], in0=za.t[:], in1=b2.t[:], op=ALU.subtract), reads=[za.tok, b2.tok], writes=[b3.tok])
                S.dma("sp", self.kbd[:, 2, :], b1.t[:], reads=[b1.tok], writes=[self.t_kbd])
                S.dma("sp", self.kbd[:, 3, :], b2.t[:], reads=[b2.tok], writes=[self.t_kbd])
                S.dma("sp", self.kbd[:, 4, :], b3.t[:], reads=[b3.tok], writes=[self.t_kbd])
                S.barrier()
        S.barrier()
        with ExitStack() as es:
            self.attention(es, nheads=8, kdim=128, bias_rows=5, vhead=lambda h: h)

    def attention(self, es, nheads, kdim, bias_rows, vhead, dest=None, dtoks=None):
        nc, S, T, NT, NB = self.nc, self.S, self.T, self.NT, self.NB
        NBUF = 2
        qT = self.ring(es, "aqT", [128, T], BF16, NBUF)
        kT = self.ring(es, "akT", [128, T], BF16, NBUF)
        vA = self.ring(es, "avA", [128, NT, 129], BF16, NBUF)
        sepbias = (kdim == 128)
        if sepbias:
            QB = self.ring(es, "aQB", [128, T], BF16, NBUF)
            KB = self.ring(es, "aKB", [128, T], BF16, NBUF)
            for b_ in QB + KB:
                S.op("pool", lambda g: g.memset(b_.t[:], 0.0), writes=[b_.tok])
        if not sepbias:
            for b_ in qT + kT:
                S.op("pool", lambda g: g.memset(b_.t[:], 0.0), writes=[b_.tok])
        pT = self.ring(es, "apT", [128, 512], BF16, 3)
        ot = self.ring(es, "aot", [128, 4, 128], F32, 2)
        rl = self.ring(es, "arl", [128, 1], F32, 4)
        sbanks = [0, 1, 2]
        accsets = [(3, 4), (5, 6)]
        ns = 0
        nqb = 0
        nr = 0
        npt = 0
        vsrc = self.vaug.rearrange("(n p) (h c) -> p n h c", p=128, c=129)
        if dest is None:
            dest, dtoks = self.obuf, self.t_obuf
        osrc = dest.rearrange("(n p) c -> p n c", p=128)
        for h in range(nheads):
            q_, k_, v_ = qT[h % NBUF], kT[h % NBUF], vA[h % NBUF]
            rows = kdim if sepbias else kdim + bias_rows
            lrows = rows
            if not sepbias:
                rows = 96
            S.dma("sp", q_.t[0:lrows, :], self.qTd[h][0:lrows, :], reads=[self.t_qT[h]], writes=[q_.tok])
            S.dma("sp", k_.t[0:lrows, :], self.kTd[h][0:lrows, :], reads=[self.t_kT[h]], writes=[k_.tok])
            for n0 in range(0, NT, 8):
                n1 = min(NT, n0 + 8)
                S.dma("sp", v_.t[:, n0:n1, :], vsrc[:, n0:n1, vhead(h), :], reads=[self.t_vaug], writes=[v_.tok])
            if sepbias:
                qb_, kb_ = QB[h % NBUF], KB[h % NBUF]
                S.dma("sp", qb_.t[0:bias_rows, :], self.qbd[h], reads=[self.t_qbd], writes=[qb_.tok])
                S.dma("sp", kb_.t[0:bias_rows, :], self.kbd[h], reads=[self.t_kbd], writes=[kb_.tok])
            for qb in range(NB):
                accs = accsets[nqb % 2]
                o_ = ot[nqb % 2]
                nqb += 1
                for b in accs:
                    S.op("pe", lambda p: p.matmul(self.PS(b), lhsT=self.zerob.t[:, 0:128], rhs=self.zerob.t[:], start=True, stop=True),
                         reads=[self.zerob.tok], writes=[self.pst[b]])
                nk = 4 * qb + 4

                def s_block(ki, sb_):
                    i = ki - 4 * qb
                    c0 = max(0, i) * 128
                    qs = slice(qb * 512 + c0, qb * 512 + 512)
                    outp = self.ps[:, sb_, c0:512]
                    diag = i >= 0
                    rd = [q_.tok, k_.tok]
                    S.op("pe", lambda p: p.matmul(outp, lhsT=k_.t[0:rows, ki * 128:(ki + 1) * 128], rhs=q_.t[0:rows, qs], start=True, stop=(not sepbias and not diag)),
                         reads=rd, writes=[self.pst[sb_]])
                    if sepbias:
                        S.op("pe", lambda p: p.matmul(outp, lhsT=kb_.t[:, ki * 128:(ki + 1) * 128], rhs=qb_.t[:, qs], start=False, stop=(not diag)),
                             reads=[qb_.tok, kb_.tok], writes=[self.pst[sb_]])
                    if diag:
                        S.op("pe", lambda p: p.matmul(self.ps[:, sb_, c0:c0 + 128], lhsT=self.identb.t[:], rhs=self.trib.t[:], start=False, stop=True),
                             reads=[self.identb.tok, self.trib.tok], writes=[self.pst[sb_]])
                    return c0

                sb_cur = sbanks[ns % 3]
                ns += 1
                c0_cur = s_block(0, sb_cur)
                for ki in range(nk):
                    if ki + 1 < nk:
                        sb_next = sbanks[ns % 3]
                        ns += 1
                        c0_next = s_block(ki + 1, sb_next)
                    p_ = pT[npt % 3]
                    npt += 1
                    c0 = c0_cur
                    S.op("act", lambda a: a.activation(out=p_.t[:, c0:512], in_=self.ps[:, sb_cur, c0:512], func=AF.Exp),
                         reads=[self.pst[sb_cur]], writes=[p_.tok])
                    for sub in range(c0 // 128, 4):
                        bacc = accs[sub // 2]
                        S.op("pe", lambda p: p.matmul(self.ps[:, bacc, (sub % 2) * 256:(sub % 2) * 256 + 129], lhsT=p_.t[:, sub * 128:(sub + 1) * 128], rhs=v_.t[:, ki, :], start=False, stop=(ki == nk - 1), skip_group_check=True),
                             reads=[p_.tok, v_.tok], writes=[self.pst[bacc]])
                    if ki + 1 < nk:
                        sb_cur, c0_cur = sb_next, c0_next
                for sub in range(4):
                    bacc = accs[sub // 2]
                    off = (sub % 2) * 256
                    r_ = rl[nr % 4]
                    nr += 1
                    S.op("dve", lambda v: v.reciprocal(out=r_.t[:], in_=self.ps[:, bacc, off + 128:off + 129]), reads=[self.pst[bacc]], writes=[r_.tok])
                    S.op("dve", lambda v: v.tensor_scalar(out=o_.t[:, sub, :], in0=self.ps[:, bacc, off:off + 128], scalar1=r_.t[:], scalar2=None, op0=ALU.mult),
                         reads=[self.pst[bacc], r_.tok], writes=[o_.tok])
                S.dma("pool", osrc[:, qb * 4:(qb + 1) * 4, h * 128:(h + 1) * 128], o_.t[:], reads=[o_.tok], writes=dtoks[qb * 4:(qb + 1) * 4])

    def even_layer(self, l):
        nc, S, T, NT, NB = self.nc, self.S, self.T, self.NT, self.NB
        j = l // 2
        lam_init = 0.8 - 0.6 * math.exp(-0.3 * l)
        WQB, WKB, WVB = 2056, 2568, 3080
        WSW = EVEN_IN
        WP = EVEN_IN + 1024
        with ExitStack() as esL:
            bT = self.sb(esL, "bT", [4, T], F32)
            aT_ = self.sb(esL, "aT_", [4, T], F32)
            esQ = esL.enter_context(ExitStack())
            qsq = self.sb(esQ, "qsq", [8, T], BF16)
            ksq = self.sb(esQ, "ksq", [8, T], BF16)
            S.barrier()
            with ExitStack() as es:
                win = self.sb(es, "win", [128, 8, WP], BF16)
                wsrc = self.wb[("in", l)][0].rearrange("(c p) n -> p c n", p=128)
                for kc in range(8):
                    S.dma("sp", win.t[:, kc, :], wsrc[:, kc, :], reads=[self.wtok[("in", l)]], writes=[win.tok])
                nw = self.sb(es, "nw", [128, D], F32)
                S.dma("sp", nw.t[:], self.norm_mix[l].partition_broadcast(128), writes=[nw.tok])
                cw = self.sb(es, "cw", [128, 12, 4], F32)
                S.dma("sp", cw.t[:], self.convw[j], writes=[cw.tok])
                invf = self.sb(es, "invf", [128, 2], F32)
                S.dma("sp", invf.t[:], self.c_rope, writes=[invf.tok])
                xr = self.ring(es, "xr", [128, D], F32, 2)
                hb = self.ring(es, "hb", [128, D], BF16, 2)
                junk = self.sb(es, "junk", [128, D], BF16)
                ss = self.ring(es, "ss", [128, 1], F32, 2)
                sd = self.ring(es, "sd", [128, 1], F32, 2)
                hT = self.ring(es, "hT", [128, 8, 512], BF16, 1, n=4)
                raw = self.ring(es, "raw", [128, 515], F32, 2)
                hist = self.sb(es, "hist", [128, 12, 3], F32)
                cv = self.ring(es, "cv", [128, 512], F32, 2)
                sl = self.ring(es, "sl", [128, 512], F32, 2)
                sq = self.ring(es, "sq", [128, 512], BF16, 2)
                rn = self.ring(es, "rn", [128, 512], F32, 1)
                qo = self.ring(es, "qo", [128, 512], BF16, 2)
                posi = self.sb(es, "posi", [128, 512], I32)
                ang = self.sb(es, "ang", [128, 512], F32)
                kf = self.sb(es, "kf", [128, 512], F32)
                ki_ = self.sb(es, "ki_", [128, 512], I32)
                ctab = self.sb(es, "ctab", [128, 512], F32)
                stab = self.sb(es, "stab", [128, 512], F32)
                t1 = self.ring(es, "t1", [128, 512], F32, 1)
                t2 = self.ring(es, "t2", [128, 512], F32, 1)
                vt = self.ring(es, "vt", [128, 8, 129], BF16, 2)
                gt = self.ring(es, "gt", [128, D], BF16, 2)
                onesb = self.sb(es, "onesb", [128, 128], BF16)
                S.op("pool", lambda g: g.memset(onesb.t[:], 1.0), writes=[onesb.tok])
                S.op("pool", lambda g: g.memset(hist.t[:], 0.0), writes=[hist.tok])
                for v_ in vt:
                    S.op("pool", lambda g: g.memset(v_.t[:], 1.0), writes=[v_.tok])
                for g_ in gt:
                    S.op("pool", lambda g: g.memset(g_.t[:], 1.0), writes=[g_.tok])
                nx = nraw = ncv = nq = nv = 0
                TWO_PI = 2.0 * math.pi
                for blk in range(NB):
                    t0 = blk * 512
                    h_T = hT[0]
                    for sub in range(4):
                        ti = blk * 4 + sub
                        xb_ = xr[nx % 2]
                        hb_ = hb[nx % 2]
                        S.dma("sp", xb_.t[:], self.xres[ti * 128:(ti + 1) * 128, :], reads=[self.t_xres[ti]], writes=[xb_.tok])
                        self.rmsnorm_h(xb_.t[:], xb_.tok, nw, hb_, ss[nx % 2], sd[nx % 2], junk)
                        self.transpose8(hb_.t[:], [hb_.tok], h_T.t[:, :, sub * 128:(sub + 1) * 128], [h_T.toks[sub]],
                                        evac=("dve" if sub % 2 == 0 else "act"))
                        nx += 1
                    S.dma("sp", posi.t[:], self.pos[0, t0:t0 + 512].partition_broadcast(128), writes=[posi.tok])
                    S.op("dve", lambda v: v.tensor_copy(out=ang.t[:], in_=posi.t[:]), reads=[posi.tok], writes=[ang.tok])
                    S.op("dve", lambda v: v.tensor_scalar(out=ang.t[:], in0=ang.t[:], scalar1=invf.t[:, 0:1], scalar2=None, op0=ALU.mult), reads=[ang.tok, invf.tok], writes=[ang.tok])
                    for (tab, shift) in ((stab, 0.0), (ctab, 0.5 * math.pi)):
                        S.op("dve", lambda v: v.tensor_scalar(out=kf.t[:], in0=ang.t[:], scalar1=shift, scalar2=1.0 / TWO_PI, op0=ALU.add, op1=ALU.mult), reads=[ang.tok], writes=[kf.tok])
                        S.op("dve", lambda v: v.tensor_copy(out=ki_.t[:], in_=kf.t[:]), reads=[kf.tok], writes=[ki_.tok])
                        S.op("dve", lambda v: v.tensor_copy(out=kf.t[:], in_=ki_.t[:]), reads=[ki_.tok], writes=[kf.tok])
                        S.op("dve", lambda v: v.scalar_tensor_tensor(out=kf.t[:], in0=kf.t[:], scalar=-TWO_PI, in1=ang.t[:], op0=ALU.mult, op1=ALU.add), reads=[kf.tok, ang.tok], writes=[kf.tok])
                        S.op("dve", lambda v: v.tensor_scalar(out=kf.t[:], in0=kf.t[:], scalar1=shift, scalar2=None, op0=ALU.add), reads=[kf.tok], writes=[kf.tok])
                        S.op("dve", lambda v: v.tensor_scalar(out=tab.t[:], in0=kf.t[:], scalar1=math.pi, scalar2=-TWO_PI, op0=ALU.is_gt, op1=ALU.mult), reads=[kf.tok], writes=[tab.tok])
                        S.op("dve", lambda v: v.tensor_tensor(out=kf.t[:], in0=kf.t[:], in1=tab.t[:], op=ALU.add), reads=[kf.tok, tab.tok], writes=[kf.tok])
                        S.op("dve", lambda v: v.tensor_scalar(out=tab.t[:], in0=kf.t[:], scalar1=-math.pi, scalar2=TWO_PI, op0=ALU.is_lt, op1=ALU.mult), reads=[kf.tok], writes=[tab.tok])
                        S.op("dve", lambda v: v.tensor_tensor(out=kf.t[:], in0=kf.t[:], in1=tab.t[:], op=ALU.add), reads=[kf.tok, tab.tok], writes=[kf.tok])
                        S.op("act", lambda a: a.activation(out=tab.t[:], in_=kf.t[:], func=AF.Sin), reads=[kf.tok], writes=[tab.tok])
                    S.op("dve", lambda v: v.tensor_scalar(out=stab.t[:], in0=stab.t[:], scalar1=invf.t[:, 1:2], scalar2=None, op0=ALU.mult), reads=[stab.tok, invf.tok], writes=[stab.tok])
                    for c in range(12):
                        b = self.psbank()
                        for kc in range(8):
                            S.op("pe", lambda p: p.matmul(self.PS(b), lhsT=win.t[:, kc, c * 128:(c + 1) * 128], rhs=h_T.t[:, kc, :], start=(kc == 0), stop=(kc == 7)),
                                 reads=[win.tok] + h_T.toks, writes=[self.pst[b]])
                        r_ = raw[nraw % 2]
                        nraw += 1
                        S.op("act", lambda a: a.copy(out=r_.t[:, 3:515], in_=self.PS(b)), reads=[self.pst[b]], writes=[r_.tok])
                        S.op("pool", lambda g: g.tensor_copy(out=r_.t[:, 0:3], in_=hist.t[:, c, :]), reads=[hist.tok, r_.tok], writes=[r_.tok])
                        S.op("pool", lambda g: g.tensor_copy(out=hist.t[:, c, :], in_=r_.t[:, 512:515]), reads=[r_.tok, hist.tok], writes=[hist.tok])
                        c_ = cv[ncv % 2]
                        s_ = sl[ncv % 2]
                        ncv += 1
                        S.op("dve", lambda v: v.tensor_scalar(out=c_.t[:], in0=r_.t[:, 0:512], scalar1=cw.t[:, c, 0:1], scalar2=None, op0=ALU.mult), reads=[r_.tok, cw.tok], writes=[c_.tok])
                        for i in (1, 2, 3):
                            S.op("dve", lambda v: v.scalar_tensor_tensor(out=c_.t[:], in0=r_.t[:, i:i + 512], scalar=cw.t[:, c, i:i + 1], in1=c_.t[:], op0=ALU.mult, op1=ALU.add),
                                 reads=[r_.tok, cw.tok, c_.tok], writes=[c_.tok])
                        S.op("act", lambda a: a.activation(out=s_.t[:], in_=c_.t[:], func=AF.Silu), reads=[c_.tok], writes=[s_.tok])
                        o_ = qo[nq % 2]
                        nq += 1
                        hh = c % 4
                        if c < 8:
                            q2 = sq[ncv % 2]
                            S.op("pool", lambda g: g.tensor_tensor(out=q2.t[:], in0=s_.t[:], in1=s_.t[:], op=ALU.mult), reads=[s_.tok], writes=[q2.tok])
                            b2 = self.psbank()
                            S.op("pe", lambda p: p.matmul(self.PS(b2), lhsT=onesb.t[:], rhs=q2.t[:], start=True, stop=True), reads=[onesb.tok, q2.tok], writes=[self.pst[b2]])
                            n_ = rn[0]
                            S.op("act", lambda a: a.activation(out=n_.t[:], in_=self.PS(b2), func=AF.Sqrt, bias=self.epsb.t[:]), reads=[self.pst[b2], self.epsb.tok], writes=[n_.tok])
                            S.op("dve", lambda v: v.reciprocal(out=n_.t[:], in_=n_.t[:]), reads=[n_.tok], writes=[n_.tok])
                            sc = (128 ** -0.5) if c < 4 else 1.0
                            S.op("dve", lambda v: v.scalar_tensor_tensor(out=o_.t[:], in0=s_.t[:], scalar=sc, in1=n_.t[:], op0=ALU.mult, op1=ALU.mult), reads=[s_.tok, n_.tok], writes=[o_.tok])
                            dst = (self.gq if c < 4 else self.gk)[hh][:, t0:t0 + 512]
                        else:
                            S.op("pool", lambda g: g.tensor_copy(out=o_.t[:], in_=s_.t[:]), reads=[s_.tok], writes=[o_.tok])
                            dst = self.gv[hh][:, t0:t0 + 512]
                        S.dma("pool", dst, o_.t[:], reads=[o_.tok], writes=[self.t_g])
                    for (dstT, c0) in ((bT, 2048), (aT_, 2052)):
                        b = self.psbank()
                        for kc in range(8):
                            S.op("pe", lambda p: p.matmul(self.PS(b), lhsT=win.t[:, kc, c0:c0 + 128], rhs=h_T.t[:, kc, :], start=(kc == 0), stop=(kc == 7)),
                                 reads=[win.tok] + h_T.toks, writes=[self.pst[b]])
                        S.op("dve", lambda v: v.tensor_copy(out=dstT.t[:, t0:t0 + 512], in_=self.ps[0:4, b, :]), reads=[self.pst[b]], writes=[dstT.tok])
                    bq = self.psbank()
                    bk = self.psbank()
                    for which in range(2):
                        bstat = bq if which == 0 else bk
                        base = WQB if which == 0 else WKB
                        for h in range(4):
                            b1 = self.psbank()
                            while b1 in (bq, bk):
                                b1 = self.psbank()
                            b2 = self.psbank()
                            while b2 in (bq, bk):
                                b2 = self.psbank()
                            c1 = base + h * 128
                            c2 = WSW + which * 512 + h * 128
                            for (bb, cc) in ((b1, c1), (b2, c2)):
                                for kc in range(8):
                                    S.op("pe", lambda p: p.matmul(self.PS(bb), lhsT=win.t[:, kc, cc:cc + 128], rhs=h_T.t[:, kc, :], start=(kc == 0), stop=(kc == 7)),
                                         reads=[win.tok] + h_T.toks, writes=[self.pst[bb]])
                            a_ = t1[0]
                            b_ = t2[0]
                            o_ = qo[nq % 2]
                            s_ = sq[nq % 2]
                            nq += 1
                            sc = 0.125 if which == 0 else 1.0
                            S.op("dve", lambda v: v.scalar_tensor_tensor(out=a_.t[:], in0=self.PS(b1), scalar=sc, in1=ctab.t[:], op0=ALU.mult, op1=ALU.mult), reads=[self.pst[b1], ctab.tok], writes=[a_.tok])
                            S.op("dve", lambda v: v.scalar_tensor_tensor(out=b_.t[:], in0=self.PS(b2), scalar=sc, in1=stab.t[:], op0=ALU.mult, op1=ALU.mult), reads=[self.pst[b2], stab.tok], writes=[b_.tok])
                            S.op("pool", lambda g: g.tensor_tensor(out=o_.t[:], in0=a_.t[:], in1=b_.t[:], op=ALU.add), reads=[a_.tok, b_.tok], writes=[o_.tok])
                            for m in range(2):
                                mh = 2 * h + m
                                dst = (self.qTd if which == 0 else self.kTd)[mh][0:64, t0:t0 + 512]
                                S.dma("pool", dst, o_.t[m * 64:(m + 1) * 64, :], reads=[o_.tok], writes=[(self.t_qT if which == 0 else self.t_kT)[mh]])
                            S.op("dve", lambda v: v.tensor_tensor(out=s_.t[:], in0=o_.t[:], in1=o_.t[:], op=ALU.mult), reads=[o_.tok], writes=[s_.tok])
                            S.op("pe", lambda p: p.matmul(self.PS(bstat), lhsT=self.esel2b.t[:, h, :], rhs=s_.t[:], start=(h == 0), stop=(h == 3)),
                                 reads=[self.esel2b.tok, s_.tok], writes=[self.pst[bstat]])
                        dstat = (qsq if which == 0 else ksq)
                        S.op("dve", lambda v: v.tensor_copy(out=dstat.t[:, t0:t0 + 512], in_=self.ps[0:8, bstat, :]), reads=[self.pst[bstat]], writes=[dstat.tok])
                    for sub in range(4):
                        ti = blk * 4 + sub
                        v_ = vt[nv % 2]
                        g_ = gt[nv % 2]
                        nv += 1
                        b = self.psbank()
                        for kc in range(8):
                            S.op("pe", lambda p: p.matmul(self.PS(b), lhsT=h_T.t[:, kc, sub * 128:(sub + 1) * 128], rhs=win.t[:, kc, WVB:WVB + 512], start=(kc == 0), stop=(kc == 7)),
                                 reads=[win.tok, h_T.toks[sub]], writes=[self.pst[b]])
                        S.op("dve", lambda v: v.tensor_copy(out=v_.t[:, 0:4, 0:128], in_=self.PS(b).rearrange("p (h c) -> p h c", c=128)), reads=[self.pst[b]], writes=[v_.tok])
                        S.dma("pool", self.vaug[ti * 128:(ti + 1) * 128, :], v_.t[:].rearrange("p h c -> p (h c)"), reads=[v_.tok], writes=[self.t_vaug])
                        b = self.psbank()
                        for kc in range(8):
                            S.op("pe", lambda p: p.matmul(self.PS(b), lhsT=h_T.t[:, kc, sub * 128:(sub + 1) * 128], rhs=win.t[:, kc, 1536:2048], start=(kc == 0), stop=(kc == 7)),
                                 reads=[win.tok, h_T.toks[sub]], writes=[self.pst[b]])
                        S.op("act", lambda a: a.activation(out=g_.t[:, 0:512], in_=self.PS(b), func=AF.Silu), reads=[self.pst[b]], writes=[g_.tok])
                        S.dma("pool", self.gbuf[ti * 128:(ti + 1) * 128, :], g_.t[:], reads=[g_.tok], writes=[self.t_gbuf[ti]])
            if STOP == "E1":
                self.stopped = True
            S.barrier()
            with ExitStack() as es:
                if self.stopped:
                    return
                km = self.sb(es, "km", [8, 1], F32)
                ya = self.sb(es, "ya", [8, T], F32)
                b1 = self.sb(es, "b1", [8, T], BF16)
                b2 = self.sb(es, "b2", [8, T], BF16)
                onesb = self.sb(es, "onesb2", [8, T], BF16)
                S.op("pool", lambda g: g.memset(onesb.t[:], 1.0), writes=[onesb.tok])
                S.op("dve", lambda v: v.tensor_reduce(out=km.t[:], in_=ksq.t[:], axis=AX.X, op=ALU.max), reads=[ksq.tok], writes=[km.tok])
                S.op("dve", lambda v: v.tensor_scalar(out=ya.t[:], in0=qsq.t[:], scalar1=64.0, scalar2=km.t[:], op0=ALU.mult, op1=ALU.add), reads=[qsq.tok, km.tok], writes=[ya.tok])
                S.op("dve", lambda v: v.tensor_scalar(out=ya.t[:], in0=ya.t[:], scalar1=-0.5 * 0.125, scalar2=None, op0=ALU.mult), reads=[ya.tok], writes=[ya.tok])
                S.op("dve", lambda v: v.tensor_copy(out=b1.t[:], in_=ya.t[:]), reads=[ya.tok], writes=[b1.tok])
                S.op("dve", lambda v: v.tensor_tensor(out=b2.t[:], in0=ya.t[:], in1=b1.t[:], op=ALU.subtract), reads=[ya.tok, b1.tok], writes=[b2.tok])
                S.dma("sp", self.qTd[:, 64, :], b1.t[:], reads=[b1.tok], writes=self.t_qT)
                S.dma("sp", self.qTd[:, 65, :], b2.t[:], reads=[b2.tok], writes=self.t_qT)
                S.dma("sp", self.kTd[:, 64, :], onesb.t[:], reads=[onesb.tok], writes=self.t_kT)
                S.dma("sp", self.kTd[:, 65, :], onesb.t[:], reads=[onesb.tok], writes=self.t_kT)
                S.barrier()
            esQ.close()
            if STOP == "E2a":
                self.stopped = True
                return
            self.gdn(l, bT, aT_)
        if STOP == "gdnB":
            self.stopped = True
        if self.stopped:
            return
        S.barrier()
        with ExitStack() as es:
            self.attention(es, nheads=8, kdim=64, bias_rows=2, vhead=lambda mh: mh // 2, dest=self.obuf2, dtoks=self.t_obuf2)
        tap = os.environ.get("K_TAP", "")
        if tap:
            S.barrier()
            src = self.obuf if tap == "o" else self.obuf2
            for i in range(0, T, 512):
                S.dma("sp", self.y[i:i + 512, :], src[i:i + 512, :], writes=[self.t_y])
            self.stopped = True

    def gdn(self, l, bT, aT_):
        nc, S, T, NT, NB = self.nc, self.S, self.T, self.NT, self.NB
        j = l // 2
        S.barrier()
        with ExitStack() as es:
            al = self.sb(es, "al", [4, 1], F32)
            dtb = self.sb(es, "dtb", [4, 1], F32)
            S.dma("sp", al.t[:], self.a_log[j].rearrange("(h o) -> h o", o=1), writes=[al.tok])
            S.dma("sp", dtb.t[:], self.dt_bias[j].rearrange("(h o) -> h o", o=1), writes=[dtb.tok])
            cols = self.sb(es, "gcols", [128, NT, 16], F32)
            with ExitStack() as esA:
                cm = self.sb(esA, "cm", [4, T], F32)
                S.dma("sp", cm.t[:], self.c_cmask, writes=[cm.tok])
                xa = aT_
                beta = bT
                ya = self.sb(esA, "gya", [4, T], F32)
                gc = self.sb(esA, "ggc", [4, T], F32)
                bec = self.sb(esA, "gbec", [4, T], F32)
                ekd = self.sb(esA, "gekd", [4, T], F32)
                egc = self.sb(esA, "gegc", [4, T], F32)
                S.op("act", lambda a: a.activation(out=beta.t[:], in_=bT.t[:], func=AF.Sigmoid), reads=[bT.tok], writes=[beta.tok])
                S.op("act", lambda a: a.activation(out=al.t[:], in_=al.t[:], func=AF.Exp), reads=[al.tok], writes=[al.tok])
                S.op("dve", lambda v: v.tensor_scalar(out=al.t[:], in0=al.t[:], scalar1=-1.0, scalar2=None, op0=ALU.mult), reads=[al.tok], writes=[al.tok])
                S.op("dve", lambda v: v.tensor_scalar(out=xa.t[:], in0=aT_.t[:], scalar1=dtb.t[:], scalar2=None, op0=ALU.add), reads=[aT_.tok, dtb.tok], writes=[xa.tok])
                S.op("dve", lambda v: v.scalar_tensor_tensor(out=ya.t[:], in0=xa.t[:], scalar=-1.0, in1=xa.t[:], op0=ALU.mult, op1=ALU.max), reads=[xa.tok], writes=[ya.tok])
                S.op("act", lambda a: a.activation(out=ya.t[:], in_=ya.t[:], func=AF.Exp, scale=-1.0), reads=[ya.tok], writes=[ya.tok])
                S.op("act", lambda a: a.activation(out=ya.t[:], in_=ya.t[:], func=AF.Ln, bias=1.0), reads=[ya.tok], writes=[ya.tok])
                S.op("dve", lambda v: v.scalar_tensor_tensor(out=xa.t[:], in0=xa.t[:], scalar=0.0, in1=ya.t[:], op0=ALU.max, op1=ALU.add), reads=[xa.tok, ya.tok], writes=[xa.tok])
                S.op("dve", lambda v: v.tensor_scalar(out=xa.t[:], in0=xa.t[:], scalar1=al.t[:], scalar2=None, op0=ALU.mult), reads=[xa.tok, al.tok], writes=[xa.tok])
                S.op("dve", lambda v: v.tensor_tensor_scan(out=gc.t[:], data0=cm.t[:], data1=xa.t[:], initial=0.0, op0=ALU.mult, op1=ALU.add), reads=[cm.tok, xa.tok], writes=[gc.tok])
                S.op("act", lambda a: a.activation(out=egc.t[:], in_=gc.t[:], func=AF.Exp), reads=[gc.tok], writes=[egc.tok])
                S.op("dve", lambda v: v.tensor_tensor(out=bec.t[:], in0=beta.t[:], in1=egc.t[:], op=ALU.mult), reads=[beta.tok, egc.tok], writes=[bec.tok])
                gcv = gc.t[:].rearrange("p (n c) -> p n c", c=128)
                S.op("dve", lambda v: v.tensor_tensor(out=ekd.t[:].rearrange("p (n c) -> p n c", c=128), in0=gcv[:, :, 127:128].to_broadcast([4, NT, 128]), in1=gcv, op=ALU.subtract),
                     reads=[gc.tok], writes=[ekd.tok])
                S.op("act", lambda a: a.activation(out=ekd.t[:], in_=ekd.t[:], func=AF.Exp), reads=[ekd.tok], writes=[ekd.tok])
                S.dma("sp", self.gcd, gc.t[:], reads=[gc.tok], writes=[self.t_gcd])
                S.dma("sp", self.egcd, egc.t[:], reads=[egc.tok], writes=[self.t_gcd])
                for ti in range(NT):
                    b = self.psbank()
                    for qi, src in enumerate((gc, beta, bec, ekd)):
                        S.op("pe", lambda p: p.transpose(out=self.ps[:, b, qi * 4:(qi + 1) * 4], in_=src.t[:, ti * 128:(ti + 1) * 128], identity=self.identf.t[0:4, 0:4]),
                             reads=[src.tok, self.identf.tok], writes=[self.pst[b]])
                    S.op("dve", lambda v: v.tensor_copy(out=cols.t[:, ti, :], in_=self.ps[:, b, 0:16]), reads=[self.pst[b]], writes=[cols.tok])
                S.barrier()
            if STOP == "gdnA":
                self.stopped = True
                return
            negup = self.sb(es, "gnegup", [128, 128], F32)
            poslow = self.sb(es, "gposlow", [128, 128], F32)
            S.dma("sp", negup.t[:], self.c_negup, writes=[negup.tok])
            S.dma("sp", poslow.t[:], self.c_poslow, writes=[poslow.tok])
            kT = self.sb(es, "gkT", [128, T], BF16)
            qT = self.sb(es, "gqT", [128, T], BF16)
            vT = self.sb(es, "gvT", [128, T], BF16)
            Rg = self.sb(es, "gRg", [128, T], F32)
            Re = self.sb(es, "gRe", [128, T], F32)
            qd = self.sb(es, "gqd", [128, T], BF16)
            QK = self.sb(es, "gQK", [128, NT, 128], BF16)
            U = self.sb(es, "gU", [128, NT, 128], F32)
            WT = self.sb(es, "gWT", [128, T], BF16)
            KD = self.sb(es, "gKD", [128, NT, 128], BF16)
            egl = self.sb(es, "gegl", [128, NT], F32)
            G = 4
            kbe = self.ring(es, "gkbe", [128, 128], BF16, G)
            vb = self.ring(es, "gvb", [128, 128], BF16, G)
            e1 = self.ring(es, "ge1", [128, 128], F32, G)
            e2 = self.ring(es, "ge2", [128, 128], F32, G)
            Pm = [self.ring(es, f"gP{i}", [128, 128], BF16, G) for i in range(2)]
            PTm = [self.ring(es, f"gPT{i}", [128, 128], BF16, G) for i in range(2)]
            TTm = [self.ring(es, f"gTT{i}", [128, 128], F32, G) for i in range(2)]
            TTs = [self.ring(es, f"gTTs{i}", [128, 128], BF16, G) for i in range(2)]
            Pl = [self.ring(es, f"gPl{i}", [128, 128], BF16, G) for i in range(2)]
            PTl_ = [self.ring(es, f"gPTl{i}", [128, 128], BF16, G) for i in range(2)]
            TSl_ = [self.ring(es, f"gTSl{i}", [128, 128], BF16, G) for i in range(2)]
            xs_ = self.ring(es, "gxs", [128, 128], F32, G)
            xs2_ = self.ring(es, "gxs2", [128, 128], F32, G)

            def split(dh, dl, src):
                S.op("pool", lambda g: g.tensor_copy(out=dh.t[:], in_=src.t[:]), reads=[src.tok], writes=[dh.tok])
                S.op("dve", lambda v: v.tensor_tensor(out=dl.t[:], in0=src.t[:], in1=dh.t[:], op=ALU.subtract), reads=[src.tok, dh.tok], writes=[dl.tok])

            def mm3(bank, *pairs):
                n = len(pairs)
                for i_, (a_, b_) in enumerate(pairs):
                    S.op("pe", lambda p: p.matmul(self.ps[:, bank, 0:128], lhsT=a_.t[:], rhs=b_.t[:], start=(i_ == 0), stop=(i_ == n - 1)),
                         reads=[a_.tok, b_.tok], writes=[self.pst[bank]])
            TTb = self.ring(es, "gTTb", [128, 128], BF16, G)
            Sst = self.sb(es, "gS", [128, 128], F32)
            Sb = self.sb(es, "gSb", [128, 128], BF16)
            Sl = self.sb(es, "gSl", [128, 128], BF16)
            vn = self.ring(es, "gvn", [128, 128], BF16, 2)
            og = self.ring(es, "gog", [128, 4, 128], F32, 2)
            osrc = self.obuf.rearrange("(n p) c -> p n c", p=128)
            for h in range(int(os.environ.get("K_H0", "0")), int(os.environ.get("K_H1", "4"))):
                S.barrier()
                S.dma("sp", kT.t[:], self.gk[h], reads=[self.t_g], writes=[kT.tok])
                S.dma("sp", qT.t[:], self.gq[h], reads=[self.t_g], writes=[qT.tok])
                S.dma("sp", vT.t[:], self.gv[h], reads=[self.t_g], writes=[vT.tok])
                for c0 in range(0, T, 1024):
                    c1 = min(T, c0 + 1024)
                    S.dma("sp", Rg.t[:, c0:c1], self.gcd[h][c0:c1].partition_broadcast(128), reads=[self.t_gcd], writes=[Rg.tok])
                    S.dma("sp", Re.t[:, c0:c1], self.egcd[h][c0:c1].partition_broadcast(128), reads=[self.t_gcd], writes=[Re.tok])
                S.op("pool", lambda g: g.tensor_tensor(out=qd.t[:], in0=qT.t[:], in1=Re.t[:], op=ALU.mult), reads=[qT.tok, Re.tok], writes=[qd.tok])
                S.op("act", lambda a: a.activation(out=egl.t[:], in_=Rg.t[:].rearrange("p (n c) -> p n c", c=128)[:, :, 127], func=AF.Exp), reads=[Rg.tok], writes=[egl.tok])
                for g0 in range(0, NT, G):
                    if g0 > 0 and (g0 // G) % 2 == 0:
                        S.barrier()
                    tiles = list(range(g0, min(NT, g0 + G)))
                    st = {}
                    for ti in tiles:
                        s = ti % G
                        tsl = slice(ti * 128, (ti + 1) * 128)
                        cgc = cols.t[:, ti, 0 + h:1 + h]
                        cbeta = cols.t[:, ti, 4 + h:5 + h]
                        cbec = cols.t[:, ti, 8 + h:9 + h]
                        cekd = cols.t[:, ti, 12 + h:13 + h]
                        b = self.psbank()
                        pv = self.PSB(b)
                        S.op("pe", lambda p: p.transpose(out=pv[:, 0, :], in_=kT.t[:, tsl], identity=self.identb.t[:]), reads=[kT.tok, self.identb.tok], writes=[self.pst[b]])
                        S.op("pe", lambda p: p.transpose(out=pv[:, 1, :], in_=vT.t[:, tsl], identity=self.identb.t[:]), reads=[vT.tok, self.identb.tok], writes=[self.pst[b]])
                        S.op("dve", lambda v: v.tensor_scalar(out=kbe[s].t[:], in0=pv[:, 0, :], scalar1=cbec, scalar2=None, op0=ALU.mult), reads=[self.pst[b], cols.tok], writes=[kbe[s].tok])
                        S.op("dve", lambda v: v.tensor_scalar(out=KD.t[:, ti, :], in0=pv[:, 0, :], scalar1=cekd, scalar2=None, op0=ALU.mult), reads=[self.pst[b], cols.tok], writes=[KD.tok])
                        S.op("dve", lambda v: v.tensor_scalar(out=vb[s].t[:], in0=pv[:, 1, :], scalar1=cbeta, scalar2=None, op0=ALU.mult), reads=[self.pst[b], cols.tok], writes=[vb[s].tok])
                        S.op("dve", lambda v: v.scalar_tensor_tensor(out=e1[s].t[:], in0=Rg.t[:, tsl], scalar=cgc, in1=negup.t[:], op0=ALU.subtract, op1=ALU.add), reads=[Rg.tok, cols.tok, negup.tok], writes=[e1[s].tok])
                        S.op("act", lambda a: a.activation(out=e1[s].t[:], in_=e1[s].t[:], func=AF.Exp), reads=[e1[s].tok], writes=[e1[s].tok])
                        S.op("dve", lambda v: v.scalar_tensor_tensor(out=e2[s].t[:], in0=Rg.t[:, tsl], scalar=cgc, in1=poslow.t[:], op0=ALU.subtract, op1=ALU.add), reads=[Rg.tok, cols.tok, poslow.tok], writes=[e2[s].tok])
                        S.op("act", lambda a: a.activation(out=e2[s].t[:], in_=e2[s].t[:], func=AF.Exp, scale=-1.0), reads=[e2[s].tok], writes=[e2[s].tok])
                        bkk = self.psbank()
                        S.op("pe", lambda p: p.matmul(self.ps[:, bkk, 0:128], lhsT=kT.t[:, tsl], rhs=kT.t[:, tsl], start=True, stop=True), reads=[kT.tok], writes=[self.pst[bkk]])
                        S.op("pe", lambda p: p.matmul(self.ps[:, bkk, 128:256], lhsT=kT.t[:, tsl], rhs=qT.t[:, tsl], start=False, stop=True, skip_group_check=True), reads=[kT.tok, qT.tok], writes=[self.pst[bkk]])
                        Lf = e2[s]
                        S.op("dve", lambda v: v.scalar_tensor_tensor(out=Lf.t[:], in0=self.ps[:, bkk, 0:128], scalar=cbeta, in1=e2[s].t[:], op0=ALU.mult, op1=ALU.mult), reads=[self.pst[bkk], cols.tok, e2[s].tok], writes=[Lf.tok])
                        S.op("dve", lambda v: v.tensor_tensor(out=QK.t[:, ti, :], in0=self.ps[:, bkk, 128:256], in1=e1[s].t[:], op=ALU.mult), reads=[self.pst[bkk], e1[s].tok], writes=[QK.tok])
                        Ph, Pl_, PTh, PTl, TT0, TSh, TSl = Pm[0][s], Pl[0][s], PTm[0][s], PTl_[0][s], TTm[0][s], TTs[0][s], TSl_[0][s]
                        split(Ph, Pl_, Lf)
                        bt = self.psbank()
                        pvt = self.PSB(bt)
                        S.op("pe", lambda p: p.transpose(out=pvt[:, 0, :], in_=Ph.t[:], identity=self.identb.t[:]), reads=[Ph.tok, self.identb.tok], writes=[self.pst[bt]])
                        S.op("pe", lambda p: p.transpose(out=pvt[:, 1, :], in_=Pl_.t[:], identity=self.identb.t[:]), reads=[Pl_.tok, self.identb.tok], writes=[self.pst[bt]])
                        S.op("dve", lambda v: v.tensor_copy(out=PTh.t[:], in_=pvt[:, 0, :]), reads=[self.pst[bt]], writes=[PTh.tok])
                        S.op("dve", lambda v: v.tensor_copy(out=PTl.t[:], in_=pvt[:, 1, :]), reads=[self.pst[bt]], writes=[PTl.tok])
                        S.op("dve", lambda v: v.scalar_tensor_tensor(out=TT0.t[:], in0=PTh.t[:], scalar=-1.0, in1=self.identf.t[:], op0=ALU.mult, op1=ALU.add), reads=[PTh.tok, self.identf.tok], writes=[TT0.tok])
                        S.op("pool", lambda g: g.tensor_tensor(out=TT0.t[:], in0=TT0.t[:], in1=PTl.t[:], op=ALU.subtract), reads=[TT0.tok, PTl.tok], writes=[TT0.tok])
                        split(TSh, TSl, TT0)
                        st[ti] = 0
                    if STOP == "gB1":
                        self.stopped = True
                        return
                    NST = 6
                    for stage in range(NST):
                        lastst = (stage == NST - 1)
                        for ti in tiles:
                            s = ti % G
                            cur = st[ti]
                            nxt = 1 - cur
                            Ph, Pl_, PTh, PTl, TTc, TSh, TSl = Pm[cur][s], Pl[cur][s], PTm[cur][s], PTl_[cur][s], TTm[cur][s], TTs[cur][s], TSl_[cur][s]
                            Pnh, Pnl, PTnh, PTnl, TTn, TSnh, TSnl = Pm[nxt][s], Pl[nxt][s], PTm[nxt][s], PTl_[nxt][s], TTm[nxt][s], TTs[nxt][s], TSl_[nxt][s]
                            X = xs_[s]
                            b = self.psbank()
                            mm3(b, (PTh, Ph), (PTh, Pl_), (PTl, Ph))
                            S.op("dve", lambda v: v.tensor_copy(out=X.t[:], in_=self.ps[:, b, 0:128]), reads=[self.pst[b]], writes=[X.tok])
                            split(Pnh, Pnl, X)
                            if not lastst:
                                b3 = self.psbank()
                                mm3(b3, (Ph, PTh), (Ph, PTl), (Pl_, PTh))
                                X2 = xs2_[s]
                                S.op("dve", lambda v: v.tensor_copy(out=X2.t[:], in_=self.ps[:, b3, 0:128]), reads=[self.pst[b3]], writes=[X2.tok])
                                split(PTnh, PTnl, X2)
                            b2 = self.psbank()
                            mm3(b2, (Pnh, TSh), (Pnh, TSl), (Pnl, TSh))
                            S.op("dve", lambda v: v.tensor_tensor(out=TTn.t[:], in0=self.ps[:, b2, 0:128], in1=TTc.t[:], op=ALU.add), reads=[self.pst[b2], TTc.tok], writes=[TTn.tok])
                            split(TSnh, TSnl, TTn)
                            st[ti] = nxt
                    if STOP == "gB2":
                        self.stopped = True
                        return
                    for ti in tiles:
                        s = ti % G
                        tsl = slice(ti * 128, (ti + 1) * 128)
                        TSh, TSl = TTs[st[ti]][s], TSl_[st[ti]][s]
                        b = self.psbank()
                        S.op("pe", lambda p: p.matmul(self.ps[:, b, 0:128], lhsT=TSh.t[:], rhs=vb[s].t[:], start=True, stop=False), reads=[TSh.tok, vb[s].tok], writes=[self.pst[b]])
                        S.op("pe", lambda p: p.matmul(self.ps[:, b, 0:128], lhsT=TSl.t[:], rhs=vb[s].t[:], start=False, stop=True), reads=[TSl.tok, vb[s].tok], writes=[self.pst[b]])
                        bw = self.psbank()
                        S.op("pe", lambda p: p.matmul(self.ps[:, bw, 0:128], lhsT=kbe[s].t[:], rhs=TSh.t[:], start=True, stop=False), reads=[TSh.tok, kbe[s].tok], writes=[self.pst[bw]])
                        S.op("pe", lambda p: p.matmul(self.ps[:, bw, 0:128], lhsT=kbe[s].t[:], rhs=TSl.t[:], start=False, stop=True), reads=[TSl.tok, kbe[s].tok], writes=[self.pst[bw]])
                        S.op("dve", lambda v: v.tensor_copy(out=U.t[:, ti, :], in_=self.ps[:, b, 0:128]), reads=[self.pst[b]], writes=[U.tok])
                        S.op("dve", lambda v: v.tensor_copy(out=WT.t[:, tsl], in_=self.ps[:, bw, 0:128]), reads=[self.pst[bw]], writes=[WT.tok])
                if STOP == "gB3":
                    self.stopped = True
                    return
                S.op("dve", lambda v: v.memset(Sst.t[:], 0.0), writes=[Sst.tok])
                S.op("pool", lambda g: g.memset(Sb.t[:], 0.0), writes=[Sb.tok])
                S.op("pool", lambda g: g.memset(Sl.t[:], 0.0), writes=[Sl.tok])
                for ti in range(NT):
                    if ti % 8 == 0:
                        S.barrier()
                    tsl = slice(ti * 128, (ti + 1) * 128)
                    ba = self.psbank()
                    bo = self.psbank()
                    S.op("pe", lambda p: p.matmul(self.ps[:, ba, 0:128], lhsT=WT.t[:, tsl], rhs=Sb.t[:], start=True, stop=False), reads=[WT.tok, Sb.tok], writes=[self.pst[ba]])
                    S.op("pe", lambda p: p.matmul(self.ps[:, ba, 0:128], lhsT=WT.t[:, tsl], rhs=Sl.t[:], start=False, stop=True), reads=[WT.tok, Sl.tok], writes=[self.pst[ba]])
                    S.op("pe", lambda p: p.matmul(self.ps[:, bo, 0:128], lhsT=qd.t[:, tsl], rhs=Sb.t[:], start=True, stop=False), reads=[qd.tok, Sb.tok], writes=[self.pst[bo]])
                    S.op("pe", lambda p: p.matmul(self.ps[:, bo, 0:128], lhsT=qd.t[:, tsl], rhs=Sl.t[:], start=False, stop=False), reads=[qd.tok, Sl.tok], writes=[self.pst[bo]])
                    v_ = vn[ti % 2]
                    S.op("dve", lambda v: v.tensor_tensor(out=v_.t[:], in0=U.t[:, ti, :], in1=self.ps[:, ba, 0:128], op=ALU.subtract), reads=[U.tok, self.pst[ba]], writes=[v_.tok])
                    S.op("pe", lambda p: p.matmul(self.ps[:, bo, 0:128], lhsT=QK.t[:, ti, :], rhs=v_.t[:], start=False, stop=True), reads=[QK.tok, v_.tok], writes=[self.pst[bo]])
                    bd = self.psbank()
                    S.op("pe", lambda p: p.matmul(self.ps[:, bd, 0:128], lhsT=KD.t[:, ti, :], rhs=v_.t[:], start=True, stop=True), reads=[KD.tok, v_.tok], writes=[self.pst[bd]])
                    S.op("dve", lambda v: v.tensor_scalar(out=Sst.t[:], in0=Sst.t[:], scalar1=egl.t[:, ti:ti + 1], scalar2=None, op0=ALU.mult), reads=[Sst.tok, egl.tok], writes=[Sst.tok])
                    S.op("dve", lambda v: v.tensor_tensor(out=Sst.t[:], in0=self.ps[:, bd, 0:128], in1=Sst.t[:], op=ALU.add), reads=[Sst.tok, self.pst[bd]], writes=[Sst.tok])
                    S.op("dve", lambda v: v.tensor_copy(out=Sb.t[:], in_=Sst.t[:]), reads=[Sst.tok], writes=[Sb.tok])
                    S.op("dve", lambda v: v.tensor_tensor(out=Sl.t[:], in0=Sst.t[:], in1=Sb.t[:], op=ALU.subtract), reads=[Sst.tok, Sb.tok], writes=[Sl.tok])
                    o_ = og[(ti // 4) % 2]
                    S.op("dve", lambda v: v.tensor_copy(out=o_.t[:, ti % 4, :], in_=self.ps[:, bo, 0:128]), reads=[self.pst[bo]], writes=[o_.tok])
                    if ti % 4 == 3 and os.environ.get("K_DBG") != "nodma":
                        S.dma("pool", osrc[:, ti - 3:ti + 1, h * 128:(h + 1) * 128], o_.t[:], reads=[o_.tok], writes=self.t_obuf[ti - 3:ti + 1])
                    if os.environ.get("K_SCAN") and ti + 1 >= int(os.environ["K_SCAN"]):
                        self.stopped = True
                        return

    def even_gate(self, l, o_, o2_, g_, m_, wn, nlam, tmp, ss8):
        S = self.S
        o2v = o2_.t[:].rearrange("p (h m c) -> p h m c", m=2, c=128)
        S.op("dve", lambda v: v.scalar_tensor_tensor(out=o_.t[:, 512:1024].rearrange("p (h c) -> p h c", c=128), in0=o2v[:, :, 1, :], scalar=nlam.t[:], in1=o2v[:, :, 0, :], op0=ALU.mult, op1=ALU.add),
             reads=[o2_.tok, nlam.tok, o_.tok], writes=[o_.tok])
        S.op("pool", lambda g: g.tensor_tensor(out=tmp.t[:], in0=o_.t[:], in1=o_.t[:], op=ALU.mult), reads=[o_.tok], writes=[tmp.tok])
        S.op("dve", lambda v: v.tensor_reduce(out=ss8.t[:], in_=tmp.t[:].rearrange("p (g c) -> p g c", c=128), axis=AX.X, op=ALU.add), reads=[tmp.tok], writes=[ss8.tok])
        S.op("act", lambda a: a.activation(out=ss8.t[:], in_=ss8.t[:], func=AF.Sqrt, scale=1.0 / 128, bias=self.epsb.t[:]), reads=[ss8.tok, self.epsb.tok], writes=[ss8.tok])
        S.op("dve", lambda v: v.reciprocal(out=ss8.t[:], in_=ss8.t[:]), reads=[ss8.tok], writes=[ss8.tok])
        S.op("dve", lambda v: v.tensor_tensor(out=tmp.t[:].rearrange("p (g c) -> p g c", c=128), in0=o_.t[:].rearrange("p (g c) -> p g c", c=128), in1=ss8.t[:].unsqueeze(2).to_broadcast([128, 8, 128]), op=ALU.mult),
             reads=[o_.tok, ss8.tok, tmp.tok], writes=[tmp.tok])
        S.op("pool", lambda g: g.tensor_tensor(out=tmp.t[:], in0=tmp.t[:], in1=wn.t[:], op=ALU.mult), reads=[tmp.tok, wn.tok], writes=[tmp.tok])
        S.op("dve", lambda v: v.tensor_tensor(out=m_.t[:], in0=tmp.t[:], in1=g_.t[:], op=ALU.mult), reads=[tmp.tok, g_.tok], writes=[m_.tok])

    def tail(self, l, last):
        nc, S, T, NT, NB = self.nc, self.S, self.T, self.NT, self.NB
        odd = (l % 2 == 1)
        S.barrier()
        with ExitStack() as es:
            wring = self.ring(es, "wr", [128, 8, 512], BF16, 5)
            nwr = [0]

            def wload(src_ap, tok):
                w_ = wring[nwr[0] % 5]
                nwr[0] += 1
                S.dma("sp", w_.t[:], src_ap, reads=[tok], writes=[w_.tok])
                return w_

            wout = self.wb[("out", l)][0].rearrange("(c p) n -> p c n", p=128)
            wup = self.wb[("up", l)][0].rearrange("(c p) n -> p c n", p=128)
            wdn = self.wb[("down", l)][0].rearrange("(c p) n -> p c n", p=128)
            wgt = self.wb[("gate", l)][0].rearrange("(c p) n -> p c n", p=128)
            wple = self.sb(es, "wple", [128, 2, D], BF16)
            S.dma("sp", wple.t[:], self.wb[("ple", l)][0].rearrange("(c p) n -> p c n", p=128), reads=[self.wtok[("ple", l)]], writes=[wple.tok])
            nwm = self.sb(es, "nwm", [128, D], F32)
            S.dma("sp", nwm.t[:], self.norm_mlp[l].partition_broadcast(128), writes=[nwm.tok])
            if last and self.final_norm:
                nwf = self.sb(es, "nwf", [128, D], F32)
                S.dma("sp", nwf.t[:], self.norm_final[0].partition_broadcast(128), writes=[nwf.tok])
            xr = self.ring(es, "txr", [128, D], F32, 8)
            orr = self.ring(es, "tor", [128, D], F32, 2)
            gr = self.ring(es, "tgr", [128, D], BF16, 2)
            mb = self.ring(es, "tmb", [128, D], BF16, 3)
            TT = self.ring(es, "tTT", [128, 8, 512], BF16, 2, n=4)
            aT = self.sb(es, "taT", [128, 32, 512], BF16)
            rr = self.ring(es, "trr", [128, 512], F32, 3)
            pTt = self.ring(es, "tpT", [128, 2, 512], BF16, 2)
            junk = self.sb(es, "tjunk", [128, D], BF16)
            ss = self.ring(es, "tss", [128, 1], F32, 2)
            sd = self.ring(es, "tsd", [128, 1], F32, 2)
            if last and self.final_norm:
                yo = self.ring(es, "tyo", [128, D], F32, 2)
            if not odd:
                jj_ = l // 2
                lam_init = 0.8 - 0.6 * math.exp(-0.3 * l)
                o2r = self.ring(es, "to2", [128, D], F32, 2)
                tmpb = self.sb(es, "ttmp", [128, D], F32)
                ss8 = self.ring(es, "tss8", [128, 8], F32, 2)
                wn = self.sb(es, "twn", [128, D], F32)
                for gi in range(4):
                    S.dma("sp", wn.t[:, gi * 128:(gi + 1) * 128], self.gdn_norm[jj_].partition_broadcast(128), writes=[wn.tok])
                    S.dma("sp", wn.t[:, 512 + gi * 128:512 + (gi + 1) * 128], self.diff_norm[jj_].partition_broadcast(128), writes=[wn.tok])
                S.op("dve", lambda v: v.tensor_scalar(out=wn.t[:, 512:1024], in0=wn.t[:, 512:1024], scalar1=1.0 - lam_init, scalar2=None, op0=ALU.mult), reads=[wn.tok], writes=[wn.tok])
                lt = [self.sb(es, f"tlam{i}", [128, 64], F32) for i in range(4)]
                for i in range(4):
                    S.dma("sp", lt[i].t[:], self.lam[i][jj_].partition_broadcast(128), writes=[lt[i].tok])
                ls = self.sb(es, "tls", [128, 2], F32)
                nlam = self.sb(es, "tnlam", [128, 1], F32)
                for i in range(2):
                    S.op("dve", lambda v: v.tensor_tensor(out=lt[2 * i].t[:], in0=lt[2 * i].t[:], in1=lt[2 * i + 1].t[:], op=ALU.mult), reads=[lt[2 * i].tok, lt[2 * i + 1].tok], writes=[lt[2 * i].tok])
                    S.op("dve", lambda v: v.tensor_reduce(out=ls.t[:, i:i + 1], in_=lt[2 * i].t[:], axis=AX.X, op=ALU.add), reads=[lt[2 * i].tok, ls.tok], writes=[ls.tok])
                S.op("act", lambda a: a.activation(out=ls.t[:], in_=ls.t[:], func=AF.Exp), reads=[ls.tok], writes=[ls.tok])
                S.op("dve", lambda v: v.tensor_tensor(out=nlam.t[:], in0=ls.t[:, 1:2], in1=ls.t[:, 0:1], op=ALU.subtract), reads=[ls.tok], writes=[nlam.tok])
                S.op("dve", lambda v: v.tensor_scalar(out=nlam.t[:], in0=nlam.t[:], scalar1=-lam_init, scalar2=None, op0=ALU.add), reads=[nlam.tok], writes=[nlam.tok])
            nT = 0
            nm = 0
            nr = 0
            for blk in range(NB):
                xs = [xr[(blk % 2) * 4 + s] for s in range(4)]
                p_ = pTt[blk % 2]
                S.dma("pool", p_.t[:], self.pT[l].rearrange("(c p) t -> p c t", p=128)[:, :, blk * 512:(blk + 1) * 512], writes=[p_.tok])
                mT = TT[nT % 2]
                nT += 1
                for sub in range(4):
                    ti = blk * 4 + sub
                    x_ = xs[sub]
                    S.dma("sp", x_.t[:], self.xres[ti * 128:(ti + 1) * 128, :], reads=[self.t_xres[ti]], writes=[x_.tok])
                    o_ = orr[ti % 2]
                    g_ = gr[ti % 2]
                    m_ = mb[nm % 3]
                    nm += 1
                    S.dma("sp", o_.t[:], self.obuf[ti * 128:(ti + 1) * 128, :], reads=[self.t_obuf[ti]], writes=[o_.tok])
                    S.dma("sp", g_.t[:], self.gbuf[ti * 128:(ti + 1) * 128, :], reads=[self.t_gbuf[ti]], writes=[g_.tok])
                    if odd:
                        S.op("pool", lambda g: g.tensor_tensor(out=m_.t[:], in0=o_.t[:], in1=g_.t[:], op=ALU.mult), reads=[o_.tok, g_.tok], writes=[m_.tok])
                    else:
                        o2_ = o2r[ti % 2]
                        S.dma("sp", o2_.t[:], self.obuf2[ti * 128:(ti + 1) * 128, :], reads=[self.t_obuf2[ti]], writes=[o2_.tok])
                        self.even_gate(l, o_, o2_, g_, m_, wn, nlam, tmpb, ss8[ti % 2])
                    self.transpose8(m_.t[:], [m_.tok], mT.t[:, :, sub * 128:(sub + 1) * 128], [mT.toks[sub]], evac=("dve" if sub % 2 == 0 else "act"))
                for nh in range(2):
                    w_ = wload(wout[:, :, nh * 512:(nh + 1) * 512], self.wtok[("out", l)])
                    for sub in range(4):
                        b = self.psbank()
                        for kc in range(8):
                            S.op("pe", lambda p: p.matmul(self.PS(b), lhsT=mT.t[:, kc, sub * 128:(sub + 1) * 128], rhs=w_.t[:, kc, :], start=(kc == 0), stop=(kc == 7)),
                                 reads=[mT.toks[sub], w_.tok], writes=[self.pst[b]])
                        x_ = xs[sub]
                        S.op("dve", lambda v: v.tensor_tensor(out=x_.t[:, nh * 512:(nh + 1) * 512], in0=self.PS(b), in1=x_.t[:, nh * 512:(nh + 1) * 512], op=ALU.add),
                             reads=[self.pst[b], x_.tok], writes=[x_.tok])
                hT = TT[nT % 2]
                nT += 1
                for sub in range(4):
                    m_ = mb[nm % 3]
                    nm += 1
                    self.rmsnorm_h(xs[sub].t[:], xs[sub].tok, nwm, m_, ss[sub % 2], sd[sub % 2], junk)
                    self.transpose8(m_.t[:], [m_.tok], hT.t[:, :, sub * 128:(sub + 1) * 128], [hT.toks[sub]], evac=("dve" if sub % 2 == 0 else "act"))
                for g in range(8):
                    w_ = wload(wup[:, :, g * 512:(g + 1) * 512], self.wtok[("up", l)])
                    for jj in range(4):
                        fc = g * 4 + jj
                        b = self.psbank()
                        for kc in range(8):
                            S.op("pe", lambda p: p.matmul(self.PS(b), lhsT=w_.t[:, kc, jj * 128:(jj + 1) * 128], rhs=hT.t[:, kc, :], start=(kc == 0), stop=(kc == 7)),
                                 reads=[w_.tok] + hT.toks, writes=[self.pst[b]])
                        r_ = rr[nr % 3]
                        nr += 1
                        S.op("act", lambda a: a.activation(out=r_.t[:], in_=self.PS(b), func=AF.Relu), reads=[self.pst[b]], writes=[r_.tok])
                        S.op("dve", lambda v: v.tensor_tensor(out=aT.t[:, fc, :], in0=r_.t[:], in1=self.PS(b), op=ALU.mult),
                             reads=[r_.tok, self.pst[b]], writes=[aT.tok])
                for nh in range(2):
                    accb = [self.psbank() for _ in range(4)]
                    for fg in range(4):
                        w_ = wload(wdn[:, fg * 8:(fg + 1) * 8, nh * 512:(nh + 1) * 512], self.wtok[("down", l)])
                        for jj in range(8):
                            fc = fg * 8 + jj
                            for sub in range(4):
                                b = accb[sub]
                                S.op("pe", lambda p: p.matmul(self.PS(b), lhsT=aT.t[:, fc, sub * 128:(sub + 1) * 128], rhs=w_.t[:, jj, :], start=(fc == 0), stop=(fc == 31)),
                                     reads=[aT.tok, w_.tok], writes=[self.pst[b]])
                    for sub in range(4):
                        x_ = xs[sub]
                        b = accb[sub]
                        S.op("dve", lambda v: v.tensor_tensor(out=x_.t[:, nh * 512:(nh + 1) * 512], in0=self.PS(b), in1=x_.t[:, nh * 512:(nh + 1) * 512], op=ALU.add),
                             reads=[self.pst[b], x_.tok], writes=[x_.tok])
                xT = TT[nT % 2]
                nT += 1
                for sub in range(4):
                    m_ = mb[nm % 3]
                    nm += 1
                    S.op("pool", lambda g: g.tensor_copy(out=m_.t[:], in_=xs[sub].t[:]), reads=[xs[sub].tok], writes=[m_.tok])
                    self.transpose8(m_.t[:], [m_.tok], xT.t[:, :, sub * 128:(sub + 1) * 128], [xT.toks[sub]], evac=("dve" if sub % 2 == 0 else "act"))
                for nh in range(2):
                    w_ = wload(wgt[:, :, nh * 512:(nh + 1) * 512], self.wtok[("gate", l)])
                    for sub in range(4):
                        bg = self.psbank()
                        for kc in range(8):
                            S.op("pe", lambda p: p.matmul(self.PS(bg), lhsT=xT.t[:, kc, sub * 128:(sub + 1) * 128], rhs=w_.t[:, kc, :], start=(kc == 0), stop=(kc == 7)),
                                 reads=[xT.toks[sub], w_.tok], writes=[self.pst[bg]])
                        bp = self.psbank()
                        for kc in range(2):
                            S.op("pe", lambda p: p.matmul(self.PS(bp), lhsT=p_.t[:, kc, sub * 128:(sub + 1) * 128], rhs=wple.t[:, kc, nh * 512:(nh + 1) * 512], start=(kc == 0), stop=(kc == 1)),
                                 reads=[p_.tok, wple.tok], writes=[self.pst[bp]])
                        r_ = rr[nr % 3]
                        nr += 1
                        S.op("act", lambda a: a.activation(out=r_.t[:], in_=self.PS(bg), func=AF.Sigmoid), reads=[self.pst[bg]], writes=[r_.tok])
                        S.op("dve", lambda v: v.tensor_tensor(out=r_.t[:], in0=r_.t[:], in1=self.PS(bp), op=ALU.mult), reads=[r_.tok, self.pst[bp]], writes=[r_.tok])
                        x_ = xs[sub]
                        S.op("pool", lambda g: g.tensor_tensor(out=x_.t[:, nh * 512:(nh + 1) * 512], in0=x_.t[:, nh * 512:(nh + 1) * 512], in1=r_.t[:], op=ALU.add),
                             reads=[r_.tok, x_.tok], writes=[x_.tok])
                for sub in range(4):
                    ti = blk * 4 + sub
                    x_ = xs[sub]
                    if last:
                        if self.final_norm:
                            y_ = yo[sub % 2]
                            self.rmsnorm_h(x_.t[:], x_.tok, nwf, y_, ss[sub % 2], sd[sub % 2], junk)
                            S.dma("pool", self.y[ti * 128:(ti + 1) * 128, :], y_.t[:], reads=[y_.tok], writes=[self.t_y])
                        else:
                            S.dma("pool", self.y[ti * 128:(ti + 1) * 128, :], x_.t[:], reads=[x_.tok], writes=[self.t_y])
                    else:
                        S.dma("pool", self.xres[ti * 128:(ti + 1) * 128, :], x_.t[:], reads=[x_.tok], writes=[self.t_xres[ti]])


_CACHE = {}


def _get_nc(T, layers, final_norm=True):
    key = (T, tuple(layers), final_norm)
    if key not in _CACHE:
        b = Builder(T, list(layers), final_norm)
        nc = b.build()
        print(f"[kernel] built T={T} layers={layers}: {b.S.ninst} instr, {b.S.nwaits} waits; per-engine "
              + str({n: e["cnt"] for n, e in b.S.eng.items()}), flush=True)
        _CACHE[key] = nc
    return _CACHE[key]


def run(inputs, T, layers, ncores, final_norm=True, x_override=None):
    layers = list(layers)
    nc = _get_nc(T, layers, final_norm)
    consts = make_consts(T)
    f32 = lambda a: np.ascontiguousarray(np.asarray(a), dtype=np.float32)
    mall, ev, mev, od, mod = layer_maps(layers)
    shared = {}
    for k in ("norm_mix", "norm_mlp", "w_mlp_up", "w_mlp_down", "w_ple_proj", "w_ple_gate"):
        shared[k] = f32(np.asarray(inputs[k])[layers])
    for k in ("a_log", "dt_bias", "gdn_norm", "lam_q1", "lam_k1", "lam_q2", "lam_k2", "diff_norm", "w_out_even"):
        shared[k] = f32(np.asarray(inputs[k])[ev])
    for k in ("w_in_odd", "b_forget", "w_out_odd"):
        shared[k] = f32(np.asarray(inputs[k])[od])
    shared["norm_final"] = f32(inputs["norm_final"]).reshape(1, D)
    wie = f32(np.asarray(inputs["w_in_even"])[ev])
    idx = []
    for which in range(2):
        for h in range(4):
            for m in range(2):
                for d in range(64):
                    idx.append(2056 + which * 512 + h * 128 + m * 64 + (d + 32) % 64)
    shared["w_in_even"] = np.ascontiguousarray(np.concatenate([wie, wie[:, :, idx]], axis=2))
    cw = f32(np.asarray(inputs["conv_w"])[ev])
    shared["convw"] = np.ascontiguousarray(cw.reshape(len(ev), 4, 12, 128).transpose(0, 3, 2, 1))
    shared.update(consts)
    x = np.asarray(inputs["x"]) if x_override is None else x_override
    p = np.asarray(inputs["p"])
    pos = np.asarray(inputs["positions"])
    in_maps = []
    for b in range(ncores):
        m = dict(shared)
        m["x"] = f32(x[b, :T])
        m["pT"] = f32(np.transpose(p[layers, b, :T, :], (0, 2, 1)))
        m["pos"] = np.ascontiguousarray(pos[b, :T].reshape(1, T).astype(np.int32))
        in_maps.append(m)
    res = run_bass_kernel_spmd(nc, in_maps, core_ids=list(range(ncores)))
    return np.stack([np.asarray(r["y"]) for r in res.results], axis=0)


def kernel(**inputs):
    return run(inputs, 4096, list(range(DEPTH)), 8, final_norm=True).astype(np.float32)
```

```python
import math
import os
from contextlib import ExitStack
import numpy as np
import concourse.bass as bass
import concourse.mybir as mybir
from concourse.bass_utils import run_bass_kernel_spmd

F32 = mybir.dt.float32
BF16 = mybir.dt.bfloat16
I32 = mybir.dt.int32
ALU = mybir.AluOpType
AF = mybir.ActivationFunctionType
AX = mybir.AxisListType

D = 1024
DFF = 4096
DEPTH = 4
EVEN_IN = 3592
ODD_IN = 4104
EPS = 1e-6
NEG = -30000.0


class Tok:
    __slots__ = ("name", "w", "r")

    def __init__(self, name=""):
        self.name = name
        self.w = None
        self.r = []


class Buf:
    def __init__(self, t, n=1, name=""):
        self.t = t
        self.toks = [Tok(f"{name}{i}") for i in range(n)]
        self.tok = self.toks[0]


class Sched:
    NSLOT = 10
    NSPARE = 76
    LIMIT = int(os.environ.get("K_LIMIT", "12000"))

    def __init__(self, nc):
        self.nc = nc
        self.sems = []
        self.eng = {}
        self._ctx = []
        names = ["pe", "dve", "act", "pool", "sp"]
        handles = [nc.tensor, nc.vector, nc.scalar, nc.gpsimd, nc.sync]
        for n, h in zip(names, handles):
            self.eng[n] = dict(h=h, sem=self._newsem(n), cnt=0, clock=None, dslots=[], dnext=0)
        for n in ["sp", "pool"]:
            e = self.eng[n]
            for i in range(self.NSLOT):
                e["dslots"].append(dict(sem=self._newsem(f"{n}d{i}"), cnt=0))
        self.spare = [self._newsem(f"sp{i}") for i in range(self.NSPARE)]
        self.final = {}
        ns = len(self.sems)
        for e in self.eng.values():
            e["clock"] = np.zeros(ns, dtype=np.int64)
            e["own"] = {e["sem"]}
        self.nwaits = 0
        self.ninst = 0
        self._war = []
        self.psn = 0

    def _newsem(self, name):
        cm = self.nc.semaphore(name)
        h = cm.__enter__()
        self._ctx.append(cm)
        self.sems.append(h)
        return len(self.sems) - 1

    def _deps(self, reads, writes):
        deps = []
        for t in reads:
            if t.w is not None:
                deps.append(t.w)
        self._war = []
        for t in writes:
            if t.w is not None:
                deps.append(t.w)
            self._war.extend(t.r)
        return deps

    def _wait_for(self, en, deps):
        e = self.eng[en]
        clock = e["clock"]
        own = e["own"]
        war = [d for d in self._war if d[0] not in own]
        self._war = []
        need = {}
        for (s, v, ck) in list(deps) + war:
            if en == "pe" and s in own:
                continue
            if clock[s] >= v:
                continue
            if need.get(s, (0, None))[0] < v:
                need[s] = (v, ck)
        for s, (v, ck) in need.items():
            if clock[s] >= v:
                continue
            e["h"].wait_ge(self.sems[s], int(v))
            self.nwaits += 1
            clock[s] = v
            if ck is not None:
                np.maximum(clock, ck, out=clock)

    def _record(self, ev, reads, writes):
        for t in reads:
            t.r = [d for d in t.r if d[0] != ev[0]]
            t.r.append(ev)
        for t in writes:
            t.w = ev
            t.r = []

    def op(self, en, fn, reads=(), writes=()):
        e = self.eng[en]
        if e["cnt"] >= self.LIMIT:
            self.final[e["sem"]] = e["cnt"]
            e["sem"] = self.spare.pop()
            e["own"].add(e["sem"])
            e["cnt"] = 0
        self._wait_for(en, self._deps(reads, writes))
        ins = fn(e["h"])
        e["cnt"] += 1
        ins.then_inc(self.sems[e["sem"]], 1)
        self.ninst += 1
        ev = (e["sem"], e["cnt"], e["clock"].copy())
        self._record(ev, reads, writes)
        return ev

    def dma(self, en, out, in_, reads=(), writes=(), nowaw=False, **kw):
        e = self.eng[en]
        if nowaw:
            for t in writes:
                t.w = None
        slot = e["dslots"][e["dnext"] % self.NSLOT]
        e["dnext"] += 1
        if slot["cnt"] * 16 >= self.LIMIT:
            self.final[slot["sem"]] = slot["cnt"] * 16
            slot["sem"] = self.spare.pop()
            slot["cnt"] = 0
        deps = self._deps(reads, writes)
        if slot["cnt"] > 0:
            deps.append((slot["sem"], slot["cnt"] * 16, None))
        self._wait_for(en, deps)
        ins = e["h"].dma_start(out=out, in_=in_, **kw)
        slot["cnt"] += 1
        ins.then_inc(self.sems[slot["sem"]], 16)
        self.ninst += 1
        ev = (slot["sem"], slot["cnt"] * 16, e["clock"].copy())
        self._record(ev, reads, writes)
        return ev

    def barrier(self):
        cur = np.zeros(len(self.sems), dtype=np.int64)
        for s_, v_ in self.final.items():
            cur[s_] = v_
        for e in self.eng.values():
            cur[e["sem"]] = e["cnt"]
            for sl in e["dslots"]:
                cur[sl["sem"]] = sl["cnt"] * 16
        for en, e in self.eng.items():
            clock = e["clock"]
            for s in range(len(self.sems)):
                if s in e["own"]:
                    continue
                if clock[s] < cur[s]:
                    e["h"].wait_ge(self.sems[s], int(cur[s]))
                    self.nwaits += 1
                    clock[s] = cur[s]

    def close(self):
        for cm in reversed(self._ctx):
            cm.__exit__(None, None, None)


def make_consts(T):
    import ml_dtypes
    c = {}
    c["ident_f"] = np.eye(128, dtype=np.float32)
    tri = np.where(np.arange(128)[:, None] <= np.arange(128)[None, :], 0.0, NEG).astype(np.float32)
    c["trimask"] = tri
    esel = np.zeros((128, 8, 128), dtype=np.float32)
    for h in range(8):
        esel[:, h, h] = 1.0
    c["esel"] = esel
    esel2 = np.zeros((128, 4, 128), dtype=np.float32)
    for h in range(4):
        for r in range(128):
            esel2[r, h, 2 * h + r // 64] = 1.0
    c["esel2"] = esel2
    r = np.arange(128)
    d = r % 64
    invf = (10000.0 ** (-(2.0 * (d % 32)) / 64.0)).astype(np.float32)
    sgn = np.where(d < 32, -1.0, 1.0).astype(np.float32)
    c["rope"] = np.stack([invf, sgn], axis=1).astype(np.float32)
    ii = np.arange(128)[:, None]
    jj = np.arange(128)[None, :]
    c["negup"] = np.where(jj >= ii, 0.0, NEG).astype(np.float32)
    c["poslow"] = np.where(jj < ii, 0.0, -NEG).astype(np.float32)
    cm = np.ones((4, T), dtype=np.float32)
    cm[:, ::128] = 0.0
    c["cmask"] = cm
    return c


import os
STOP = os.environ.get("K_STOP", "")


class StopBuild(Exception):
    pass


class LIdx:
    def __init__(self, ap, mapping):
        self.ap = ap
        self.m = mapping

    def __getitem__(self, l):
        return self.ap[self.m[l]]


def layer_maps(layers):
    mall = {l: i for i, l in enumerate(layers)}
    ev = [l // 2 for l in layers if l % 2 == 0]
    od = [l // 2 for l in layers if l % 2 == 1]
    mev = {j: i for i, j in enumerate(ev)}
    mod = {j: i for i, j in enumerate(od)}
    return mall, (ev or [0]), mev, (od or [0]), mod


class Builder:
    def __init__(self, T, layers, final_norm=True):
        self.T = T
        self.NT = T // 128
        self.NB = T // 512
        self.layers = layers
        self.final_norm = final_norm
        self.nc = bass.Bass("TRN2", target_bir_lowering=False)
        self.S = None

    def declare(self):
        nc, T = self.nc, self.T
        I = lambda n, s, d=F32: nc.dram_tensor(n, s, d, kind="ExternalInput").ap()
        self.x = I("x", [T, D])
        mall, ev, mev, od, mod = layer_maps(self.layers)
        NL, NE, NO = len(self.layers), len(ev), len(od)
        A = lambda n, s: LIdx(I(n, [NL] + s), mall)
        E = lambda n, s: LIdx(I(n, [NE] + s), mev)
        O = lambda n, s: LIdx(I(n, [NO] + s), mod)
        self.pT = A("pT", [256, T])
        self.pos = I("pos", [1, T], I32)
        self.norm_mix = A("norm_mix", [D])
        self.norm_mlp = A("norm_mlp", [D])
        self.norm_final = I("norm_final", [1, D])
        self.w_in_even = E("w_in_even", [D, EVEN_IN + 1024])
        self.convw = E("convw", [128, 12, 4])
        self.a_log = E("a_log", [4])
        self.dt_bias = E("dt_bias", [4])
        self.gdn_norm = E("gdn_norm", [128])
        self.lam = [E(n, [64]) for n in ("lam_q1", "lam_k1", "lam_q2", "lam_k2")]
        self.diff_norm = E("diff_norm", [128])
        self.w_out_even = E("w_out_even", [D, D])
        self.w_in_odd = O("w_in_odd", [D, ODD_IN])
        self.b_forget = O("b_forget", [8])
        self.w_out_odd = O("w_out_odd", [D, D])
        self.w_mlp_up = A("w_mlp_up", [D, DFF])
        self.w_mlp_down = A("w_mlp_down", [DFF, D])
        self.w_ple_proj = A("w_ple_proj", [256, D])
        self.w_ple_gate = A("w_ple_gate", [D, D])
        self.c_ident = I("ident_f", [128, 128])
        self.c_tri = I("trimask", [128, 128])
        self.c_esel = I("esel", [128, 8, 128])
        self.c_esel2 = I("esel2", [128, 4, 128])
        self.c_rope = I("rope", [128, 2])
        self.c_negup = I("negup", [128, 128])
        self.c_poslow = I("poslow", [128, 128])
        self.c_cmask = I("cmask", [4, T])
        self.y = nc.dram_tensor("y", [T, D], F32, kind="ExternalOutput").ap()
        Sc = lambda n, s, d=F32: nc.dram_tensor(n, s, d, kind="Internal").ap()
        self.xres = Sc("xres", [T, D])
        self.obuf = Sc("obuf", [T, D])
        self.gbuf = Sc("gbuf", [T, D], BF16)
        self.qTd = Sc("qTd", [8, 128, T], BF16)
        self.kTd = Sc("kTd", [8, 128, T], BF16)
        self.vaug = Sc("vaug", [T, 8 * 129], BF16)
        self.obuf2 = Sc("obuf2", [T, D])
        self.gq = Sc("gq", [4, 128, T], BF16)
        self.gk = Sc("gk", [4, 128, T], BF16)
        self.gv = Sc("gv", [4, 128, T], BF16)
        self.gcd = Sc("gcd", [4, T])
        self.egcd = Sc("egcd", [4, T])
        self.t_g = Tok("g")
        self.t_gcd = Tok("gcd")
        self.t_obuf2 = [Tok(f"obuf2{i}") for i in range(self.NT)]
        self.qbd = Sc("qbd", [8, 5, T], BF16)
        self.kbd = Sc("kbd", [8, 5, T], BF16)
        self.wb = {}
        for l in self.layers:
            j = l // 2
            if l % 2 == 0:
                self.wb[("in", l)] = (Sc(f"wbin{l}", [D, EVEN_IN + 1024], BF16), self.w_in_even[j])
                self.wb[("out", l)] = (Sc(f"wbout{l}", [D, D], BF16), self.w_out_even[j])
            else:
                self.wb[("in", l)] = (Sc(f"wbin{l}", [D, ODD_IN], BF16), self.w_in_odd[j])
                self.wb[("out", l)] = (Sc(f"wbout{l}", [D, D], BF16), self.w_out_odd[j])
            self.wb[("up", l)] = (Sc(f"wbup{l}", [D, DFF], BF16), self.w_mlp_up[l])
            self.wb[("down", l)] = (Sc(f"wbdn{l}", [DFF, D], BF16), self.w_mlp_down[l])
            self.wb[("ple", l)] = (Sc(f"wbple{l}", [256, D], BF16), self.w_ple_proj[l])
            self.wb[("gate", l)] = (Sc(f"wbgate{l}", [D, D], BF16), self.w_ple_gate[l])
        self.wtok = {k: Tok(str(k)) for k in self.wb}
        self.t_xres = [Tok(f"xres{i}") for i in range(self.NT)]
        self.t_obuf = [Tok(f"obuf{i}") for i in range(self.NT)]
        self.t_gbuf = [Tok(f"gbuf{i}") for i in range(self.NT)]
        self.t_qT = [Tok(f"qT{h}") for h in range(8)]
        self.t_kT = [Tok(f"kT{h}") for h in range(8)]
        self.t_vaug = Tok("vaug")
        self.t_qbd = Tok("qbd")
        self.t_kbd = Tok("kbd")
        self.t_y = Tok("y")

    def sb(self, es, name, shape, dt, n=1):
        self._uid = getattr(self, "_uid", 0) + 1
        name = f"{name}_u{self._uid}"
        t = es.enter_context(self.nc.sbuf_tensor(name, shape, dt))
        return Buf(t, n, name)

    def ring(self, es, name, shape, dt, k, n=1):
        return [self.sb(es, f"{name}{i}", shape, dt, n) for i in range(k)]

    def psbank(self):
        i = self.S.psn % 8
        self.S.psn += 1
        return i

    def PS(self, b):
        return self.ps[:, b, :]

    def PSB(self, b):
        return self.ps[:, b, :].bitcast(BF16).rearrange("p (c n) -> p c n", n=128)

    def transpose8(self, src, src_toks, dstT, dst_toks, evac="dve"):
        S = self.S
        b = self.psbank()
        pv = self.PSB(b)
        for c in range(8):
            S.op("pe", lambda p: p.transpose(out=pv[:, c, :], in_=src[:, c * 128:(c + 1) * 128], identity=self.identb.t[:]),
                 reads=list(src_toks) + [self.identb.tok], writes=[self.pst[b]])
        if evac == "dve":
            S.op("dve", lambda v: v.tensor_copy(out=dstT, in_=pv), reads=[self.pst[b]], writes=dst_toks)
        else:
            S.op("act", lambda a: a.copy(out=dstT, in_=pv), reads=[self.pst[b]], writes=dst_toks)

    def rmsnorm_h(self, xt, xtok, wtile, hb, ss, sd, junk):
        S = self.S
        S.op("act", lambda a: a.activation(out=junk.t[:], in_=xt, func=AF.Square, accum_out=ss.t[:]),
             reads=[xtok], writes=[junk.tok, ss.tok])
        S.op("act", lambda a: a.activation(out=sd.t[:], in_=ss.t[:], func=AF.Sqrt, scale=1.0 / D, bias=self.epsb.t[:]),
             reads=[ss.tok, self.epsb.tok], writes=[sd.tok])
        S.op("dve", lambda v: v.reciprocal(out=sd.t[:], in_=sd.t[:]), reads=[sd.tok], writes=[sd.tok])
        S.op("dve", lambda v: v.scalar_tensor_tensor(out=hb.t[:], in0=xt, scalar=sd.t[:], in1=wtile.t[:], op0=ALU.mult, op1=ALU.mult),
             reads=[xtok, sd.tok, wtile.tok], writes=[hb.tok])

    def build(self):
        nc = self.nc
        self.declare()
        self.S = S = Sched(nc)
        with ExitStack() as es0:
            self.ps = es0.enter_context(nc.psum_tensor("ps", [128, 8, 512], F32))
            self.pst = [Tok(f"ps{i}") for i in range(8)]
            self.identb = self.sb(es0, "identb", [128, 128], BF16)
            self.identf = self.sb(es0, "identf", [128, 128], F32)
            self.trib = self.sb(es0, "trib", [128, 128], BF16)
            self.eselb = self.sb(es0, "eselb", [128, 8, 128], BF16)
            self.esel2b = self.sb(es0, "esel2b", [128, 4, 128], BF16)
            S.dma("pool", self.esel2b.t[:], self.c_esel2, writes=[self.esel2b.tok])
            self.epsb = self.sb(es0, "epsb", [128, 1], F32)
            self.zerob = self.sb(es0, "zerob", [128, 512], BF16)
            S.dma("sp", self.identf.t[:], self.c_ident, writes=[self.identf.tok])
            S.dma("pool", self.identb.t[:], self.c_ident, writes=[self.identb.tok])
            S.dma("pool", self.trib.t[:], self.c_tri, writes=[self.trib.tok])
            S.dma("pool", self.eselb.t[:], self.c_esel, writes=[self.eselb.tok])
            S.op("dve", lambda v: v.memset(self.epsb.t[:], EPS), writes=[self.epsb.tok])
            S.op("dve", lambda v: v.memset(self.zerob.t[:], 0.0), writes=[self.zerob.tok])
            xv = self.x.rearrange("(n p) c -> p n c", p=128)
            xr = self.xres.rearrange("(n p) c -> p n c", p=128)
            for i in range(0, self.NT, 4):
                S.dma("sp", xr[:, i:i + 4, :], xv[:, i:i + 4, :], writes=self.t_xres[i:i + 4])
            for l in self.layers:
                for kind in ("in", "out", "up", "down", "ple", "gate"):
                    dst, src = self.wb[(kind, l)]
                    rows = dst.shape[0]
                    step = 512
                    for r0 in range(0, rows, step):
                        r1 = min(rows, r0 + step)
                        S.dma("pool", dst[r0:r1, :], src[r0:r1, :], writes=[self.wtok[(kind, l)]])
            for li, l in enumerate(self.layers):
                last = (li == len(self.layers) - 1)
                self.stopped = False
                if l % 2 == 1:
                    self.odd_layer(l)
                else:
                    self.even_layer(l)
                if STOP == "E3":
                    self.stopped = True
                if not self.stopped:
                    self.tail(l, last)
            S._wait_for("sp", [self.t_y.w] if self.t_y.w else [])
            S.barrier()
        S.close()
        return nc

    def odd_layer(self, l):
        nc, S, T, NT, NB = self.nc, self.S, self.T, self.NT, self.NB
        j = l // 2
        scale = 128 ** -0.5
        with ExitStack() as esL:
            fT = self.sb(esL, "fT", [8, T], F32)
            qsq = self.sb(esL, "qsq", [8, T], F32)
            ksq = self.sb(esL, "ksq", [8, T], F32)
            S.barrier()
            with ExitStack() as es:
                WP = ODD_IN + 120
                win = self.sb(es, "win", [128, 8, WP], BF16)
                wsrc = self.wb[("in", l)][0].rearrange("(c p) n -> p c n", p=128)
                S.op("pool", lambda g: g.memset(win.t[:, :, ODD_IN:WP], 0.0), writes=[win.tok])
                for kc in range(8):
                    S.dma("sp", win.t[:, kc, 0:ODD_IN], wsrc[:, kc, :], reads=[self.wtok[("in", l)]], writes=[win.tok])
                nw = self.sb(es, "nw", [128, D], F32)
                S.dma("sp", nw.t[:], self.norm_mix[l].partition_broadcast(128), writes=[nw.tok])
                xr = self.ring(es, "xr", [128, D], F32, 3)
                hb = self.ring(es, "hb", [128, D], BF16, 2)
                junk = self.sb(es, "junk", [128, D], BF16)
                ss = self.ring(es, "ss", [128, 1], F32, 2)
                sd = self.ring(es, "sd", [128, 1], F32, 2)
                hT = self.ring(es, "hT", [128, 8, 512], BF16, 2, n=4)
                qt = self.ring(es, "qt", [128, 512], BF16, 4)
                sq = self.ring(es, "sq", [128, 512], BF16, 3)
                vt = self.ring(es, "vt", [128, 8, 129], BF16, 2)
                gt = self.ring(es, "gt", [128, D], BF16, 2)
                for v_ in vt:
                    S.op("pool", lambda g: g.memset(v_.t[:], 1.0), writes=[v_.tok])
                nx = 0
                nq = 0
                nv = 0
                for blk in range(NB):
                    t0 = blk * 512
                    h_T = hT[blk % 2]
                    for sub in range(4):
                        ti = blk * 4 + sub
                        xb_ = xr[nx % 3]
                        hb_ = hb[nx % 2]
                        S.dma("sp", xb_.t[:], self.xres[ti * 128:(ti + 1) * 128, :], reads=[self.t_xres[ti]], writes=[xb_.tok])
                        self.rmsnorm_h(xb_.t[:], xb_.tok, nw, hb_, ss[nx % 2], sd[nx % 2], junk)
                        self.transpose8(hb_.t[:], [hb_.tok], h_T.t[:, :, sub * 128:(sub + 1) * 128], [h_T.toks[sub]],
                                        evac=("dve" if sub % 2 == 0 else "act"))
                        nx += 1
                    bq = self.psbank()
                    bk = self.psbank()
                    for which in range(2):
                        bstat = bq if which == 0 else bk
                        for h in range(8):
                            b = self.psbank()
                            while b in (bq, bk):
                                b = self.psbank()
                            c0 = which * 1024 + h * 128
                            for kc in range(8):
                                S.op("pe", lambda p: p.matmul(self.PS(b), lhsT=win.t[:, kc, c0:c0 + 128], rhs=h_T.t[:, kc, :], start=(kc == 0), stop=(kc == 7)),
                                     reads=[win.tok] + h_T.toks, writes=[self.pst[b]])
                            q_ = qt[nq % 4]
                            s_ = sq[nq % 3]
                            nq += 1
                            S.op("act", lambda a: a.activation(out=q_.t[:], in_=self.PS(b), func=AF.Copy, scale=(scale if which == 0 else 1.0)),
                                 reads=[self.pst[b]], writes=[q_.tok])
                            dst = (self.qTd if which == 0 else self.kTd)[h][:, t0:t0 + 512]
                            S.dma("pool", dst, q_.t[:], reads=[q_.tok], writes=[(self.t_qT if which == 0 else self.t_kT)[h]])
                            S.op("dve", lambda v: v.tensor_tensor(out=s_.t[:], in0=q_.t[:], in1=q_.t[:], op=ALU.mult), reads=[q_.tok], writes=[s_.tok])
                            S.op("pe", lambda p: p.matmul(self.PS(bstat), lhsT=self.eselb.t[:, h, :], rhs=s_.t[:], start=(h == 0), stop=(h == 7)),
                                 reads=[self.eselb.tok, s_.tok], writes=[self.pst[bstat]])
                        dstat = (qsq if which == 0 else ksq)
                        S.op("dve", lambda v: v.tensor_copy(out=dstat.t[:, t0:t0 + 512], in_=self.ps[0:8, bstat, :]), reads=[self.pst[bstat]], writes=[dstat.tok])
                    b = self.psbank()
                    for kc in range(8):
                        S.op("pe", lambda p: p.matmul(self.PS(b), lhsT=win.t[:, kc, 4096:4096 + 128], rhs=h_T.t[:, kc, :], start=(kc == 0), stop=(kc == 7)),
                             reads=[win.tok] + h_T.toks, writes=[self.pst[b]])
                    S.op("dve", lambda v: v.tensor_copy(out=fT.t[:, t0:t0 + 512], in_=self.ps[0:8, b, :]), reads=[self.pst[b]], writes=[fT.tok])
                    for sub in range(4):
                        ti = blk * 4 + sub
                        v_ = vt[nv % 2]
                        g_ = gt[nv % 2]
                        nv += 1
                        for nh in range(2):
                            b = self.psbank()
                            for kc in range(8):
                                S.op("pe", lambda p: p.matmul(self.PS(b), lhsT=h_T.t[:, kc, sub * 128:(sub + 1) * 128], rhs=win.t[:, kc, 2048 + nh * 512:2048 + (nh + 1) * 512], start=(kc == 0), stop=(kc == 7)),
                                     reads=[win.tok, h_T.toks[sub]], writes=[self.pst[b]])
                            S.op("dve", lambda v: v.tensor_copy(out=v_.t[:, nh * 4:(nh + 1) * 4, 0:128], in_=self.PS(b).rearrange("p (h c) -> p h c", c=128)),
                                 reads=[self.pst[b]], writes=[v_.tok])
                        S.dma("pool", self.vaug[ti * 128:(ti + 1) * 128, :], v_.t[:].rearrange("p h c -> p (h c)"), reads=[v_.tok], writes=[self.t_vaug], nowaw=True)
                        for nh in range(2):
                            b = self.psbank()
                            for kc in range(8):
                                S.op("pe", lambda p: p.matmul(self.PS(b), lhsT=h_T.t[:, kc, sub * 128:(sub + 1) * 128], rhs=win.t[:, kc, 3072 + nh * 512:3072 + (nh + 1) * 512], start=(kc == 0), stop=(kc == 7)),
                                     reads=[win.tok, h_T.toks[sub]], writes=[self.pst[b]])
                            S.op("act", lambda a: a.activation(out=g_.t[:, nh * 512:(nh + 1) * 512], in_=self.PS(b), func=AF.Sigmoid),
                                 reads=[self.pst[b]], writes=[g_.tok])
                        S.dma("pool", self.gbuf[ti * 128:(ti + 1) * 128, :], g_.t[:], reads=[g_.tok], writes=[self.t_gbuf[ti]])
            S.barrier()
            with ExitStack() as es:
                bf = self.sb(es, "bf", [8, 1], F32)
                S.dma("sp", bf.t[:], self.b_forget[j].rearrange("(h o) -> h o", o=1), writes=[bf.tok])
                xa = self.sb(es, "xa", [8, T], F32)
                ya = self.sb(es, "ya", [8, T], F32)
                za = self.sb(es, "za", [8, T], F32)
                ones = self.sb(es, "ones", [8, T], F32)
                km = self.sb(es, "km", [8, 1], F32)
                b1 = self.sb(es, "b1", [8, T], BF16)
                b2 = self.sb(es, "b2", [8, T], BF16)
                b3 = self.sb(es, "b3", [8, T], BF16)
                onesb = self.sb(es, "onesb", [8, T], BF16)
                S.op("dve", lambda v: v.memset(ones.t[:], 1.0), writes=[ones.tok])
                S.op("pool", lambda g: g.memset(onesb.t[:], 1.0), writes=[onesb.tok])
                S.op("dve", lambda v: v.tensor_scalar(out=xa.t[:], in0=fT.t[:], scalar1=bf.t[:], scalar2=None, op0=ALU.add), reads=[fT.tok, bf.tok], writes=[xa.tok])
                S.op("dve", lambda v: v.scalar_tensor_tensor(out=ya.t[:], in0=xa.t[:], scalar=-1.0, in1=xa.t[:], op0=ALU.mult, op1=ALU.max), reads=[xa.tok], writes=[ya.tok])
                S.op("act", lambda a: a.activation(out=ya.t[:], in_=ya.t[:], func=AF.Exp, scale=-1.0), reads=[ya.tok], writes=[ya.tok])
                S.op("act", lambda a: a.activation(out=ya.t[:], in_=ya.t[:], func=AF.Ln, bias=1.0), reads=[ya.tok], writes=[ya.tok])
                S.op("dve", lambda v: v.scalar_tensor_tensor(out=za.t[:], in0=xa.t[:], scalar=0.0, in1=ya.t[:], op0=ALU.min, op1=ALU.subtract), reads=[xa.tok, ya.tok], writes=[za.tok])
                S.op("dve", lambda v: v.tensor_tensor_scan(out=xa.t[:], data0=ones.t[:], data1=za.t[:], initial=0.0, op0=ALU.mult, op1=ALU.add), reads=[ones.tok, za.tok, xa.tok], writes=[xa.tok])
                S.op("dve", lambda v: v.tensor_reduce(out=km.t[:], in_=ksq.t[:], axis=AX.X, op=ALU.max), reads=[ksq.tok], writes=[km.tok])
                S.op("dve", lambda v: v.tensor_scalar(out=ya.t[:], in0=qsq.t[:], scalar1=128.0, scalar2=km.t[:], op0=ALU.mult, op1=ALU.add), reads=[qsq.tok, km.tok], writes=[ya.tok])
                S.op("dve", lambda v: v.scalar_tensor_tensor(out=ya.t[:], in0=ya.t[:], scalar=-0.5 * scale, in1=xa.t[:], op0=ALU.mult, op1=ALU.add), reads=[ya.tok, xa.tok], writes=[ya.tok])
                S.op("dve", lambda v: v.tensor_copy(out=b1.t[:], in_=ya.t[:]), reads=[ya.tok], writes=[b1.tok])
                S.op("dve", lambda v: v.tensor_tensor(out=b2.t[:], in0=ya.t[:], in1=b1.t[:], op=ALU.subtract), reads=[ya.tok, b1.tok], writes=[b2.tok])
                S.dma("sp", self.qbd[:, 0, :], b1.t[:], reads=[b1.tok], writes=[self.t_qbd])
                S.dma("sp", self.qbd[:, 1, :], b2.t[:], reads=[b2.tok], writes=[self.t_qbd])
                for r in (2, 3, 4):
                    S.dma("sp", self.qbd[:, r, :], onesb.t[:], reads=[onesb.tok], writes=[self.t_qbd])
                for r in (0, 1):
                    S.dma("sp", self.kbd[:, r, :], onesb.t[:], reads=[onesb.tok], writes=[self.t_kbd])
                S.op("dve", lambda v: v.tensor_scalar(out=za.t[:], in0=xa.t[:], scalar1=-1.0, scalar2=None, op0=ALU.mult), reads=[xa.tok], writes=[za.tok])
                S.op("dve", lambda v: v.tensor_copy(out=b1.t[:], in_=za.t[:]), reads=[za.tok, b1.tok], writes=[b1.tok])
                S.op("dve", lambda v: v.tensor_tensor(out=za.t[:], in0=za.t[:], in1=b1.t[:], op=ALU.subtract), reads=[za.tok, b1.tok], writes=[za.tok])
                S.op("dve", lambda v: v.tensor_copy(out=b2.t[:], in_=za.t[:]), reads=[za.tok, b2.tok], writes=[b2.tok])
                S.op("dve", lambda v: v.tensor_tensor(out=b3.t[:], in0=za.t[:], in1=b2.t[:], op=ALU.subtract), reads=[za.tok, b2.tok], writes=[b3.tok])
                S.dma("sp", self.kbd[:, 2, :], b1.t[:], reads=[b1.tok], writes=[self.t_kbd])
                S.dma("sp", self.kbd[:, 3, :], b2.t[:], reads=[b2.tok], writes=[self.t_kbd])
                S.dma("sp", self.kbd[:, 4, :], b3.t[:], reads=[b3.tok], writes=[self.t_kbd])
                S.barrier()
        S.barrier()
        with ExitStack() as es:
            self.attention(es, nheads=8, kdim=128, bias_rows=5, vhead=lambda h: h)

    def attention(self, es, nheads, kdim, bias_rows, vhead, dest=None, dtoks=None):
        nc, S, T, NT, NB = self.nc, self.S, self.T, self.NT, self.NB
        NBUF = 2
        qT = self.ring(es, "aqT", [128, T], BF16, NBUF)
        kT = self.ring(es, "akT", [128, T], BF16, NBUF)
        vA = self.ring(es, "avA", [128, NT, 129], BF16, NBUF)
        sepbias = (kdim == 128)
        if sepbias:
            QB = self.ring(es, "aQB", [128, T], BF16, NBUF)
            KB = self.ring(es, "aKB", [128, T], BF16, NBUF)
            for b_ in QB + KB:
                S.op("pool", lambda g: g.memset(b_.t[:], 0.0), writes=[b_.tok])
        if not sepbias:
            for b_ in qT + kT:
                S.op("pool", lambda g: g.memset(b_.t[:], 0.0), writes=[b_.tok])
        pT = self.ring(es, "apT", [128, 512], BF16, 3)
        ot = self.ring(es, "aot", [128, 4, 128], F32, 2)
        rl = self.ring(es, "arl", [128, 1], F32, 4)
        sbanks = [0, 1, 2]
        accsets = [(3, 4), (5, 6)]
        ns = 0
        nqb = 0
        nr = 0
        npt = 0
        vsrc = self.vaug.rearrange("(n p) (h c) -> p n h c", p=128, c=129)
        if dest is None:
            dest, dtoks = self.obuf, self.t_obuf
        osrc = dest.rearrange("(n p) c -> p n c", p=128)
        for h in range(nheads):
            q_, k_, v_ = qT[h % NBUF], kT[h % NBUF], vA[h % NBUF]
            rows = kdim if sepbias else kdim + bias_rows
            lrows = rows
            if not sepbias:
                rows = 96
            S.dma("sp", q_.t[0:lrows, :], self.qTd[h][0:lrows, :], reads=[self.t_qT[h]], writes=[q_.tok])
            S.dma("sp", k_.t[0:lrows, :], self.kTd[h][0:lrows, :], reads=[self.t_kT[h]], writes=[k_.tok])
            for n0 in range(0, NT, 8):
                n1 = min(NT, n0 + 8)
                S.dma("sp", v_.t[:, n0:n1, :], vsrc[:, n0:n1, vhead(h), :], reads=[self.t_vaug], writes=[v_.tok])
            if sepbias:
                qb_, kb_ = QB[h % NBUF], KB[h % NBUF]
                S.dma("sp", qb_.t[0:bias_rows, :], self.qbd[h], reads=[self.t_qbd], writes=[qb_.tok])
                S.dma("sp", kb_.t[0:bias_rows, :], self.kbd[h], reads=[self.t_kbd], writes=[kb_.tok])
            for qb in range(NB):
                accs = accsets[nqb % 2]
                o_ = ot[nqb % 2]
                nqb += 1
                for b in accs:
                    S.op("pe", lambda p: p.matmul(self.PS(b), lhsT=self.zerob.t[:, 0:128], rhs=self.zerob.t[:], start=True, stop=True),
                         reads=[self.zerob.tok], writes=[self.pst[b]])
                nk = 4 * qb + 4

                def s_block(ki, sb_):
                    i = ki - 4 * qb
                    c0 = max(0, i) * 128
                    qs = slice(qb * 512 + c0, qb * 512 + 512)
                    outp = self.ps[:, sb_, c0:512]
                    diag = i >= 0
                    rd = [q_.tok, k_.tok]
                    S.op("pe", lambda p: p.matmul(outp, lhsT=k_.t[0:rows, ki * 128:(ki + 1) * 128], rhs=q_.t[0:rows, qs], start=True, stop=(not sepbias and not diag)),
                         reads=rd, writes=[self.pst[sb_]])
                    if sepbias:
                        S.op("pe", lambda p: p.matmul(outp, lhsT=kb_.t[:, ki * 128:(ki + 1) * 128], rhs=qb_.t[:, qs], start=False, stop=(not diag)),
                             reads=[qb_.tok, kb_.tok], writes=[self.pst[sb_]])
                    if diag:
                        S.op("pe", lambda p: p.matmul(self.ps[:, sb_, c0:c0 + 128], lhsT=self.identb.t[:], rhs=self.trib.t[:], start=False, stop=True),
                             reads=[self.identb.tok, self.trib.tok], writes=[self.pst[sb_]])
                    return c0

                sb_cur = sbanks[ns % 3]
                ns += 1
                c0_cur = s_block(0, sb_cur)
                for ki in range(nk):
                    if ki + 1 < nk:
                        sb_next = sbanks[ns % 3]
                        ns += 1
                        c0_next = s_block(ki + 1, sb_next)
                    p_ = pT[npt % 3]
                    npt += 1
                    c0 = c0_cur
                    S.op("act", lambda a: a.activation(out=p_.t[:, c0:512], in_=self.ps[:, sb_cur, c0:512], func=AF.Exp),
                         reads=[self.pst[sb_cur]], writes=[p_.tok])
                    for sub in range(c0 // 128, 4):
                        bacc = accs[sub // 2]
                        S.op("pe", lambda p: p.matmul(self.ps[:, bacc, (sub % 2) * 256:(sub % 2) * 256 + 129], lhsT=p_.t[:, sub * 128:(sub + 1) * 128], rhs=v_.t[:, ki, :], start=False, stop=(ki == nk - 1), skip_group_check=True),
                             reads=[p_.tok, v_.tok], writes=[self.pst[bacc]])
                    if ki + 1 < nk:
                        sb_cur, c0_cur = sb_next, c0_next
                for sub in range(4):
                    bacc = accs[sub // 2]
                    off = (sub % 2) * 256
                    r_ = rl[nr % 4]
                    nr += 1
                    S.op("dve", lambda v: v.reciprocal(out=r_.t[:], in_=self.ps[:, bacc, off + 128:off + 129]), reads=[self.pst[bacc]], writes=[r_.tok])
                    S.op("dve", lambda v: v.tensor_scalar(out=o_.t[:, sub, :], in0=self.ps[:, bacc, off:off + 128], scalar1=r_.t[:], scalar2=None, op0=ALU.mult),
                         reads=[self.pst[bacc], r_.tok], writes=[o_.tok])
                S.dma("pool", osrc[:, qb * 4:(qb + 1) * 4, h * 128:(h + 1) * 128], o_.t[:], reads=[o_.tok], writes=dtoks[qb * 4:(qb + 1) * 4])

    def even_layer(self, l):
        nc, S, T, NT, NB = self.nc, self.S, self.T, self.NT, self.NB
        j = l // 2
        lam_init = 0.8 - 0.6 * math.exp(-0.3 * l)
        WQB, WKB, WVB = 2056, 2568, 3080
        WSW = EVEN_IN
        WP = EVEN_IN + 1024
        with ExitStack() as esL:
            bT = self.sb(esL, "bT", [4, T], F32)
            aT_ = self.sb(esL, "aT_", [4, T], F32)
            esQ = esL.enter_context(ExitStack())
            qsq = self.sb(esQ, "qsq", [8, T], BF16)
            ksq = self.sb(esQ, "ksq", [8, T], BF16)
            S.barrier()
            with ExitStack() as es:
                win = self.sb(es, "win", [128, 8, WP], BF16)
                wsrc = self.wb[("in", l)][0].rearrange("(c p) n -> p c n", p=128)
                for kc in range(8):
                    S.dma("sp", win.t[:, kc, :], wsrc[:, kc, :], reads=[self.wtok[("in", l)]], writes=[win.tok])
                nw = self.sb(es, "nw", [128, D], F32)
                S.dma("sp", nw.t[:], self.norm_mix[l].partition_broadcast(128), writes=[nw.tok])
                cw = self.sb(es, "cw", [128, 12, 4], F32)
                S.dma("sp", cw.t[:], self.convw[j], writes=[cw.tok])
                invf = self.sb(es, "invf", [128, 2], F32)
                S.dma("sp", invf.t[:], self.c_rope, writes=[invf.tok])
                xr = self.ring(es, "xr", [128, D], F32, 2)
                hb = self.ring(es, "hb", [128, D], BF16, 2)
                junk = self.sb(es, "junk", [128, D], BF16)
                ss = self.ring(es, "ss", [128, 1], F32, 2)
                sd = self.ring(es, "sd", [128, 1], F32, 2)
                hT = self.ring(es, "hT", [128, 8, 512], BF16, 1, n=4)
                raw = self.ring(es, "raw", [128, 515], F32, 2)
                hist = self.sb(es, "hist", [128, 12, 3], F32)
                cv = self.ring(es, "cv", [128, 512], F32, 2)
                sl = self.ring(es, "sl", [128, 512], F32, 2)
                sq = self.ring(es, "sq", [128, 512], BF16, 2)
                rn = self.ring(es, "rn", [128, 512], F32, 1)
                qo = self.ring(es, "qo", [128, 512], BF16, 2)
                posi = self.sb(es, "posi", [128, 512], I32)
                ang = self.sb(es, "ang", [128, 512], F32)
                kf = self.sb(es, "kf", [128, 512], F32)
                ki_ = self.sb(es, "ki_", [128, 512], I32)
                ctab = self.sb(es, "ctab", [128, 512], F32)
                stab = self.sb(es, "stab", [128, 512], F32)
                t1 = self.ring(es, "t1", [128, 512], F32, 1)
                t2 = self.ring(es, "t2", [128, 512], F32, 1)
                vt = self.ring(es, "vt", [128, 8, 129], BF16, 2)
                gt = self.ring(es, "gt", [128, D], BF16, 2)
                onesb = self.sb(es, "onesb", [128, 128], BF16)
                S.op("pool", lambda g: g.memset(onesb.t[:], 1.0), writes=[onesb.tok])
                S.op("pool", lambda g: g.memset(hist.t[:], 0.0), writes=[hist.tok])
                for v_ in vt:
                    S.op("pool", lambda g: g.memset(v_.t[:], 1.0), writes=[v_.tok])
                for g_ in gt:
                    S.op("pool", lambda g: g.memset(g_.t[:], 1.0), writes=[g_.tok])
                nx = nraw = ncv = nq = nv = 0
                TWO_PI = 2.0 * math.pi
                for blk in range(NB):
                    t0 = blk * 512
                    h_T = hT[0]
                    for sub in range(4):
                        ti = blk * 4 + sub
                        xb_ = xr[nx % 2]
                        hb_ = hb[nx % 2]
                        S.dma("sp", xb_.t[:], self.xres[ti * 128:(ti + 1) * 128, :], reads=[self.t_xres[ti]], writes=[xb_.tok])
                        self.rmsnorm_h(xb_.t[:], xb_.tok, nw, hb_, ss[nx % 2], sd[nx % 2], junk)
                        self.transpose8(hb_.t[:], [hb_.tok], h_T.t[:, :, sub * 128:(sub + 1) * 128], [h_T.toks[sub]],
                                        evac=("dve" if sub % 2 == 0 else "act"))
                        nx += 1
                    S.dma("sp", posi.t[:], self.pos[0, t0:t0 + 512].partition_broadcast(128), writes=[posi.tok])
                    S.op("dve", lambda v: v.tensor_copy(out=ang.t[:], in_=posi.t[:]), reads=[posi.tok], writes=[ang.tok])
                    S.op("dve", lambda v: v.tensor_scalar(out=ang.t[:], in0=ang.t[:], scalar1=invf.t[:, 0:1], scalar2=None, op0=ALU.mult), reads=[ang.tok, invf.tok], writes=[ang.tok])
                    for (tab, shift) in ((stab, 0.0), (ctab, 0.5 * math.pi)):
                        S.op("dve", lambda v: v.tensor_scalar(out=kf.t[:], in0=ang.t[:], scalar1=shift, scalar2=1.0 / TWO_PI, op0=ALU.add, op1=ALU.mult), reads=[ang.tok], writes=[kf.tok])
                        S.op("dve", lambda v: v.tensor_copy(out=ki_.t[:], in_=kf.t[:]), reads=[kf.tok], writes=[ki_.tok])
                        S.op("dve", lambda v: v.tensor_copy(out=kf.t[:], in_=ki_.t[:]), reads=[ki_.tok], writes=[kf.tok])
                        S.op("dve", lambda v: v.scalar_tensor_tensor(out=kf.t[:], in0=kf.t[:], scalar=-TWO_PI, in1=ang.t[:], op0=ALU.mult, op1=ALU.add), reads=[kf.tok, ang.tok], writes=[kf.tok])
                        S.op("dve", lambda v: v.tensor_scalar(out=kf.t[:], in0=kf.t[:], scalar1=shift, scalar2=None, op0=ALU.add), reads=[kf.tok], writes=[kf.tok])
                        S.op("dve", lambda v: v.tensor_scalar(out=tab.t[:], in0=kf.t[:], scalar1=math.pi, scalar2=-TWO_PI, op0=ALU.is_gt, op1=ALU.mult), reads=[kf.tok], writes=[tab.tok])
                        S.op("dve", lambda v: v.tensor_tensor(out=kf.t[:], in0=kf.t[:], in1=tab.t[:], op=ALU.add), reads=[kf.tok, tab.tok], writes=[kf.tok])
                        S.op("dve", lambda v: v.tensor_scalar(out=tab.t[:], in0=kf.t[:], scalar1=-math.pi, scalar2=TWO_PI, op0=ALU.is_lt, op1=ALU.mult), reads=[kf.tok], writes=[tab.tok])
                        S.op("dve", lambda v: v.tensor_tensor(out=kf.t[:], in0=kf.t[:], in1=tab.t[:], op=ALU.add), reads=[kf.tok, tab.tok], writes=[kf.tok])
                        S.op("act", lambda a: a.activation(out=tab.t[:], in_=kf.t[:], func=AF.Sin), reads=[kf.tok], writes=[tab.tok])
                    S.op("dve", lambda v: v.tensor_scalar(out=stab.t[:], in0=stab.t[:], scalar1=invf.t[:, 1:2], scalar2=None, op0=ALU.mult), reads=[stab.tok, invf.tok], writes=[stab.tok])
                    for c in range(12):
                        b = self.psbank()
                        for kc in range(8):
                            S.op("pe", lambda p: p.matmul(self.PS(b), lhsT=win.t[:, kc, c * 128:(c + 1) * 128], rhs=h_T.t[:, kc, :], start=(kc == 0), stop=(kc == 7)),
                                 reads=[win.tok] + h_T.toks, writes=[self.pst[b]])
                        r_ = raw[nraw % 2]
                        nraw += 1
                        S.op("act", lambda a: a.copy(out=r_.t[:, 3:515], in_=self.PS(b)), reads=[self.pst[b]], writes=[r_.tok])
                        S.op("pool", lambda g: g.tensor_copy(out=r_.t[:, 0:3], in_=hist.t[:, c, :]), reads=[hist.tok, r_.tok], writes=[r_.tok])
                        S.op("pool", lambda g: g.tensor_copy(out=hist.t[:, c, :], in_=r_.t[:, 512:515]), reads=[r_.tok, hist.tok], writes=[hist.tok])
                        c_ = cv[ncv % 2]
                        s_ = sl[ncv % 2]
                        ncv += 1
                        S.op("dve", lambda v: v.tensor_scalar(out=c_.t[:], in0=r_.t[:, 0:512], scalar1=cw.t[:, c, 0:1], scalar2=None, op0=ALU.mult), reads=[r_.tok, cw.tok], writes=[c_.tok])
                        for i in (1, 2, 3):
                            S.op("dve", lambda v: v.scalar_tensor_tensor(out=c_.t[:], in0=r_.t[:, i:i + 512], scalar=cw.t[:, c, i:i + 1], in1=c_.t[:], op0=ALU.mult, op1=ALU.add),
                                 reads=[r_.tok, cw.tok, c_.tok], writes=[c_.tok])
                        S.op("act", lambda a: a.activation(out=s_.t[:], in_=c_.t[:], func=AF.Silu), reads=[c_.tok], writes=[s_.tok])
                        o_ = qo[nq % 2]
                        nq += 1
                        hh = c % 4
                        if c < 8:
                            q2 = sq[ncv % 2]
                            S.op("pool", lambda g: g.tensor_tensor(out=q2.t[:], in0=s_.t[:], in1=s_.t[:], op=ALU.mult), reads=[s_.tok], writes=[q2.tok])
                            b2 = self.psbank()
                            S.op("pe", lambda p: p.matmul(self.PS(b2), lhsT=onesb.t[:], rhs=q2.t[:], start=True, stop=True), reads=[onesb.tok, q2.tok], writes=[self.pst[b2]])
                            n_ = rn[0]
                            S.op("act", lambda a: a.activation(out=n_.t[:], in_=self.PS(b2), func=AF.Sqrt, bias=self.epsb.t[:]), reads=[self.pst[b2], self.epsb.tok], writes=[n_.tok])
                            S.op("dve", lambda v: v.reciprocal(out=n_.t[:], in_=n_.t[:]), reads=[n_.tok], writes=[n_.tok])
                            sc = (128 ** -0.5) if c < 4 else 1.0
                            S.op("dve", lambda v: v.scalar_tensor_tensor(out=o_.t[:], in0=s_.t[:], scalar=sc, in1=n_.t[:], op0=ALU.mult, op1=ALU.mult), reads=[s_.tok, n_.tok], writes=[o_.tok])
                            dst = (self.gq if c < 4 else self.gk)[hh][:, t0:t0 + 512]
                        else:
                            S.op("pool", lambda g: g.tensor_copy(out=o_.t[:], in_=s_.t[:]), reads=[s_.tok], writes=[o_.tok])
                            dst = self.gv[hh][:, t0:t0 + 512]
                        S.dma("pool", dst, o_.t[:], reads=[o_.tok], writes=[self.t_g], nowaw=True)
                    for (dstT, c0) in ((bT, 2048), (aT_, 2052)):
                        b = self.psbank()
                        for kc in range(8):
                            S.op("pe", lambda p: p.matmul(self.PS(b), lhsT=win.t[:, kc, c0:c0 + 128], rhs=h_T.t[:, kc, :], start=(kc == 0), stop=(kc == 7)),
                                 reads=[win.tok] + h_T.toks, writes=[self.pst[b]])
                        S.op("dve", lambda v: v.tensor_copy(out=dstT.t[:, t0:t0 + 512], in_=self.ps[0:4, b, :]), reads=[self.pst[b]], writes=[dstT.tok])
                    bq = self.psbank()
                    bk = self.psbank()
                    for which in range(2):
                        bstat = bq if which == 0 else bk
                        base = WQB if which == 0 else WKB
                        for h in range(4):
                            b1 = self.psbank()
                            while b1 in (bq, bk):
                                b1 = self.psbank()
                            b2 = self.psbank()
                            while b2 in (bq, bk):
                                b2 = self.psbank()
                            c1 = base + h * 128
                            c2 = WSW + which * 512 + h * 128
                            for (bb, cc) in ((b1, c1), (b2, c2)):
                                for kc in range(8):
                                    S.op("pe", lambda p: p.matmul(self.PS(bb), lhsT=win.t[:, kc, cc:cc + 128], rhs=h_T.t[:, kc, :], start=(kc == 0), stop=(kc == 7)),
                                         reads=[win.tok] + h_T.toks, writes=[self.pst[bb]])
                            a_ = t1[0]
                            b_ = t2[0]
                            o_ = qo[nq % 2]
                            s_ = sq[nq % 2]
                            nq += 1
                            sc = 0.125 if which == 0 else 1.0
                            S.op("dve", lambda v: v.scalar_tensor_tensor(out=a_.t[:], in0=self.PS(b1), scalar=sc, in1=ctab.t[:], op0=ALU.mult, op1=ALU.mult), reads=[self.pst[b1], ctab.tok], writes=[a_.tok])
                            S.op("dve", lambda v: v.scalar_tensor_tensor(out=b_.t[:], in0=self.PS(b2), scalar=sc, in1=stab.t[:], op0=ALU.mult, op1=ALU.mult), reads=[self.pst[b2], stab.tok], writes=[b_.tok])
                            S.op("pool", lambda g: g.tensor_tensor(out=o_.t[:], in0=a_.t[:], in1=b_.t[:], op=ALU.add), reads=[a_.tok, b_.tok], writes=[o_.tok])
                            for m in range(2):
                                mh = 2 * h + m
                                dst = (self.qTd if which == 0 else self.kTd)[mh][0:64, t0:t0 + 512]
                                S.dma("pool", dst, o_.t[m * 64:(m + 1) * 64, :], reads=[o_.tok], writes=[(self.t_qT if which == 0 else self.t_kT)[mh]])
                            S.op("dve", lambda v: v.tensor_tensor(out=s_.t[:], in0=o_.t[:], in1=o_.t[:], op=ALU.mult), reads=[o_.tok], writes=[s_.tok])
                            S.op("pe", lambda p: p.matmul(self.PS(bstat), lhsT=self.esel2b.t[:, h, :], rhs=s_.t[:], start=(h == 0), stop=(h == 3)),
                                 reads=[self.esel2b.tok, s_.tok], writes=[self.pst[bstat]])
                        dstat = (qsq if which == 0 else ksq)
                        S.op("dve", lambda v: v.tensor_copy(out=dstat.t[:, t0:t0 + 512], in_=self.ps[0:8, bstat, :]), reads=[self.pst[bstat]], writes=[dstat.tok])
                    for sub in range(4):
                        ti = blk * 4 + sub
                        v_ = vt[nv % 2]
                        g_ = gt[nv % 2]
                        nv += 1
                        b = self.psbank()
                        for kc in range(8):
                            S.op("pe", lambda p: p.matmul(self.PS(b), lhsT=h_T.t[:, kc, sub * 128:(sub + 1) * 128], rhs=win.t[:, kc, WVB:WVB + 512], start=(kc == 0), stop=(kc == 7)),
                                 reads=[win.tok, h_T.toks[sub]], writes=[self.pst[b]])
                        S.op("dve", lambda v: v.tensor_copy(out=v_.t[:, 0:4, 0:128], in_=self.PS(b).rearrange("p (h c) -> p h c", c=128)), reads=[self.pst[b]], writes=[v_.tok])
                        S.dma("pool", self.vaug[ti * 128:(ti + 1) * 128, :], v_.t[:].rearrange("p h c -> p (h c)"), reads=[v_.tok], writes=[self.t_vaug], nowaw=True)
                        b = self.psbank()
                        for kc in range(8):
                            S.op("pe", lambda p: p.matmul(self.PS(b), lhsT=h_T.t[:, kc, sub * 128:(sub + 1) * 128], rhs=win.t[:, kc, 1536:2048], start=(kc == 0), stop=(kc == 7)),
                                 reads=[win.tok, h_T.toks[sub]], writes=[self.pst[b]])
                        S.op("act", lambda a: a.activation(out=g_.t[:, 0:512], in_=self.PS(b), func=AF.Silu), reads=[self.pst[b]], writes=[g_.tok])
                        S.dma("pool", self.gbuf[ti * 128:(ti + 1) * 128, :], g_.t[:], reads=[g_.tok], writes=[self.t_gbuf[ti]])
            if STOP == "E1":
                self.stopped = True
            S.barrier()
            with ExitStack() as es:
                if self.stopped:
                    return
                km = self.sb(es, "km", [8, 1], F32)
                ya = self.sb(es, "ya", [8, T], F32)
                b1 = self.sb(es, "b1", [8, T], BF16)
                b2 = self.sb(es, "b2", [8, T], BF16)
                onesb = self.sb(es, "onesb2", [8, T], BF16)
                S.op("pool", lambda g: g.memset(onesb.t[:], 1.0), writes=[onesb.tok])
                S.op("dve", lambda v: v.tensor_reduce(out=km.t[:], in_=ksq.t[:], axis=AX.X, op=ALU.max), reads=[ksq.tok], writes=[km.tok])
                S.op("dve", lambda v: v.tensor_scalar(out=ya.t[:], in0=qsq.t[:], scalar1=64.0, scalar2=km.t[:], op0=ALU.mult, op1=ALU.add), reads=[qsq.tok, km.tok], writes=[ya.tok])
                S.op("dve", lambda v: v.tensor_scalar(out=ya.t[:], in0=ya.t[:], scalar1=-0.5 * 0.125, scalar2=None, op0=ALU.mult), reads=[ya.tok], writes=[ya.tok])
                S.op("dve", lambda v: v.tensor_copy(out=b1.t[:], in_=ya.t[:]), reads=[ya.tok], writes=[b1.tok])
                S.op("dve", lambda v: v.tensor_tensor(out=b2.t[:], in0=ya.t[:], in1=b1.t[:], op=ALU.subtract), reads=[ya.tok, b1.tok], writes=[b2.tok])
                S.dma("sp", self.qTd[:, 64, :], b1.t[:], reads=[b1.tok], writes=self.t_qT)
                S.dma("sp", self.qTd[:, 65, :], b2.t[:], reads=[b2.tok], writes=self.t_qT)
                S.dma("sp", self.kTd[:, 64, :], onesb.t[:], reads=[onesb.tok], writes=self.t_kT)
                S.dma("sp", self.kTd[:, 65, :], onesb.t[:], reads=[onesb.tok], writes=self.t_kT)
                S.barrier()
            esQ.close()
            if STOP == "E2a":
                self.stopped = True
                return
            self.gdn(l, bT, aT_)
        if STOP == "gdnB":
            self.stopped = True
        if self.stopped:
            return
        S.barrier()
        with ExitStack() as es:
            self.attention(es, nheads=8, kdim=64, bias_rows=2, vhead=lambda mh: mh // 2, dest=self.obuf2, dtoks=self.t_obuf2)
        tap = os.environ.get("K_TAP", "")
        if tap:
            S.barrier()
            src = self.obuf if tap == "o" else self.obuf2
            for i in range(0, T, 512):
                S.dma("sp", self.y[i:i + 512, :], src[i:i + 512, :], writes=[self.t_y])
            self.stopped = True

    def gdn(self, l, bT, aT_):
        nc, S, T, NT, NB = self.nc, self.S, self.T, self.NT, self.NB
        j = l // 2
        S.barrier()
        with ExitStack() as es:
            al = self.sb(es, "al", [4, 1], F32)
            dtb = self.sb(es, "dtb", [4, 1], F32)
            S.dma("sp", al.t[:], self.a_log[j].rearrange("(h o) -> h o", o=1), writes=[al.tok])
            S.dma("sp", dtb.t[:], self.dt_bias[j].rearrange("(h o) -> h o", o=1), writes=[dtb.tok])
            cols = self.sb(es, "gcols", [128, NT, 16], F32)
            with ExitStack() as esA:
                cm = self.sb(esA, "cm", [4, T], F32)
                S.dma("sp", cm.t[:], self.c_cmask, writes=[cm.tok])
                xa = aT_
                beta = bT
                ya = self.sb(esA, "gya", [4, T], F32)
                gc = self.sb(esA, "ggc", [4, T], F32)
                bec = self.sb(esA, "gbec", [4, T], F32)
                ekd = self.sb(esA, "gekd", [4, T], F32)
                egc = self.sb(esA, "gegc", [4, T], F32)
                S.op("act", lambda a: a.activation(out=beta.t[:], in_=bT.t[:], func=AF.Sigmoid), reads=[bT.tok], writes=[beta.tok])
                S.op("act", lambda a: a.activation(out=al.t[:], in_=al.t[:], func=AF.Exp), reads=[al.tok], writes=[al.tok])
                S.op("dve", lambda v: v.tensor_scalar(out=al.t[:], in0=al.t[:], scalar1=-1.0, scalar2=None, op0=ALU.mult), reads=[al.tok], writes=[al.tok])
                S.op("dve", lambda v: v.tensor_scalar(out=xa.t[:], in0=aT_.t[:], scalar1=dtb.t[:], scalar2=None, op0=ALU.add), reads=[aT_.tok, dtb.tok], writes=[xa.tok])
                S.op("dve", lambda v: v.scalar_tensor_tensor(out=ya.t[:], in0=xa.t[:], scalar=-1.0, in1=xa.t[:], op0=ALU.mult, op1=ALU.max), reads=[xa.tok], writes=[ya.tok])
                S.op("act", lambda a: a.activation(out=ya.t[:], in_=ya.t[:], func=AF.Exp, scale=-1.0), reads=[ya.tok], writes=[ya.tok])
                S.op("act", lambda a: a.activation(out=ya.t[:], in_=ya.t[:], func=AF.Ln, bias=1.0), reads=[ya.tok], writes=[ya.tok])
                S.op("dve", lambda v: v.scalar_tensor_tensor(out=xa.t[:], in0=xa.t[:], scalar=0.0, in1=ya.t[:], op0=ALU.max, op1=ALU.add), reads=[xa.tok, ya.tok], writes=[xa.tok])
                S.op("dve", lambda v: v.tensor_scalar(out=xa.t[:], in0=xa.t[:], scalar1=al.t[:], scalar2=None, op0=ALU.mult), reads=[xa.tok, al.tok], writes=[xa.tok])
                S.op("dve", lambda v: v.tensor_tensor_scan(out=gc.t[:], data0=cm.t[:], data1=xa.t[:], initial=0.0, op0=ALU.mult, op1=ALU.add), reads=[cm.tok, xa.tok], writes=[gc.tok])
                S.op("act", lambda a: a.activation(out=egc.t[:], in_=gc.t[:], func=AF.Exp), reads=[gc.tok], writes=[egc.tok])
                S.op("dve", lambda v: v.tensor_tensor(out=bec.t[:], in0=beta.t[:], in1=egc.t[:], op=ALU.mult), reads=[beta.tok, egc.tok], writes=[bec.tok])
                gcv = gc.t[:].rearrange("p (n c) -> p n c", c=128)
                S.op("dve", lambda v: v.tensor_tensor(out=ekd.t[:].rearrange("p (n c) -> p n c", c=128), in0=gcv[:, :, 127:128].to_broadcast([4, NT, 128]), in1=gcv, op=ALU.subtract),
                     reads=[gc.tok], writes=[ekd.tok])
                S.op("act", lambda a: a.activation(out=ekd.t[:], in_=ekd.t[:], func=AF.Exp), reads=[ekd.tok], writes=[ekd.tok])
                S.dma("sp", self.gcd, gc.t[:], reads=[gc.tok], writes=[self.t_gcd])
                S.dma("sp", self.egcd, egc.t[:], reads=[egc.tok], writes=[self.t_gcd])
                for ti in range(NT):
                    b = self.psbank()
                    for qi, src in enumerate((gc, beta, bec, ekd)):
                        S.op("pe", lambda p: p.transpose(out=self.ps[:, b, qi * 4:(qi + 1) * 4], in_=src.t[:, ti * 128:(ti + 1) * 128], identity=self.identf.t[0:4, 0:4]),
                             reads=[src.tok, self.identf.tok], writes=[self.pst[b]])
                    S.op("dve", lambda v: v.tensor_copy(out=cols.t[:, ti, :], in_=self.ps[:, b, 0:16]), reads=[self.pst[b]], writes=[cols.tok])
                S.barrier()
            if STOP == "gdnA":
                self.stopped = True
                return
            negup = self.sb(es, "gnegup", [128, 128], F32)
            poslow = self.sb(es, "gposlow", [128, 128], F32)
            S.dma("sp", negup.t[:], self.c_negup, writes=[negup.tok])
            S.dma("sp", poslow.t[:], self.c_poslow, writes=[poslow.tok])
            kT = self.sb(es, "gkT", [128, T], BF16)
            qT = self.sb(es, "gqT", [128, T], BF16)
            vT = self.sb(es, "gvT", [128, T], BF16)
            Rg = self.sb(es, "gRg", [128, T], F32)
            Re = self.sb(es, "gRe", [128, T], F32)
            qd = self.sb(es, "gqd", [128, T], BF16)
            QK = self.sb(es, "gQK", [128, NT, 128], BF16)
            U = self.sb(es, "gU", [128, NT, 128], F32)
            WT = self.sb(es, "gWT", [128, T], BF16)
            KD = self.sb(es, "gKD", [128, NT, 128], BF16)
            egl = self.sb(es, "gegl", [128, NT], F32)
            G = 4
            kbe = self.ring(es, "gkbe", [128, 128], BF16, G)
            vb = self.ring(es, "gvb", [128, 128], BF16, G)
            e1 = self.ring(es, "ge1", [128, 128], F32, G)
            e2 = self.ring(es, "ge2", [128, 128], F32, G)
            Pm = [self.ring(es, f"gP{i}", [128, 128], BF16, G) for i in range(2)]
            PTm = [self.ring(es, f"gPT{i}", [128, 128], BF16, G) for i in range(2)]
            TTm = [self.ring(es, f"gTT{i}", [128, 128], F32, G) for i in range(2)]
            TTs = [self.ring(es, f"gTTs{i}", [128, 128], BF16, G) for i in range(2)]
            Pl = [self.ring(es, f"gPl{i}", [128, 128], BF16, G) for i in range(2)]
            PTl_ = [self.ring(es, f"gPTl{i}", [128, 128], BF16, G) for i in range(2)]
            TSl_ = [self.ring(es, f"gTSl{i}", [128, 128], BF16, G) for i in range(2)]
            xs_ = self.ring(es, "gxs", [128, 128], F32, G)
            xs2_ = self.ring(es, "gxs2", [128, 128], F32, G)

            def split(dh, dl, src):
                S.op("pool", lambda g: g.tensor_copy(out=dh.t[:], in_=src.t[:]), reads=[src.tok], writes=[dh.tok])
                S.op("dve", lambda v: v.tensor_tensor(out=dl.t[:], in0=src.t[:], in1=dh.t[:], op=ALU.subtract), reads=[src.tok, dh.tok], writes=[dl.tok])

            def mm3(bank, *pairs):
                n = len(pairs)
                for i_, (a_, b_) in enumerate(pairs):
                    S.op("pe", lambda p: p.matmul(self.ps[:, bank, 0:128], lhsT=a_.t[:], rhs=b_.t[:], start=(i_ == 0), stop=(i_ == n - 1)),
                         reads=[a_.tok, b_.tok], writes=[self.pst[bank]])
            TTb = self.ring(es, "gTTb", [128, 128], BF16, G)
            Sst = self.sb(es, "gS", [128, 128], F32)
            Sb = self.sb(es, "gSb", [128, 128], BF16)
            Sl = self.sb(es, "gSl", [128, 128], BF16)
            vn = self.ring(es, "gvn", [128, 128], BF16, 2)
            og = self.ring(es, "gog", [128, 4, 128], F32, 2)
            osrc = self.obuf.rearrange("(n p) c -> p n c", p=128)
            for h in range(int(os.environ.get("K_H0", "0")), int(os.environ.get("K_H1", "4"))):
                S.barrier()
                S.dma("sp", kT.t[:], self.gk[h], reads=[self.t_g], writes=[kT.tok])
                S.dma("sp", qT.t[:], self.gq[h], reads=[self.t_g], writes=[qT.tok])
                S.dma("sp", vT.t[:], self.gv[h], reads=[self.t_g], writes=[vT.tok])
                for c0 in range(0, T, 1024):
                    c1 = min(T, c0 + 1024)
                    S.dma("sp", Rg.t[:, c0:c1], self.gcd[h][c0:c1].partition_broadcast(128), reads=[self.t_gcd], writes=[Rg.tok])
                    S.dma("sp", Re.t[:, c0:c1], self.egcd[h][c0:c1].partition_broadcast(128), reads=[self.t_gcd], writes=[Re.tok])
                S.op("pool", lambda g: g.tensor_tensor(out=qd.t[:], in0=qT.t[:], in1=Re.t[:], op=ALU.mult), reads=[qT.tok, Re.tok], writes=[qd.tok])
                S.op("act", lambda a: a.activation(out=egl.t[:], in_=Rg.t[:].rearrange("p (n c) -> p n c", c=128)[:, :, 127], func=AF.Exp), reads=[Rg.tok], writes=[egl.tok])
                for g0 in range(0, NT, G):
                    tiles = list(range(g0, min(NT, g0 + G)))
                    st = {}
                    for ti in tiles:
                        s = ti % G
                        tsl = slice(ti * 128, (ti + 1) * 128)
                        cgc = cols.t[:, ti, 0 + h:1 + h]
                        cbeta = cols.t[:, ti, 4 + h:5 + h]
                        cbec = cols.t[:, ti, 8 + h:9 + h]
                        cekd = cols.t[:, ti, 12 + h:13 + h]
                        b = self.psbank()
                        pv = self.PSB(b)
                        S.op("pe", lambda p: p.transpose(out=pv[:, 0, :], in_=kT.t[:, tsl], identity=self.identb.t[:]), reads=[kT.tok, self.identb.tok], writes=[self.pst[b]])
                        S.op("pe", lambda p: p.transpose(out=pv[:, 1, :], in_=vT.t[:, tsl], identity=self.identb.t[:]), reads=[vT.tok, self.identb.tok], writes=[self.pst[b]])
                        S.op("dve", lambda v: v.tensor_scalar(out=kbe[s].t[:], in0=pv[:, 0, :], scalar1=cbec, scalar2=None, op0=ALU.mult), reads=[self.pst[b], cols.tok], writes=[kbe[s].tok])
                        S.op("dve", lambda v: v.tensor_scalar(out=KD.t[:, ti, :], in0=pv[:, 0, :], scalar1=cekd, scalar2=None, op0=ALU.mult), reads=[self.pst[b], cols.tok], writes=[KD.tok])
                        S.op("dve", lambda v: v.tensor_scalar(out=vb[s].t[:], in0=pv[:, 1, :], scalar1=cbeta, scalar2=None, op0=ALU.mult), reads=[self.pst[b], cols.tok], writes=[vb[s].tok])
                        S.op("dve", lambda v: v.scalar_tensor_tensor(out=e1[s].t[:], in0=Rg.t[:, tsl], scalar=cgc, in1=negup.t[:], op0=ALU.subtract, op1=ALU.add), reads=[Rg.tok, cols.tok, negup.tok], writes=[e1[s].tok])
                        S.op("act", lambda a: a.activation(out=e1[s].t[:], in_=e1[s].t[:], func=AF.Exp), reads=[e1[s].tok], writes=[e1[s].tok])
                        S.op("dve", lambda v: v.scalar_tensor_tensor(out=e2[s].t[:], in0=Rg.t[:, tsl], scalar=cgc, in1=poslow.t[:], op0=ALU.subtract, op1=ALU.add), reads=[Rg.tok, cols.tok, poslow.tok], writes=[e2[s].tok])
                        S.op("act", lambda a: a.activation(out=e2[s].t[:], in_=e2[s].t[:], func=AF.Exp, scale=-1.0), reads=[e2[s].tok], writes=[e2[s].tok])
                        bkk = self.psbank()
                        S.op("pe", lambda p: p.matmul(self.ps[:, bkk, 0:128], lhsT=kT.t[:, tsl], rhs=kT.t[:, tsl], start=True, stop=True), reads=[kT.tok], writes=[self.pst[bkk]])
                        S.op("pe", lambda p: p.matmul(self.ps[:, bkk, 128:256], lhsT=kT.t[:, tsl], rhs=qT.t[:, tsl], start=False, stop=True, skip_group_check=True), reads=[kT.tok, qT.tok], writes=[self.pst[bkk]])
                        Lf = e2[s]
                        S.op("dve", lambda v: v.scalar_tensor_tensor(out=Lf.t[:], in0=self.ps[:, bkk, 0:128], scalar=cbeta, in1=e2[s].t[:], op0=ALU.mult, op1=ALU.mult), reads=[self.pst[bkk], cols.tok, e2[s].tok], writes=[Lf.tok])
                        S.op("dve", lambda v: v.tensor_tensor(out=QK.t[:, ti, :], in0=self.ps[:, bkk, 128:256], in1=e1[s].t[:], op=ALU.mult), reads=[self.pst[bkk], e1[s].tok], writes=[QK.tok])
                        Ph, Pl_, PTh, PTl, TT0, TSh, TSl = Pm[0][s], Pl[0][s], PTm[0][s], PTl_[0][s], TTm[0][s], TTs[0][s], TSl_[0][s]
                        split(Ph, Pl_, Lf)
                        bt = self.psbank()
                        pvt = self.PSB(bt)
                        S.op("pe", lambda p: p.transpose(out=pvt[:, 0, :], in_=Ph.t[:], identity=self.identb.t[:]), reads=[Ph.tok, self.identb.tok], writes=[self.pst[bt]])
                        S.op("pe", lambda p: p.transpose(out=pvt[:, 1, :], in_=Pl_.t[:], identity=self.identb.t[:]), reads=[Pl_.tok, self.identb.tok], writes=[self.pst[bt]])
                        S.op("dve", lambda v: v.tensor_copy(out=PTh.t[:], in_=pvt[:, 0, :]), reads=[self.pst[bt]], writes=[PTh.tok])
                        S.op("dve", lambda v: v.tensor_copy(out=PTl.t[:], in_=pvt[:, 1, :]), reads=[self.pst[bt]], writes=[PTl.tok])
                        S.op("dve", lambda v: v.scalar_tensor_tensor(out=TT0.t[:], in0=PTh.t[:], scalar=-1.0, in1=self.identf.t[:], op0=ALU.mult, op1=ALU.add), reads=[PTh.tok, self.identf.tok], writes=[TT0.tok])
                        S.op("pool", lambda g: g.tensor_tensor(out=TT0.t[:], in0=TT0.t[:], in1=PTl.t[:], op=ALU.subtract), reads=[TT0.tok, PTl.tok], writes=[TT0.tok])
                        split(TSh, TSl, TT0)
                        st[ti] = 0
                    if STOP == "gB1":
                        self.stopped = True
                        return
                    NST = 6
                    for stage in range(NST):
                        lastst = (stage == NST - 1)
                        for ti in tiles:
                            s = ti % G
                            cur = st[ti]
                            nxt = 1 - cur
                            Ph, Pl_, PTh, PTl, TTc, TSh, TSl = Pm[cur][s], Pl[cur][s], PTm[cur][s], PTl_[cur][s], TTm[cur][s], TTs[cur][s], TSl_[cur][s]
                            Pnh, Pnl, PTnh, PTnl, TTn, TSnh, TSnl = Pm[nxt][s], Pl[nxt][s], PTm[nxt][s], PTl_[nxt][s], TTm[nxt][s], TTs[nxt][s], TSl_[nxt][s]
                            X = xs_[s]
                            b = self.psbank()
                            mm3(b, (PTh, Ph), (PTh, Pl_), (PTl, Ph))
                            S.op("dve", lambda v: v.tensor_copy(out=X.t[:], in_=self.ps[:, b, 0:128]), reads=[self.pst[b]], writes=[X.tok])
                            split(Pnh, Pnl, X)
                            if not lastst:
                                b3 = self.psbank()
                                mm3(b3, (Ph, PTh), (Ph, PTl), (Pl_, PTh))
                                X2 = xs2_[s]
                                S.op("dve", lambda v: v.tensor_copy(out=X2.t[:], in_=self.ps[:, b3, 0:128]), reads=[self.pst[b3]], writes=[X2.tok])
                                split(PTnh, PTnl, X2)
                            b2 = self.psbank()
                            mm3(b2, (Pnh, TSh), (Pnh, TSl), (Pnl, TSh))
                            S.op("dve", lambda v: v.tensor_tensor(out=TTn.t[:], in0=self.ps[:, b2, 0:128], in1=TTc.t[:], op=ALU.add), reads=[self.pst[b2], TTc.tok], writes=[TTn.tok])
                            split(TSnh, TSnl, TTn)
                            st[ti] = nxt
                    if STOP == "gB2":
                        self.stopped = True
                        return
                    for ti in tiles:
                        s = ti % G
                        tsl = slice(ti * 128, (ti + 1) * 128)
                        TSh, TSl = TTs[st[ti]][s], TSl_[st[ti]][s]
                        b = self.psbank()
                        S.op("pe", lambda p: p.matmul(self.ps[:, b, 0:128], lhsT=TSh.t[:], rhs=vb[s].t[:], start=True, stop=False), reads=[TSh.tok, vb[s].tok], writes=[self.pst[b]])
                        S.op("pe", lambda p: p.matmul(self.ps[:, b, 0:128], lhsT=TSl.t[:], rhs=vb[s].t[:], start=False, stop=True), reads=[TSl.tok, vb[s].tok], writes=[self.pst[b]])
                        bw = self.psbank()
                        S.op("pe", lambda p: p.matmul(self.ps[:, bw, 0:128], lhsT=kbe[s].t[:], rhs=TSh.t[:], start=True, stop=False), reads=[TSh.tok, kbe[s].tok], writes=[self.pst[bw]])
                        S.op("pe", lambda p: p.matmul(self.ps[:, bw, 0:128], lhsT=kbe[s].t[:], rhs=TSl.t[:], start=False, stop=True), reads=[TSl.tok, kbe[s].tok], writes=[self.pst[bw]])
                        S.op("dve", lambda v: v.tensor_copy(out=U.t[:, ti, :], in_=self.ps[:, b, 0:128]), reads=[self.pst[b]], writes=[U.tok])
                        S.op("dve", lambda v: v.tensor_copy(out=WT.t[:, tsl], in_=self.ps[:, bw, 0:128]), reads=[self.pst[bw]], writes=[WT.tok])
                if STOP == "gB3":
                    self.stopped = True
                    return
                S.op("dve", lambda v: v.memset(Sst.t[:], 0.0), writes=[Sst.tok])
                S.op("pool", lambda g: g.memset(Sb.t[:], 0.0), writes=[Sb.tok])
                S.op("pool", lambda g: g.memset(Sl.t[:], 0.0), writes=[Sl.tok])
                for ti in range(NT):
                    tsl = slice(ti * 128, (ti + 1) * 128)
                    ba = self.psbank()
                    bo = self.psbank()
                    S.op("pe", lambda p: p.matmul(self.ps[:, ba, 0:128], lhsT=WT.t[:, tsl], rhs=Sb.t[:], start=True, stop=False), reads=[WT.tok, Sb.tok], writes=[self.pst[ba]])
                    S.op("pe", lambda p: p.matmul(self.ps[:, ba, 0:128], lhsT=WT.t[:, tsl], rhs=Sl.t[:], start=False, stop=True), reads=[WT.tok, Sl.tok], writes=[self.pst[ba]])
                    S.op("pe", lambda p: p.matmul(self.ps[:, bo, 0:128], lhsT=qd.t[:, tsl], rhs=Sb.t[:], start=True, stop=False), reads=[qd.tok, Sb.tok], writes=[self.pst[bo]])
                    S.op("pe", lambda p: p.matmul(self.ps[:, bo, 0:128], lhsT=qd.t[:, tsl], rhs=Sl.t[:], start=False, stop=False), reads=[qd.tok, Sl.tok], writes=[self.pst[bo]])
                    v_ = vn[ti % 2]
                    S.op("dve", lambda v: v.tensor_tensor(out=v_.t[:], in0=U.t[:, ti, :], in1=self.ps[:, ba, 0:128], op=ALU.subtract), reads=[U.tok, self.pst[ba]], writes=[v_.tok])
                    S.op("pe", lambda p: p.matmul(self.ps[:, bo, 0:128], lhsT=QK.t[:, ti, :], rhs=v_.t[:], start=False, stop=True), reads=[QK.tok, v_.tok], writes=[self.pst[bo]])
                    bd = self.psbank()
                    S.op("pe", lambda p: p.matmul(self.ps[:, bd, 0:128], lhsT=KD.t[:, ti, :], rhs=v_.t[:], start=True, stop=True), reads=[KD.tok, v_.tok], writes=[self.pst[bd]])
                    S.op("dve", lambda v: v.tensor_scalar(out=Sst.t[:], in0=Sst.t[:], scalar1=egl.t[:, ti:ti + 1], scalar2=None, op0=ALU.mult), reads=[Sst.tok, egl.tok], writes=[Sst.tok])
                    S.op("dve", lambda v: v.tensor_tensor(out=Sst.t[:], in0=self.ps[:, bd, 0:128], in1=Sst.t[:], op=ALU.add), reads=[Sst.tok, self.pst[bd]], writes=[Sst.tok])
                    S.op("dve", lambda v: v.tensor_copy(out=Sb.t[:], in_=Sst.t[:]), reads=[Sst.tok], writes=[Sb.tok])
                    S.op("dve", lambda v: v.tensor_tensor(out=Sl.t[:], in0=Sst.t[:], in1=Sb.t[:], op=ALU.subtract), reads=[Sst.tok, Sb.tok], writes=[Sl.tok])
                    o_ = og[(ti // 4) % 2]
                    S.op("dve", lambda v: v.tensor_copy(out=o_.t[:, ti % 4, :], in_=self.ps[:, bo, 0:128]), reads=[self.pst[bo]], writes=[o_.tok])
                    if ti % 4 == 3 and os.environ.get("K_DBG") != "nodma":
                        S.dma("pool", osrc[:, ti - 3:ti + 1, h * 128:(h + 1) * 128], o_.t[:], reads=[o_.tok], writes=self.t_obuf[ti - 3:ti + 1])
                    if os.environ.get("K_SCAN") and ti + 1 >= int(os.environ["K_SCAN"]):
                        self.stopped = True
                        return

    def even_gate(self, l, o_, o2_, g_, m_, wn, nlam, tmp, ss8):
        S = self.S
        o2v = o2_.t[:].rearrange("p (h m c) -> p h m c", m=2, c=128)
        S.op("dve", lambda v: v.scalar_tensor_tensor(out=o_.t[:, 512:1024].rearrange("p (h c) -> p h c", c=128), in0=o2v[:, :, 1, :], scalar=nlam.t[:], in1=o2v[:, :, 0, :], op0=ALU.mult, op1=ALU.add),
             reads=[o2_.tok, nlam.tok, o_.tok], writes=[o_.tok])
        S.op("pool", lambda g: g.tensor_tensor(out=tmp.t[:], in0=o_.t[:], in1=o_.t[:], op=ALU.mult), reads=[o_.tok], writes=[tmp.tok])
        S.op("dve", lambda v: v.tensor_reduce(out=ss8.t[:], in_=tmp.t[:].rearrange("p (g c) -> p g c", c=128), axis=AX.X, op=ALU.add), reads=[tmp.tok], writes=[ss8.tok])
        S.op("act", lambda a: a.activation(out=ss8.t[:], in_=ss8.t[:], func=AF.Sqrt, scale=1.0 / 128, bias=self.epsb.t[:]), reads=[ss8.tok, self.epsb.tok], writes=[ss8.tok])
        S.op("dve", lambda v: v.reciprocal(out=ss8.t[:], in_=ss8.t[:]), reads=[ss8.tok], writes=[ss8.tok])
        S.op("dve", lambda v: v.tensor_tensor(out=tmp.t[:].rearrange("p (g c) -> p g c", c=128), in0=o_.t[:].rearrange("p (g c) -> p g c", c=128), in1=ss8.t[:].unsqueeze(2).to_broadcast([128, 8, 128]), op=ALU.mult),
             reads=[o_.tok, ss8.tok, tmp.tok], writes=[tmp.tok])
        S.op("pool", lambda g: g.tensor_tensor(out=tmp.t[:], in0=tmp.t[:], in1=wn.t[:], op=ALU.mult), reads=[tmp.tok, wn.tok], writes=[tmp.tok])
        S.op("dve", lambda v: v.tensor_tensor(out=m_.t[:], in0=tmp.t[:], in1=g_.t[:], op=ALU.mult), reads=[tmp.tok, g_.tok], writes=[m_.tok])

    def tail(self, l, last):
        nc, S, T, NT, NB = self.nc, self.S, self.T, self.NT, self.NB
        odd = (l % 2 == 1)
        S.barrier()
        with ExitStack() as es:
            wring = self.ring(es, "wr", [128, 8, 512], BF16, 5)
            nwr = [0]

            def wload(src_ap, tok):
                w_ = wring[nwr[0] % 5]
                nwr[0] += 1
                S.dma("sp", w_.t[:], src_ap, reads=[tok], writes=[w_.tok])
                return w_

            wout = self.wb[("out", l)][0].rearrange("(c p) n -> p c n", p=128)
            wup = self.wb[("up", l)][0].rearrange("(c p) n -> p c n", p=128)
            wdn = self.wb[("down", l)][0].rearrange("(c p) n -> p c n", p=128)
            wgt = self.wb[("gate", l)][0].rearrange("(c p) n -> p c n", p=128)
            wple = self.sb(es, "wple", [128, 2, D], BF16)
            S.dma("sp", wple.t[:], self.wb[("ple", l)][0].rearrange("(c p) n -> p c n", p=128), reads=[self.wtok[("ple", l)]], writes=[wple.tok])
            nwm = self.sb(es, "nwm", [128, D], F32)
            S.dma("sp", nwm.t[:], self.norm_mlp[l].partition_broadcast(128), writes=[nwm.tok])
            if last and self.final_norm:
                nwf = self.sb(es, "nwf", [128, D], F32)
                S.dma("sp", nwf.t[:], self.norm_final[0].partition_broadcast(128), writes=[nwf.tok])
            xr = self.ring(es, "txr", [128, D], F32, 8)
            orr = self.ring(es, "tor", [128, D], F32, 2)
            gr = self.ring(es, "tgr", [128, D], BF16, 2)
            mb = self.ring(es, "tmb", [128, D], BF16, 3)
            TT = self.ring(es, "tTT", [128, 8, 512], BF16, 2, n=4)
            aT = self.sb(es, "taT", [128, 32, 512], BF16)
            rr = self.ring(es, "trr", [128, 512], F32, 3)
            pTt = self.ring(es, "tpT", [128, 2, 512], BF16, 2)
            junk = self.sb(es, "tjunk", [128, D], BF16)
            ss = self.ring(es, "tss", [128, 1], F32, 2)
            sd = self.ring(es, "tsd", [128, 1], F32, 2)
            if last and self.final_norm:
                yo = self.ring(es, "tyo", [128, D], F32, 2)
            if not odd:
                jj_ = l // 2
                lam_init = 0.8 - 0.6 * math.exp(-0.3 * l)
                o2r = self.ring(es, "to2", [128, D], F32, 2)
                tmpb = self.sb(es, "ttmp", [128, D], F32)
                ss8 = self.ring(es, "tss8", [128, 8], F32, 2)
                wn = self.sb(es, "twn", [128, D], F32)
                for gi in range(4):
                    S.dma("sp", wn.t[:, gi * 128:(gi + 1) * 128], self.gdn_norm[jj_].partition_broadcast(128), writes=[wn.tok])
                    S.dma("sp", wn.t[:, 512 + gi * 128:512 + (gi + 1) * 128], self.diff_norm[jj_].partition_broadcast(128), writes=[wn.tok])
                S.op("dve", lambda v: v.tensor_scalar(out=wn.t[:, 512:1024], in0=wn.t[:, 512:1024], scalar1=1.0 - lam_init, scalar2=None, op0=ALU.mult), reads=[wn.tok], writes=[wn.tok])
                lt = [self.sb(es, f"tlam{i}", [128, 64], F32) for i in range(4)]
                for i in range(4):
                    S.dma("sp", lt[i].t[:], self.lam[i][jj_].partition_broadcast(128), writes=[lt[i].tok])
                ls = self.sb(es, "tls", [128, 2], F32)
                nlam = self.sb(es, "tnlam", [128, 1], F32)
                for i in range(2):
                    S.op("dve", lambda v: v.tensor_tensor(out=lt[2 * i].t[:], in0=lt[2 * i].t[:], in1=lt[2 * i + 1].t[:], op=ALU.mult), reads=[lt[2 * i].tok, lt[2 * i + 1].tok], writes=[lt[2 * i].tok])
                    S.op("dve", lambda v: v.tensor_reduce(out=ls.t[:, i:i + 1], in_=lt[2 * i].t[:], axis=AX.X, op=ALU.add), reads=[lt[2 * i].tok, ls.tok], writes=[ls.tok])
                S.op("act", lambda a: a.activation(out=ls.t[:], in_=ls.t[:], func=AF.Exp), reads=[ls.tok], writes=[ls.tok])
                S.op("dve", lambda v: v.tensor_tensor(out=nlam.t[:], in0=ls.t[:, 1:2], in1=ls.t[:, 0:1], op=ALU.subtract), reads=[ls.tok], writes=[nlam.tok])
                S.op("dve", lambda v: v.tensor_scalar(out=nlam.t[:], in0=nlam.t[:], scalar1=-lam_init, scalar2=None, op0=ALU.add), reads=[nlam.tok], writes=[nlam.tok])
            nT = 0
            nm = 0
            nr = 0
            for blk in range(NB):
                xs = [xr[(blk % 2) * 4 + s] for s in range(4)]
                p_ = pTt[blk % 2]
                S.dma("pool", p_.t[:], self.pT[l].rearrange("(c p) t -> p c t", p=128)[:, :, blk * 512:(blk + 1) * 512], writes=[p_.tok])
                mT = TT[nT % 2]
                nT += 1
                for sub in range(4):
                    ti = blk * 4 + sub
                    x_ = xs[sub]
                    S.dma("sp", x_.t[:], self.xres[ti * 128:(ti + 1) * 128, :], reads=[self.t_xres[ti]], writes=[x_.tok])
                    o_ = orr[ti % 2]
                    g_ = gr[ti % 2]
                    m_ = mb[nm % 3]
                    nm += 1
                    S.dma("sp", o_.t[:], self.obuf[ti * 128:(ti + 1) * 128, :], reads=[self.t_obuf[ti]], writes=[o_.tok])
                    S.dma("sp", g_.t[:], self.gbuf[ti * 128:(ti + 1) * 128, :], reads=[self.t_gbuf[ti]], writes=[g_.tok])
                    if odd:
                        S.op("pool", lambda g: g.tensor_tensor(out=m_.t[:], in0=o_.t[:], in1=g_.t[:], op=ALU.mult), reads=[o_.tok, g_.tok], writes=[m_.tok])
                    else:
                        o2_ = o2r[ti % 2]
                        S.dma("sp", o2_.t[:], self.obuf2[ti * 128:(ti + 1) * 128, :], reads=[self.t_obuf2[ti]], writes=[o2_.tok])
                        self.even_gate(l, o_, o2_, g_, m_, wn, nlam, tmpb, ss8[ti % 2])
                    self.transpose8(m_.t[:], [m_.tok], mT.t[:, :, sub * 128:(sub + 1) * 128], [mT.toks[sub]], evac=("dve" if sub % 2 == 0 else "act"))
                for nh in range(2):
                    w_ = wload(wout[:, :, nh * 512:(nh + 1) * 512], self.wtok[("out", l)])
                    for sub in range(4):
                        b = self.psbank()
                        for kc in range(8):
                            S.op("pe", lambda p: p.matmul(self.PS(b), lhsT=mT.t[:, kc, sub * 128:(sub + 1) * 128], rhs=w_.t[:, kc, :], start=(kc == 0), stop=(kc == 7)),
                                 reads=[mT.toks[sub], w_.tok], writes=[self.pst[b]])
                        x_ = xs[sub]
                        S.op("dve", lambda v: v.tensor_tensor(out=x_.t[:, nh * 512:(nh + 1) * 512], in0=self.PS(b), in1=x_.t[:, nh * 512:(nh + 1) * 512], op=ALU.add),
                             reads=[self.pst[b], x_.tok], writes=[x_.tok])
                hT = TT[nT % 2]
                nT += 1
                for sub in range(4):
                    m_ = mb[nm % 3]
                    nm += 1
                    self.rmsnorm_h(xs[sub].t[:], xs[sub].tok, nwm, m_, ss[sub % 2], sd[sub % 2], junk)
                    self.transpose8(m_.t[:], [m_.tok], hT.t[:, :, sub * 128:(sub + 1) * 128], [hT.toks[sub]], evac=("dve" if sub % 2 == 0 else "act"))
                for g in range(8):
                    w_ = wload(wup[:, :, g * 512:(g + 1) * 512], self.wtok[("up", l)])
                    for jj in range(4):
                        fc = g * 4 + jj
                        b = self.psbank()
                        for kc in range(8):
                            S.op("pe", lambda p: p.matmul(self.PS(b), lhsT=w_.t[:, kc, jj * 128:(jj + 1) * 128], rhs=hT.t[:, kc, :], start=(kc == 0), stop=(kc == 7)),
                                 reads=[w_.tok] + hT.toks, writes=[self.pst[b]])
                        r_ = rr[nr % 3]
                        nr += 1
                        S.op("act", lambda a: a.activation(out=r_.t[:], in_=self.PS(b), func=AF.Relu), reads=[self.pst[b]], writes=[r_.tok])
                        S.op("dve", lambda v: v.tensor_tensor(out=aT.t[:, fc, :], in0=r_.t[:], in1=self.PS(b), op=ALU.mult),
                             reads=[r_.tok, self.pst[b]], writes=[aT.tok])
                for nh in range(2):
                    accb = [self.psbank() for _ in range(4)]
                    for fg in range(4):
                        w_ = wload(wdn[:, fg * 8:(fg + 1) * 8, nh * 512:(nh + 1) * 512], self.wtok[("down", l)])
                        for jj in range(8):
                            fc = fg * 8 + jj
                            for sub in range(4):
                                b = accb[sub]
                                S.op("pe", lambda p: p.matmul(self.PS(b), lhsT=aT.t[:, fc, sub * 128:(sub + 1) * 128], rhs=w_.t[:, jj, :], start=(fc == 0), stop=(fc == 31)),
                                     reads=[aT.tok, w_.tok], writes=[self.pst[b]])
                    for sub in range(4):
                        x_ = xs[sub]
                        b = accb[sub]
                        S.op("dve", lambda v: v.tensor_tensor(out=x_.t[:, nh * 512:(nh + 1) * 512], in0=self.PS(b), in1=x_.t[:, nh * 512:(nh + 1) * 512], op=ALU.add),
                             reads=[self.pst[b], x_.tok], writes=[x_.tok])
                xT = TT[nT % 2]
                nT += 1
                for sub in range(4):
                    m_ = mb[nm % 3]
                    nm += 1
                    S.op("pool", lambda g: g.tensor_copy(out=m_.t[:], in_=xs[sub].t[:]), reads=[xs[sub].tok], writes=[m_.tok])
                    self.transpose8(m_.t[:], [m_.tok], xT.t[:, :, sub * 128:(sub + 1) * 128], [xT.toks[sub]], evac=("dve" if sub % 2 == 0 else "act"))
                for nh in range(2):
                    w_ = wload(wgt[:, :, nh * 512:(nh + 1) * 512], self.wtok[("gate", l)])
                    for sub in range(4):
                        bg = self.psbank()
                        for kc in range(8):
                            S.op("pe", lambda p: p.matmul(self.PS(bg), lhsT=xT.t[:, kc, sub * 128:(sub + 1) * 128], rhs=w_.t[:, kc, :], start=(kc == 0), stop=(kc == 7)),
                                 reads=[xT.toks[sub], w_.tok], writes=[self.pst[bg]])
                        bp = self.psbank()
                        for kc in range(2):
                            S.op("pe", lambda p: p.matmul(self.PS(bp), lhsT=p_.t[:, kc, sub * 128:(sub + 1) * 128], rhs=wple.t[:, kc, nh * 512:(nh + 1) * 512], start=(kc == 0), stop=(kc == 1)),
                                 reads=[p_.tok, wple.tok], writes=[self.pst[bp]])
                        r_ = rr[nr % 3]
                        nr += 1
                        S.op("act", lambda a: a.activation(out=r_.t[:], in_=self.PS(bg), func=AF.Sigmoid), reads=[self.pst[bg]], writes=[r_.tok])
                        S.op("dve", lambda v: v.tensor_tensor(out=r_.t[:], in0=r_.t[:], in1=self.PS(bp), op=ALU.mult), reads=[r_.tok, self.pst[bp]], writes=[r_.tok])
                        x_ = xs[sub]
                        S.op("pool", lambda g: g.tensor_tensor(out=x_.t[:, nh * 512:(nh + 1) * 512], in0=x_.t[:, nh * 512:(nh + 1) * 512], in1=r_.t[:], op=ALU.add),
                             reads=[r_.tok, x_.tok], writes=[x_.tok])
                for sub in range(4):
                    ti = blk * 4 + sub
                    x_ = xs[sub]
                    if last:
                        if self.final_norm:
                            y_ = yo[sub % 2]
                            self.rmsnorm_h(x_.t[:], x_.tok, nwf, y_, ss[sub % 2], sd[sub % 2], junk)
                            S.dma("pool", self.y[ti * 128:(ti + 1) * 128, :], y_.t[:], reads=[y_.tok], writes=[self.t_y])
                        else:
                            S.dma("pool", self.y[ti * 128:(ti + 1) * 128, :], x_.t[:], reads=[x_.tok], writes=[self.t_y])
                    else:
                        S.dma("pool", self.xres[ti * 128:(ti + 1) * 128, :], x_.t[:], reads=[x_.tok], writes=[self.t_xres[ti]])


_CACHE = {}


def _get_nc(T, layers, final_norm=True):
    key = (T, tuple(layers), final_norm)
    if key not in _CACHE:
        b = Builder(T, list(layers), final_norm)
        nc = b.build()
        print(f"[kernel] built T={T} layers={layers}: {b.S.ninst} instr, {b.S.nwaits} waits; per-engine "
              + str({n: e["cnt"] for n, e in b.S.eng.items()}), flush=True)
        _CACHE[key] = nc
    return _CACHE[key]


def run(inputs, T, layers, ncores, final_norm=True, x_override=None):
    layers = list(layers)
    nc = _get_nc(T, layers, final_norm)
    consts = make_consts(T)
    f32 = lambda a: np.ascontiguousarray(np.asarray(a), dtype=np.float32)
    mall, ev, mev, od, mod = layer_maps(layers)
    shared = {}
    for k in ("norm_mix", "norm_mlp", "w_mlp_up", "w_mlp_down", "w_ple_proj", "w_ple_gate"):
        shared[k] = f32(np.asarray(inputs[k])[layers])
    for k in ("a_log", "dt_bias", "gdn_norm", "lam_q1", "lam_k1", "lam_q2", "lam_k2", "diff_norm", "w_out_even"):
        shared[k] = f32(np.asarray(inputs[k])[ev])
    for k in ("w_in_odd", "b_forget", "w_out_odd"):
        shared[k] = f32(np.asarray(inputs[k])[od])
    shared["norm_final"] = f32(inputs["norm_final"]).reshape(1, D)
    wie = f32(np.asarray(inputs["w_in_even"])[ev])
    idx = []
    for which in range(2):
        for h in range(4):
            for m in range(2):
                for d in range(64):
                    idx.append(2056 + which * 512 + h * 128 + m * 64 + (d + 32) % 64)
    shared["w_in_even"] = np.ascontiguousarray(np.concatenate([wie, wie[:, :, idx]], axis=2))
    cw = f32(np.asarray(inputs["conv_w"])[ev])
    shared["convw"] = np.ascontiguousarray(cw.reshape(len(ev), 4, 12, 128).transpose(0, 3, 2, 1))
    shared.update(consts)
    x = np.asarray(inputs["x"]) if x_override is None else x_override
    p = np.asarray(inputs["p"])
    pos = np.asarray(inputs["positions"])
    in_maps = []
    for b in range(ncores):
        m = dict(shared)
        m["x"] = f32(x[b, :T])
        m["pT"] = f32(np.transpose(p[layers, b, :T, :], (0, 2, 1)))
        m["pos"] = np.ascontiguousarray(pos[b, :T].reshape(1, T).astype(np.int32))
        in_maps.append(m)
    res = run_bass_kernel_spmd(nc, in_maps, core_ids=list(range(ncores)))
    return np.stack([np.asarray(r["y"]) for r in res.results], axis=0)


def kernel(**inputs):
    return run(inputs, 4096, list(range(DEPTH)), 8, final_norm=True).astype(np.float32)
```

```python
import math
import os
from contextlib import ExitStack
import numpy as np
import concourse.bass as bass
import concourse.mybir as mybir
from concourse.bass_utils import run_bass_kernel_spmd

F32 = mybir.dt.float32
BF16 = mybir.dt.bfloat16
I32 = mybir.dt.int32
ALU = mybir.AluOpType
AF = mybir.ActivationFunctionType
AX = mybir.AxisListType

D = 1024
DFF = 4096
DEPTH = 4
EVEN_IN = 3592
ODD_IN = 4104
EPS = 1e-6
NEG = -30000.0


class Tok:
    __slots__ = ("name", "w", "r")

    def __init__(self, name=""):
        self.name = name
        self.w = None
        self.r = []


class Buf:
    def __init__(self, t, n=1, name=""):
        self.t = t
        self.toks = [Tok(f"{name}{i}") for i in range(n)]
        self.tok = self.toks[0]


class Sched:
    NSLOT = 10
    NSPARE = 76
    LIMIT = int(os.environ.get("K_LIMIT", "12000"))

    def __init__(self, nc):
        self.nc = nc
        self.sems = []
        self.eng = {}
        self._ctx = []
        names = ["pe", "dve", "act", "pool", "sp"]
        handles = [nc.tensor, nc.vector, nc.scalar, nc.gpsimd, nc.sync]
        for n, h in zip(names, handles):
            self.eng[n] = dict(h=h, sem=self._newsem(n), cnt=0, clock=None, dslots=[], dnext=0)
        for n in ["sp", "pool"]:
            e = self.eng[n]
            for i in range(self.NSLOT):
                e["dslots"].append(dict(sem=self._newsem(f"{n}d{i}"), cnt=0))
        self.spare = [self._newsem(f"sp{i}") for i in range(self.NSPARE)]
        self.final = {}
        ns = len(self.sems)
        for e in self.eng.values():
            e["clock"] = np.zeros(ns, dtype=np.int64)
            e["own"] = {e["sem"]}
        self.nwaits = 0
        self.ninst = 0
        self._war = []
        self.psn = 0

    def _newsem(self, name):
        cm = self.nc.semaphore(name)
        h = cm.__enter__()
        self._ctx.append(cm)
        self.sems.append(h)
        return len(self.sems) - 1

    def _deps(self, reads, writes):
        deps = []
        for t in reads:
            if t.w is not None:
                deps.append(t.w)
        self._war = []
        for t in writes:
            if t.w is not None:
                deps.append(t.w)
            self._war.extend(t.r)
        return deps

    def _wait_for(self, en, deps):
        e = self.eng[en]
        clock = e["clock"]
        own = e["own"]
        war = [d for d in self._war if d[0] not in own]
        self._war = []
        need = {}
        for (s, v, ck) in list(deps) + war:
            if en == "pe" and s in own:
                continue
            if clock[s] >= v:
                continue
            if need.get(s, (0, None))[0] < v:
                need[s] = (v, ck)
        for s, (v, ck) in need.items():
            if clock[s] >= v:
                continue
            e["h"].wait_ge(self.sems[s], int(v))
            self.nwaits += 1
            clock[s] = v
            if ck is not None:
                np.maximum(clock, ck, out=clock)

    def _record(self, ev, reads, writes):
        for t in reads:
            t.r = [d for d in t.r if d[0] != ev[0]]
            t.r.append(ev)
        for t in writes:
            t.w = ev
            t.r = []

    def op(self, en, fn, reads=(), writes=()):
        e = self.eng[en]
        if e["cnt"] >= self.LIMIT:
            self.final[e["sem"]] = e["cnt"]
            e["sem"] = self.spare.pop()
            e["own"].add(e["sem"])
            e["cnt"] = 0
        self._wait_for(en, self._deps(reads, writes))
        ins = fn(e["h"])
        e["cnt"] += 1
        ins.then_inc(self.sems[e["sem"]], 1)
        self.ninst += 1
        ev = (e["sem"], e["cnt"], e["clock"].copy())
        self._record(ev, reads, writes)
        return ev

    def dma(self, en, out, in_, reads=(), writes=(), nowaw=False, **kw):
        e = self.eng[en]
        if nowaw:
            for t in writes:
                t.w = None
        slot = e["dslots"][e["dnext"] % self.NSLOT]
        e["dnext"] += 1
        if slot["cnt"] * 16 >= self.LIMIT:
            self.final[slot["sem"]] = slot["cnt"] * 16
            slot["sem"] = self.spare.pop()
            slot["cnt"] = 0
        deps = self._deps(reads, writes)
        if slot["cnt"] > 0:
            deps.append((slot["sem"], slot["cnt"] * 16, None))
        self._wait_for(en, deps)
        ins = e["h"].dma_start(out=out, in_=in_, **kw)
        slot["cnt"] += 1
        ins.then_inc(self.sems[slot["sem"]], 16)
        self.ninst += 1
        ev = (slot["sem"], slot["cnt"] * 16, e["clock"].copy())
        self._record(ev, reads, writes)
        return ev

    def barrier(self):
        cur = np.zeros(len(self.sems), dtype=np.int64)
        for s_, v_ in self.final.items():
            cur[s_] = v_
        for e in self.eng.values():
            cur[e["sem"]] = e["cnt"]
            for sl in e["dslots"]:
                cur[sl["sem"]] = sl["cnt"] * 16
        for en, e in self.eng.items():
            clock = e["clock"]
            for s in range(len(self.sems)):
                if s in e["own"]:
                    continue
                if clock[s] < cur[s]:
                    e["h"].wait_ge(self.sems[s], int(cur[s]))
                    self.nwaits += 1
                    clock[s] = cur[s]

    def close(self):
        for cm in reversed(self._ctx):
            cm.__exit__(None, None, None)


def make_consts(T):
    import ml_dtypes
    c = {}
    c["ident_f"] = np.eye(128, dtype=np.float32)
    tri = np.where(np.arange(128)[:, None] <= np.arange(128)[None, :], 0.0, NEG).astype(np.float32)
    c["trimask"] = tri
    esel = np.zeros((128, 8, 128), dtype=np.float32)
    for h in range(8):
        esel[:, h, h] = 1.0
    c["esel"] = esel
    esel2 = np.zeros((128, 4, 128), dtype=np.float32)
    for h in range(4):
        for r in range(128):
            esel2[r, h, 2 * h + r // 64] = 1.0
    c["esel2"] = esel2
    r = np.arange(128)
    d = r % 64
    invf = (10000.0 ** (-(2.0 * (d % 32)) / 64.0)).astype(np.float32)
    sgn = np.where(d < 32, -1.0, 1.0).astype(np.float32)
    c["rope"] = np.stack([invf, sgn], axis=1).astype(np.float32)
    ii = np.arange(128)[:, None]
    jj = np.arange(128)[None, :]
    c["negup"] = np.where(jj >= ii, 0.0, NEG).astype(np.float32)
    c["poslow"] = np.where(jj < ii, 0.0, -NEG).astype(np.float32)
    cm = np.ones((4, T), dtype=np.float32)
    cm[:, ::128] = 0.0
    c["cmask"] = cm
    return c


import os
STOP = os.environ.get("K_STOP", "")


class StopBuild(Exception):
    pass


class LIdx:
    def __init__(self, ap, mapping):
        self.ap = ap
        self.m = mapping

    def __getitem__(self, l):
        return self.ap[self.m[l]]


def layer_maps(layers):
    mall = {l: i for i, l in enumerate(layers)}
    ev = [l // 2 for l in layers if l % 2 == 0]
    od = [l // 2 for l in layers if l % 2 == 1]
    mev = {j: i for i, j in enumerate(ev)}
    mod = {j: i for i, j in enumerate(od)}
    return mall, (ev or [0]), mev, (od or [0]), mod


class Builder:
    def __init__(self, T, layers, final_norm=True):
        self.T = T
        self.NT = T // 128
        self.NB = T // 512
        self.layers = layers
        self.final_norm = final_norm
        self.nc = bass.Bass("TRN2", target_bir_lowering=False)
        self.S = None

    def declare(self):
        nc, T = self.nc, self.T
        I = lambda n, s, d=F32: nc.dram_tensor(n, s, d, kind="ExternalInput").ap()
        self.x = I("x", [T, D])
        mall, ev, mev, od, mod = layer_maps(self.layers)
        NL, NE, NO = len(self.layers), len(ev), len(od)
        A = lambda n, s: LIdx(I(n, [NL] + s), mall)
        E = lambda n, s: LIdx(I(n, [NE] + s), mev)
        O = lambda n, s: LIdx(I(n, [NO] + s), mod)
        self.pT = A("pT", [256, T])
        self.pos = I("pos", [1, T], I32)
        self.norm_mix = A("norm_mix", [D])
        self.norm_mlp = A("norm_mlp", [D])
        self.norm_final = I("norm_final", [1, D])
        self.w_in_even = E("w_in_even", [D, EVEN_IN + 1024])
        self.convw = E("convw", [128, 12, 4])
        self.a_log = E("a_log", [4])
        self.dt_bias = E("dt_bias", [4])
        self.gdn_norm = E("gdn_norm", [128])
        self.lam = [E(n, [64]) for n in ("lam_q1", "lam_k1", "lam_q2", "lam_k2")]
        self.diff_norm = E("diff_norm", [128])
        self.w_out_even = E("w_out_even", [D, D])
        self.w_in_odd = O("w_in_odd", [D, ODD_IN])
        self.b_forget = O("b_forget", [8])
        self.w_out_odd = O("w_out_odd", [D, D])
        self.w_mlp_up = A("w_mlp_up", [D, DFF])
        self.w_mlp_down = A("w_mlp_down", [DFF, D])
        self.w_ple_proj = A("w_ple_proj", [256, D])
        self.w_ple_gate = A("w_ple_gate", [D, D])
        self.c_ident = I("ident_f", [128, 128])
        self.c_tri = I("trimask", [128, 128])
        self.c_esel = I("esel", [128, 8, 128])
        self.c_esel2 = I("esel2", [128, 4, 128])
        self.c_rope = I("rope", [128, 2])
        self.c_negup = I("negup", [128, 128])
        self.c_poslow = I("poslow", [128, 128])
        self.c_cmask = I("cmask", [4, T])
        self.y = nc.dram_tensor("y", [T, D], F32, kind="ExternalOutput").ap()
        Sc = lambda n, s, d=F32: nc.dram_tensor(n, s, d, kind="Internal").ap()
        self.xres = Sc("xres", [T, D])
        self.obuf = Sc("obuf", [T, D])
        self.gbuf = Sc("gbuf", [T, D], BF16)
        self.qTd = Sc("qTd", [8, 128, T], BF16)
        self.kTd = Sc("kTd", [8, 128, T], BF16)
        self.vaug = Sc("vaug", [T, 8 * 129], BF16)
        self.obuf2 = Sc("obuf2", [T, D])
        self.gq = Sc("gq", [4, 128, T], BF16)
        self.gk = Sc("gk", [4, 128, T], BF16)
        self.gv = Sc("gv", [4, 128, T], BF16)
        self.gcd = Sc("gcd", [4, T])
        self.egcd = Sc("egcd", [4, T])
        self.t_g = Tok("g")
        self.t_gcd = Tok("gcd")
        self.t_obuf2 = [Tok(f"obuf2{i}") for i in range(self.NT)]
        self.qbd = Sc("qbd", [8, 5, T], BF16)
        self.kbd = Sc("kbd", [8, 5, T], BF16)
        self.wb = {}
        for l in self.layers:
            j = l // 2
            if l % 2 == 0:
                self.wb[("in", l)] = (Sc(f"wbin{l}", [D, EVEN_IN + 1024], BF16), self.w_in_even[j])
                self.wb[("out", l)] = (Sc(f"wbout{l}", [D, D], BF16), self.w_out_even[j])
            else:
                self.wb[("in", l)] = (Sc(f"wbin{l}", [D, ODD_IN], BF16), self.w_in_odd[j])
                self.wb[("out", l)] = (Sc(f"wbout{l}", [D, D], BF16), self.w_out_odd[j])
            self.wb[("up", l)] = (Sc(f"wbup{l}", [D, DFF], BF16), self.w_mlp_up[l])
            self.wb[("down", l)] = (Sc(f"wbdn{l}", [DFF, D], BF16), self.w_mlp_down[l])
            self.wb[("ple", l)] = (Sc(f"wbple{l}", [256, D], BF16), self.w_ple_proj[l])
            self.wb[("gate", l)] = (Sc(f"wbgate{l}", [D, D], BF16), self.w_ple_gate[l])
        self.wtok = {k: Tok(str(k)) for k in self.wb}
        self.t_xres = [Tok(f"xres{i}") for i in range(self.NT)]
        self.t_obuf = [Tok(f"obuf{i}") for i in range(self.NT)]
        self.t_gbuf = [Tok(f"gbuf{i}") for i in range(self.NT)]
        self.t_qT = [Tok(f"qT{h}") for h in range(8)]
        self.t_kT = [Tok(f"kT{h}") for h in range(8)]
        self.t_vaug = Tok("vaug")
        self.t_qbd = Tok("qbd")
        self.t_kbd = Tok("kbd")
        self.t_y = Tok("y")

    def sb(self, es, name, shape, dt, n=1):
        self._uid = getattr(self, "_uid", 0) + 1
        name = f"{name}_u{self._uid}"
        t = es.enter_context(self.nc.sbuf_tensor(name, shape, dt))
        return Buf(t, n, name)

    def ring(self, es, name, shape, dt, k, n=1):
        return [self.sb(es, f"{name}{i}", shape, dt, n) for i in range(k)]

    def psbank(self):
        i = self.S.psn % 8
        self.S.psn += 1
        return i

    def PS(self, b):
        return self.ps[:, b, :]

    def PSB(self, b):
        return self.ps[:, b, :].bitcast(BF16).rearrange("p (c n) -> p c n", n=128)

    def transpose8(self, src, src_toks, dstT, dst_toks, evac="dve"):
        S = self.S
        b = self.psbank()
        pv = self.PSB(b)
        for c in range(8):
            S.op("pe", lambda p: p.transpose(out=pv[:, c, :], in_=src[:, c * 128:(c + 1) * 128], identity=self.identb.t[:]),
                 reads=list(src_toks) + [self.identb.tok], writes=[self.pst[b]])
        if evac == "dve":
            S.op("dve", lambda v: v.tensor_copy(out=dstT, in_=pv), reads=[self.pst[b]], writes=dst_toks)
        else:
            S.op("act", lambda a: a.copy(out=dstT, in_=pv), reads=[self.pst[b]], writes=dst_toks)

    def rmsnorm_h(self, xt, xtok, wtile, hb, ss, sd, junk):
        S = self.S
        S.op("act", lambda a: a.activation(out=junk.t[:], in_=xt, func=AF.Square, accum_out=ss.t[:]),
             reads=[xtok], writes=[junk.tok, ss.tok])
        S.op("act", lambda a: a.activation(out=sd.t[:], in_=ss.t[:], func=AF.Sqrt, scale=1.0 / D, bias=self.epsb.t[:]),
             reads=[ss.tok, self.epsb.tok], writes=[sd.tok])
        S.op("dve", lambda v: v.reciprocal(out=sd.t[:], in_=sd.t[:]), reads=[sd.tok], writes=[sd.tok])
        S.op("dve", lambda v: v.scalar_tensor_tensor(out=hb.t[:], in0=xt, scalar=sd.t[:], in1=wtile.t[:], op0=ALU.mult, op1=ALU.mult),
             reads=[xtok, sd.tok, wtile.tok], writes=[hb.tok])

    def build(self):
        nc = self.nc
        self.declare()
        self.S = S = Sched(nc)
        with ExitStack() as es0:
            self.ps = es0.enter_context(nc.psum_tensor("ps", [128, 8, 512], F32))
            self.pst = [Tok(f"ps{i}") for i in range(8)]
            self.identb = self.sb(es0, "identb", [128, 128], BF16)
            self.identf = self.sb(es0, "identf", [128, 128], F32)
            self.trib = self.sb(es0, "trib", [128, 128], BF16)
            self.eselb = self.sb(es0, "eselb", [128, 8, 128], BF16)
            self.esel2b = self.sb(es0, "esel2b", [128, 4, 128], BF16)
            S.dma("pool", self.esel2b.t[:], self.c_esel2, writes=[self.esel2b.tok])
            self.epsb = self.sb(es0, "epsb", [128, 1], F32)
            self.zerob = self.sb(es0, "zerob", [128, 512], BF16)
            S.dma("sp", self.identf.t[:], self.c_ident, writes=[self.identf.tok])
            S.dma("pool", self.identb.t[:], self.c_ident, writes=[self.identb.tok])
            S.dma("pool", self.trib.t[:], self.c_tri, writes=[self.trib.tok])
            S.dma("pool", self.eselb.t[:], self.c_esel, writes=[self.eselb.tok])
            S.op("dve", lambda v: v.memset(self.epsb.t[:], EPS), writes=[self.epsb.tok])
            S.op("dve", lambda v: v.memset(self.zerob.t[:], 0.0), writes=[self.zerob.tok])
            xv = self.x.rearrange("(n p) c -> p n c", p=128)
            xr = self.xres.rearrange("(n p) c -> p n c", p=128)
            for i in range(0, self.NT, 4):
                S.dma("sp", xr[:, i:i + 4, :], xv[:, i:i + 4, :], writes=self.t_xres[i:i + 4])
            def cast_weights(l):
                for kind in ("in", "out", "up", "down", "ple", "gate"):
                    dst, src = self.wb[(kind, l)]
                    rows = dst.shape[0]
                    step = 512
                    for r0 in range(0, rows, step):
                        r1 = min(rows, r0 + step)
                        S.dma("pool", dst[r0:r1, :], src[r0:r1, :], writes=[self.wtok[(kind, l)]])

            cast_weights(self.layers[0])
            for li, l in enumerate(self.layers):
                last = (li == len(self.layers) - 1)
                if not last:
                    cast_weights(self.layers[li + 1])
                self.stopped = False
                if l % 2 == 1:
                    self.odd_layer(l)
                else:
                    self.even_layer(l)
                if STOP == "E3":
                    self.stopped = True
                if not self.stopped:
                    self.tail(l, last)
            S._wait_for("sp", [self.t_y.w] if self.t_y.w else [])
            S.barrier()
        S.close()
        return nc

    def odd_layer(self, l):
        nc, S, T, NT, NB = self.nc, self.S, self.T, self.NT, self.NB
        j = l // 2
        scale = 128 ** -0.5
        with ExitStack() as esL:
            fT = self.sb(esL, "fT", [8, T], F32)
            qsq = self.sb(esL, "qsq", [8, T], F32)
            ksq = self.sb(esL, "ksq", [8, T], F32)
            S.barrier()
            with ExitStack() as es:
                WP = ODD_IN + 120
                win = self.sb(es, "win", [128, 8, WP], BF16)
                wsrc = self.wb[("in", l)][0].rearrange("(c p) n -> p c n", p=128)
                S.op("pool", lambda g: g.memset(win.t[:, :, ODD_IN:WP], 0.0), writes=[win.tok])
                for kc in range(8):
                    S.dma("sp", win.t[:, kc, 0:ODD_IN], wsrc[:, kc, :], reads=[self.wtok[("in", l)]], writes=[win.tok])
                nw = self.sb(es, "nw", [128, D], F32)
                S.dma("sp", nw.t[:], self.norm_mix[l].partition_broadcast(128), writes=[nw.tok])
                xr = self.ring(es, "xr", [128, D], F32, 3)
                hb = self.ring(es, "hb", [128, D], BF16, 2)
                junk = self.sb(es, "junk", [128, D], BF16)
                ss = self.ring(es, "ss", [128, 1], F32, 2)
                sd = self.ring(es, "sd", [128, 1], F32, 2)
                hT = self.ring(es, "hT", [128, 8, 512], BF16, 2, n=4)
                qt = self.ring(es, "qt", [128, 512], BF16, 4)
                sq = self.ring(es, "sq", [128, 512], BF16, 3)
                vt = self.ring(es, "vt", [128, 8, 129], BF16, 2)
                gt = self.ring(es, "gt", [128, D], BF16, 2)
                for v_ in vt:
                    S.op("pool", lambda g: g.memset(v_.t[:], 1.0), writes=[v_.tok])
                nx = 0
                nq = 0
                nv = 0
                for blk in range(NB):
                    t0 = blk * 512
                    h_T = hT[blk % 2]
                    for sub in range(4):
                        ti = blk * 4 + sub
                        xb_ = xr[nx % 3]
                        hb_ = hb[nx % 2]
                        S.dma("sp", xb_.t[:], self.xres[ti * 128:(ti + 1) * 128, :], reads=[self.t_xres[ti]], writes=[xb_.tok])
                        self.rmsnorm_h(xb_.t[:], xb_.tok, nw, hb_, ss[nx % 2], sd[nx % 2], junk)
                        self.transpose8(hb_.t[:], [hb_.tok], h_T.t[:, :, sub * 128:(sub + 1) * 128], [h_T.toks[sub]],
                                        evac=("dve" if sub % 2 == 0 else "act"))
                        nx += 1
                    bq = self.psbank()
                    bk = self.psbank()
                    for which in range(2):
                        bstat = bq if which == 0 else bk
                        for h in range(8):
                            b = self.psbank()
                            while b in (bq, bk):
                                b = self.psbank()
                            c0 = which * 1024 + h * 128
                            for kc in range(8):
                                S.op("pe", lambda p: p.matmul(self.PS(b), lhsT=win.t[:, kc, c0:c0 + 128], rhs=h_T.t[:, kc, :], start=(kc == 0), stop=(kc == 7)),
                                     reads=[win.tok] + h_T.toks, writes=[self.pst[b]])
                            q_ = qt[nq % 4]
                            s_ = sq[nq % 3]
                            nq += 1
                            S.op("act", lambda a: a.activation(out=q_.t[:], in_=self.PS(b), func=AF.Copy, scale=(scale if which == 0 else 1.0)),
                                 reads=[self.pst[b]], writes=[q_.tok])
                            dst = (self.qTd if which == 0 else self.kTd)[h][:, t0:t0 + 512]
                            S.dma("pool", dst, q_.t[:], reads=[q_.tok], writes=[(self.t_qT if which == 0 else self.t_kT)[h]])
                            S.op("dve", lambda v: v.tensor_tensor(out=s_.t[:], in0=q_.t[:], in1=q_.t[:], op=ALU.mult), reads=[q_.tok], writes=[s_.tok])
                            S.op("pe", lambda p: p.matmul(self.PS(bstat), lhsT=self.eselb.t[:, h, :], rhs=s_.t[:], start=(h == 0), stop=(h == 7)),
                                 reads=[self.eselb.tok, s_.tok], writes=[self.pst[bstat]])
                        dstat = (qsq if which == 0 else ksq)
                        S.op("dve", lambda v: v.tensor_copy(out=dstat.t[:, t0:t0 + 512], in_=self.ps[0:8, bstat, :]), reads=[self.pst[bstat]], writes=[dstat.tok])
                    b = self.psbank()
                    for kc in range(8):
                        S.op("pe", lambda p: p.matmul(self.PS(b), lhsT=win.t[:, kc, 4096:4096 + 128], rhs=h_T.t[:, kc, :], start=(kc == 0), stop=(kc == 7)),
                             reads=[win.tok] + h_T.toks, writes=[self.pst[b]])
                    S.op("dve", lambda v: v.tensor_copy(out=fT.t[:, t0:t0 + 512], in_=self.ps[0:8, b, :]), reads=[self.pst[b]], writes=[fT.tok])
                    for sub in range(4):
                        ti = blk * 4 + sub
                        v_ = vt[nv % 2]
                        g_ = gt[nv % 2]
                        nv += 1
                        for nh in range(2):
                            b = self.psbank()
                            for kc in range(8):
                                S.op("pe", lambda p: p.matmul(self.PS(b), lhsT=h_T.t[:, kc, sub * 128:(sub + 1) * 128], rhs=win.t[:, kc, 2048 + nh * 512:2048 + (nh + 1) * 512], start=(kc == 0), stop=(kc == 7)),
                                     reads=[win.tok, h_T.toks[sub]], writes=[self.pst[b]])
                            S.op("dve", lambda v: v.tensor_copy(out=v_.t[:, nh * 4:(nh + 1) * 4, 0:128], in_=self.PS(b).rearrange("p (h c) -> p h c", c=128)),
                                 reads=[self.pst[b]], writes=[v_.tok])
                        S.dma("pool", self.vaug[ti * 128:(ti + 1) * 128, :], v_.t[:].rearrange("p h c -> p (h c)"), reads=[v_.tok], writes=[self.t_vaug], nowaw=True)
                        for nh in range(2):
                            b = self.psbank()
                            for kc in range(8):
                                S.op("pe", lambda p: p.matmul(self.PS(b), lhsT=h_T.t[:, kc, sub * 128:(sub + 1) * 128], rhs=win.t[:, kc, 3072 + nh * 512:3072 + (nh + 1) * 512], start=(kc == 0), stop=(kc == 7)),
                                     reads=[win.tok, h_T.toks[sub]], writes=[self.pst[b]])
                            S.op("act", lambda a: a.activation(out=g_.t[:, nh * 512:(nh + 1) * 512], in_=self.PS(b), func=AF.Sigmoid),
                                 reads=[self.pst[b]], writes=[g_.tok])
                        S.dma("pool", self.gbuf[ti * 128:(ti + 1) * 128, :], g_.t[:], reads=[g_.tok], writes=[self.t_gbuf[ti]])
            S.barrier()
            with ExitStack() as es:
                bf = self.sb(es, "bf", [8, 1], F32)
                S.dma("sp", bf.t[:], self.b_forget[j].rearrange("(h o) -> h o", o=1), writes=[bf.tok])
                xa = self.sb(es, "xa", [8, T], F32)
                ya = self.sb(es, "ya", [8, T], F32)
                za = self.sb(es, "za", [8, T], F32)
                ones = self.sb(es, "ones", [8, T], F32)
                km = self.sb(es, "km", [8, 1], F32)
                b1 = self.sb(es, "b1", [8, T], BF16)
                b2 = self.sb(es, "b2", [8, T], BF16)
                b3 = self.sb(es, "b3", [8, T], BF16)
                onesb = self.sb(es, "onesb", [8, T], BF16)
                S.op("dve", lambda v: v.memset(ones.t[:], 1.0), writes=[ones.tok])
                S.op("pool", lambda g: g.memset(onesb.t[:], 1.0), writes=[onesb.tok])
                S.op("dve", lambda v: v.tensor_scalar(out=xa.t[:], in0=fT.t[:], scalar1=bf.t[:], scalar2=None, op0=ALU.add), reads=[fT.tok, bf.tok], writes=[xa.tok])
                S.op("dve", lambda v: v.scalar_tensor_tensor(out=ya.t[:], in0=xa.t[:], scalar=-1.0, in1=xa.t[:], op0=ALU.mult, op1=ALU.max), reads=[xa.tok], writes=[ya.tok])
                S.op("act", lambda a: a.activation(out=ya.t[:], in_=ya.t[:], func=AF.Exp, scale=-1.0), reads=[ya.tok], writes=[ya.tok])
                S.op("act", lambda a: a.activation(out=ya.t[:], in_=ya.t[:], func=AF.Ln, bias=1.0), reads=[ya.tok], writes=[ya.tok])
                S.op("dve", lambda v: v.scalar_tensor_tensor(out=za.t[:], in0=xa.t[:], scalar=0.0, in1=ya.t[:], op0=ALU.min, op1=ALU.subtract), reads=[xa.tok, ya.tok], writes=[za.tok])
                S.op("dve", lambda v: v.tensor_tensor_scan(out=xa.t[:], data0=ones.t[:], data1=za.t[:], initial=0.0, op0=ALU.mult, op1=ALU.add), reads=[ones.tok, za.tok, xa.tok], writes=[xa.tok])
                S.op("dve", lambda v: v.tensor_reduce(out=km.t[:], in_=ksq.t[:], axis=AX.X, op=ALU.max), reads=[ksq.tok], writes=[km.tok])
                S.op("dve", lambda v: v.tensor_scalar(out=ya.t[:], in0=qsq.t[:], scalar1=128.0, scalar2=km.t[:], op0=ALU.mult, op1=ALU.add), reads=[qsq.tok, km.tok], writes=[ya.tok])
                S.op("dve", lambda v: v.scalar_tensor_tensor(out=ya.t[:], in0=ya.t[:], scalar=-0.5 * scale, in1=xa.t[:], op0=ALU.mult, op1=ALU.add), reads=[ya.tok, xa.tok], writes=[ya.tok])
                S.op("dve", lambda v: v.tensor_copy(out=b1.t[:], in_=ya.t[:]), reads=[ya.tok], writes=[b1.tok])
                S.op("dve", lambda v: v.tensor_tensor(out=b2.t[:], in0=ya.t[:], in1=b1.t[:], op=ALU.subtract), reads=[ya.tok, b1.tok], writes=[b2.tok])
                S.dma("sp", self.qbd[:, 0, :], b1.t[:], reads=[b1.tok], writes=[self.t_qbd])
                S.dma("sp", self.qbd[:, 1, :], b2.t[:], reads=[b2.tok], writes=[self.t_qbd])
                for r in (2, 3, 4):
                    S.dma("sp", self.qbd[:, r, :], onesb.t[:], reads=[onesb.tok], writes=[self.t_qbd])
                for r in (0, 1):
                    S.dma("sp", self.kbd[:, r, :], onesb.t[:], reads=[onesb.tok], writes=[self.t_kbd])
                S.op("dve", lambda v: v.tensor_scalar(out=za.t[:], in0=xa.t[:], scalar1=-1.0, scalar2=None, op0=ALU.mult), reads=[xa.tok], writes=[za.tok])
                S.op("dve", lambda v: v.tensor_copy(out=b1.t[:], in_=za.t[:]), reads=[za.tok, b1.tok], writes=[b1.tok])
                S.op("dve", lambda v: v.tensor_tensor(out=za.t[:], in0=za.t[:], in1=b1.t[:], op=ALU.subtract), reads=[za.tok, b1.tok], writes=[za.tok])
                S.op("dve", lambda v: v.tensor_copy(out=b2.t[:], in_=za.t[:]), reads=[za.tok, b2.tok], writes=[b2.tok])
                S.op("dve", lambda v: v.tensor_tensor(out=b3.t[:], in0=za.t[:], in1=b2.t[:], op=ALU.subtract), reads=[za.tok, b2.tok], writes=[b3.tok])
                S.dma("sp", self.kbd[:, 2, :], b1.t[:], reads=[b1.tok], writes=[self.t_kbd])
                S.dma("sp", self.kbd[:, 3, :], b2.t[:], reads=[b2.tok], writes=[self.t_kbd])
                S.dma("sp", self.kbd[:, 4, :], b3.t[:], reads=[b3.tok], writes=[self.t_kbd])
                S.barrier()
        S.barrier()
        with ExitStack() as es:
            self.attention(es, nheads=8, kdim=128, bias_rows=5, vhead=lambda h: h)

    def attention(self, es, nheads, kdim, bias_rows, vhead, dest=None, dtoks=None):
        nc, S, T, NT, NB = self.nc, self.S, self.T, self.NT, self.NB
        NBUF = 2
        qT = self.ring(es, "aqT", [128, T], BF16, NBUF)
        kT = self.ring(es, "akT", [128, T], BF16, NBUF)
        vA = self.ring(es, "avA", [128, NT, 129], BF16, NBUF)
        sepbias = (kdim == 128)
        if sepbias:
            QB = self.ring(es, "aQB", [128, T], BF16, NBUF)
            KB = self.ring(es, "aKB", [128, T], BF16, NBUF)
            for b_ in QB + KB:
                S.op("pool", lambda g: g.memset(b_.t[:], 0.0), writes=[b_.tok])
        if not sepbias:
            for b_ in qT + kT:
                S.op("pool", lambda g: g.memset(b_.t[:], 0.0), writes=[b_.tok])
        pT = self.ring(es, "apT", [128, 512], BF16, 3)
        ot = self.ring(es, "aot", [128, 4, 128], F32, 2)
        rl = self.ring(es, "arl", [128, 1], F32, 4)
        sbanks = [0, 1, 2]
        accsets = [(3, 4), (5, 6)]
        ns = 0
        nqb = 0
        nr = 0
        npt = 0
        vsrc = self.vaug.rearrange("(n p) (h c) -> p n h c", p=128, c=129)
        if dest is None:
            dest, dtoks = self.obuf, self.t_obuf
        osrc = dest.rearrange("(n p) c -> p n c", p=128)
        for h in range(nheads):
            q_, k_, v_ = qT[h % NBUF], kT[h % NBUF], vA[h % NBUF]
            rows = kdim if sepbias else kdim + bias_rows
            lrows = rows
            if not sepbias:
                rows = 96
            S.dma("sp", q_.t[0:lrows, :], self.qTd[h][0:lrows, :], reads=[self.t_qT[h]], writes=[q_.tok])
            S.dma("sp", k_.t[0:lrows, :], self.kTd[h][0:lrows, :], reads=[self.t_kT[h]], writes=[k_.tok])
            for n0 in range(0, NT, 8):
                n1 = min(NT, n0 + 8)
                S.dma("sp", v_.t[:, n0:n1, :], vsrc[:, n0:n1, vhead(h), :], reads=[self.t_vaug], writes=[v_.tok])
            if sepbias:
                qb_, kb_ = QB[h % NBUF], KB[h % NBUF]
                S.dma("sp", qb_.t[0:bias_rows, :], self.qbd[h], reads=[self.t_qbd], writes=[qb_.tok])
                S.dma("sp", kb_.t[0:bias_rows, :], self.kbd[h], reads=[self.t_kbd], writes=[kb_.tok])
            for qb in range(NB):
                accs = accsets[nqb % 2]
                o_ = ot[nqb % 2]
                nqb += 1
                for b in accs:
                    S.op("pe", lambda p: p.matmul(self.PS(b), lhsT=self.zerob.t[:, 0:128], rhs=self.zerob.t[:], start=True, stop=True),
                         reads=[self.zerob.tok], writes=[self.pst[b]])
                nk = 4 * qb + 4

                def s_block(ki, sb_):
                    i = ki - 4 * qb
                    c0 = max(0, i) * 128
                    qs = slice(qb * 512 + c0, qb * 512 + 512)
                    outp = self.ps[:, sb_, c0:512]
                    diag = i >= 0
                    rd = [q_.tok, k_.tok]
                    S.op("pe", lambda p: p.matmul(outp, lhsT=k_.t[0:rows, ki * 128:(ki + 1) * 128], rhs=q_.t[0:rows, qs], start=True, stop=(not sepbias and not diag)),
                         reads=rd, writes=[self.pst[sb_]])
                    if sepbias:
                        S.op("pe", lambda p: p.matmul(outp, lhsT=kb_.t[:, ki * 128:(ki + 1) * 128], rhs=qb_.t[:, qs], start=False, stop=(not diag)),
                             reads=[qb_.tok, kb_.tok], writes=[self.pst[sb_]])
                    if diag:
                        S.op("pe", lambda p: p.matmul(self.ps[:, sb_, c0:c0 + 128], lhsT=self.identb.t[:], rhs=self.trib.t[:], start=False, stop=True),
                             reads=[self.identb.tok, self.trib.tok], writes=[self.pst[sb_]])
                    return c0

                sb_cur = sbanks[ns % 3]
                ns += 1
                c0_cur = s_block(0, sb_cur)
                for ki in range(nk):
                    if ki + 1 < nk:
                        sb_next = sbanks[ns % 3]
                        ns += 1
                        c0_next = s_block(ki + 1, sb_next)
                    p_ = pT[npt % 3]
                    npt += 1
                    c0 = c0_cur
                    S.op("act", lambda a: a.activation(out=p_.t[:, c0:512], in_=self.ps[:, sb_cur, c0:512], func=AF.Exp),
                         reads=[self.pst[sb_cur]], writes=[p_.tok])
                    for sub in range(c0 // 128, 4):
                        bacc = accs[sub // 2]
                        S.op("pe", lambda p: p.matmul(self.ps[:, bacc, (sub % 2) * 256:(sub % 2) * 256 + 129], lhsT=p_.t[:, sub * 128:(sub + 1) * 128], rhs=v_.t[:, ki, :], start=False, stop=(ki == nk - 1), skip_group_check=True),
                             reads=[p_.tok, v_.tok], writes=[self.pst[bacc]])
                    if ki + 1 < nk:
                        sb_cur, c0_cur = sb_next, c0_next
                for sub in range(4):
                    bacc = accs[sub // 2]
                    off = (sub % 2) * 256
                    r_ = rl[nr % 4]
                    nr += 1
                    S.op("dve", lambda v: v.reciprocal(out=r_.t[:], in_=self.ps[:, bacc, off + 128:off + 129]), reads=[self.pst[bacc]], writes=[r_.tok])
                    S.op("dve", lambda v: v.tensor_scalar(out=o_.t[:, sub, :], in0=self.ps[:, bacc, off:off + 128], scalar1=r_.t[:], scalar2=None, op0=ALU.mult),
                         reads=[self.pst[bacc], r_.tok], writes=[o_.tok])
                S.dma("pool", osrc[:, qb * 4:(qb + 1) * 4, h * 128:(h + 1) * 128], o_.t[:], reads=[o_.tok], writes=dtoks[qb * 4:(qb + 1) * 4])

    def even_layer(self, l):
        nc, S, T, NT, NB = self.nc, self.S, self.T, self.NT, self.NB
        j = l // 2
        lam_init = 0.8 - 0.6 * math.exp(-0.3 * l)
        WQB, WKB, WVB = 2056, 2568, 3080
        WSW = EVEN_IN
        WP = EVEN_IN + 1024
        with ExitStack() as esL:
            bT = self.sb(esL, "bT", [4, T], F32)
            aT_ = self.sb(esL, "aT_", [4, T], F32)
            esQ = esL.enter_context(ExitStack())
            qsq = self.sb(esQ, "qsq", [8, T], BF16)
            ksq = self.sb(esQ, "ksq", [8, T], BF16)
            S.barrier()
            with ExitStack() as es:
                win = self.sb(es, "win", [128, 8, WP], BF16)
                wsrc = self.wb[("in", l)][0].rearrange("(c p) n -> p c n", p=128)
                for kc in range(8):
                    S.dma("sp", win.t[:, kc, :], wsrc[:, kc, :], reads=[self.wtok[("in", l)]], writes=[win.tok])
                nw = self.sb(es, "nw", [128, D], F32)
                S.dma("sp", nw.t[:], self.norm_mix[l].partition_broadcast(128), writes=[nw.tok])
                cw = self.sb(es, "cw", [128, 12, 4], F32)
                S.dma("sp", cw.t[:], self.convw[j], writes=[cw.tok])
                invf = self.sb(es, "invf", [128, 2], F32)
                S.dma("sp", invf.t[:], self.c_rope, writes=[invf.tok])
                xr = self.ring(es, "xr", [128, D], F32, 2)
                hb = self.ring(es, "hb", [128, D], BF16, 2)
                junk = self.sb(es, "junk", [128, D], BF16)
                ss = self.ring(es, "ss", [128, 1], F32, 2)
                sd = self.ring(es, "sd", [128, 1], F32, 2)
                hT = self.ring(es, "hT", [128, 8, 512], BF16, 1, n=4)
                raw = self.ring(es, "raw", [128, 515], F32, 2)
                hist = self.sb(es, "hist", [128, 12, 3], F32)
                cv = self.ring(es, "cv", [128, 512], F32, 2)
                sl = self.ring(es, "sl", [128, 512], F32, 2)
                sq = self.ring(es, "sq", [128, 512], BF16, 2)
                rn = self.ring(es, "rn", [128, 512], F32, 1)
                qo = self.ring(es, "qo", [128, 512], BF16, 2)
                posi = self.sb(es, "posi", [128, 512], I32)
                ang = self.sb(es, "ang", [128, 512], F32)
                kf = self.sb(es, "kf", [128, 512], F32)
                ki_ = self.sb(es, "ki_", [128, 512], I32)
                ctab = self.sb(es, "ctab", [128, 512], F32)
                stab = self.sb(es, "stab", [128, 512], F32)
                t1 = self.ring(es, "t1", [128, 512], F32, 1)
                t2 = self.ring(es, "t2", [128, 512], F32, 1)
                vt = self.ring(es, "vt", [128, 8, 129], BF16, 2)
                gt = self.ring(es, "gt", [128, D], BF16, 2)
                onesb = self.sb(es, "onesb", [128, 128], BF16)
                S.op("pool", lambda g: g.memset(onesb.t[:], 1.0), writes=[onesb.tok])
                S.op("pool", lambda g: g.memset(hist.t[:], 0.0), writes=[hist.tok])
                for v_ in vt:
                    S.op("pool", lambda g: g.memset(v_.t[:], 1.0), writes=[v_.tok])
                for g_ in gt:
                    S.op("pool", lambda g: g.memset(g_.t[:], 1.0), writes=[g_.tok])
                nx = nraw = ncv = nq = nv = 0
                TWO_PI = 2.0 * math.pi
                for blk in range(NB):
                    t0 = blk * 512
                    h_T = hT[0]
                    for sub in range(4):
                        ti = blk * 4 + sub
                        xb_ = xr[nx % 2]
                        hb_ = hb[nx % 2]
                        S.dma("sp", xb_.t[:], self.xres[ti * 128:(ti + 1) * 128, :], reads=[self.t_xres[ti]], writes=[xb_.tok])
                        self.rmsnorm_h(xb_.t[:], xb_.tok, nw, hb_, ss[nx % 2], sd[nx % 2], junk)
                        self.transpose8(hb_.t[:], [hb_.tok], h_T.t[:, :, sub * 128:(sub + 1) * 128], [h_T.toks[sub]],
                                        evac=("dve" if sub % 2 == 0 else "act"))
                        nx += 1
                    S.dma("sp", posi.t[:], self.pos[0, t0:t0 + 512].partition_broadcast(128), writes=[posi.tok])
                    S.op("dve", lambda v: v.tensor_copy(out=ang.t[:], in_=posi.t[:]), reads=[posi.tok], writes=[ang.tok])
                    S.op("dve", lambda v: v.tensor_scalar(out=ang.t[:], in0=ang.t[:], scalar1=invf.t[:, 0:1], scalar2=None, op0=ALU.mult), reads=[ang.tok, invf.tok], writes=[ang.tok])
                    for (tab, shift) in ((stab, 0.0), (ctab, 0.5 * math.pi)):
                        S.op("dve", lambda v: v.tensor_scalar(out=kf.t[:], in0=ang.t[:], scalar1=shift, scalar2=1.0 / TWO_PI, op0=ALU.add, op1=ALU.mult), reads=[ang.tok], writes=[kf.tok])
                        S.op("dve", lambda v: v.tensor_copy(out=ki_.t[:], in_=kf.t[:]), reads=[kf.tok], writes=[ki_.tok])
                        S.op("dve", lambda v: v.tensor_copy(out=kf.t[:], in_=ki_.t[:]), reads=[ki_.tok], writes=[kf.tok])
                        S.op("dve", lambda v: v.scalar_tensor_tensor(out=kf.t[:], in0=kf.t[:], scalar=-TWO_PI, in1=ang.t[:], op0=ALU.mult, op1=ALU.add), reads=[kf.tok, ang.tok], writes=[kf.tok])
                        S.op("dve", lambda v: v.tensor_scalar(out=kf.t[:], in0=kf.t[:], scalar1=shift, scalar2=None, op0=ALU.add), reads=[kf.tok], writes=[kf.tok])
                        S.op("dve", lambda v: v.tensor_scalar(out=tab.t[:], in0=kf.t[:], scalar1=math.pi, scalar2=-TWO_PI, op0=ALU.is_gt, op1=ALU.mult), reads=[kf.tok], writes=[tab.tok])
                        S.op("dve", lambda v: v.tensor_tensor(out=kf.t[:], in0=kf.t[:], in1=tab.t[:], op=ALU.add), reads=[kf.tok, tab.tok], writes=[kf.tok])
                        S.op("dve", lambda v: v.tensor_scalar(out=tab.t[:], in0=kf.t[:], scalar1=-math.pi, scalar2=TWO_PI, op0=ALU.is_lt, op1=ALU.mult), reads=[kf.tok], writes=[tab.tok])
                        S.op("dve", lambda v: v.tensor_tensor(out=kf.t[:], in0=kf.t[:], in1=tab.t[:], op=ALU.add), reads=[kf.tok, tab.tok], writes=[kf.tok])
                        S.op("act", lambda a: a.activation(out=tab.t[:], in_=kf.t[:], func=AF.Sin), reads=[kf.tok], writes=[tab.tok])
                    S.op("dve", lambda v: v.tensor_scalar(out=stab.t[:], in0=stab.t[:], scalar1=invf.t[:, 1:2], scalar2=None, op0=ALU.mult), reads=[stab.tok, invf.tok], writes=[stab.tok])
                    for c in range(12):
                        b = self.psbank()
                        for kc in range(8):
                            S.op("pe", lambda p: p.matmul(self.PS(b), lhsT=win.t[:, kc, c * 128:(c + 1) * 128], rhs=h_T.t[:, kc, :], start=(kc == 0), stop=(kc == 7)),
                                 reads=[win.tok] + h_T.toks, writes=[self.pst[b]])
                        r_ = raw[nraw % 2]
                        nraw += 1
                        S.op("act", lambda a: a.copy(out=r_.t[:, 3:515], in_=self.PS(b)), reads=[self.pst[b]], writes=[r_.tok])
                        S.op("pool", lambda g: g.tensor_copy(out=r_.t[:, 0:3], in_=hist.t[:, c, :]), reads=[hist.tok, r_.tok], writes=[r_.tok])
                        S.op("pool", lambda g: g.tensor_copy(out=hist.t[:, c, :], in_=r_.t[:, 512:515]), reads=[r_.tok, hist.tok], writes=[hist.tok])
                        c_ = cv[ncv % 2]
                        s_ = sl[ncv % 2]
                        ncv += 1
                        S.op("dve", lambda v: v.tensor_scalar(out=c_.t[:], in0=r_.t[:, 0:512], scalar1=cw.t[:, c, 0:1], scalar2=None, op0=ALU.mult), reads=[r_.tok, cw.tok], writes=[c_.tok])
                        for i in (1, 2, 3):
                            S.op("dve", lambda v: v.scalar_tensor_tensor(out=c_.t[:], in0=r_.t[:, i:i + 512], scalar=cw.t[:, c, i:i + 1], in1=c_.t[:], op0=ALU.mult, op1=ALU.add),
                                 reads=[r_.tok, cw.tok, c_.tok], writes=[c_.tok])
                        S.op("act", lambda a: a.activation(out=s_.t[:], in_=c_.t[:], func=AF.Silu), reads=[c_.tok], writes=[s_.tok])
                        o_ = qo[nq % 2]
                        nq += 1
                        hh = c % 4
                        if c < 8:
                            q2 = sq[ncv % 2]
                            S.op("pool", lambda g: g.tensor_tensor(out=q2.t[:], in0=s_.t[:], in1=s_.t[:], op=ALU.mult), reads=[s_.tok], writes=[q2.tok])
                            b2 = self.psbank()
                            S.op("pe", lambda p: p.matmul(self.PS(b2), lhsT=onesb.t[:], rhs=q2.t[:], start=True, stop=True), reads=[onesb.tok, q2.tok], writes=[self.pst[b2]])
                            n_ = rn[0]
                            S.op("act", lambda a: a.activation(out=n_.t[:], in_=self.PS(b2), func=AF.Sqrt, bias=self.epsb.t[:]), reads=[self.pst[b2], self.epsb.tok], writes=[n_.tok])
                            S.op("dve", lambda v: v.reciprocal(out=n_.t[:], in_=n_.t[:]), reads=[n_.tok], writes=[n_.tok])
                            sc = (128 ** -0.5) if c < 4 else 1.0
                            S.op("dve", lambda v: v.scalar_tensor_tensor(out=o_.t[:], in0=s_.t[:], scalar=sc, in1=n_.t[:], op0=ALU.mult, op1=ALU.mult), reads=[s_.tok, n_.tok], writes=[o_.tok])
                            dst = (self.gq if c < 4 else self.gk)[hh][:, t0:t0 + 512]
                        else:
                            S.op("pool", lambda g: g.tensor_copy(out=o_.t[:], in_=s_.t[:]), reads=[s_.tok], writes=[o_.tok])
                            dst = self.gv[hh][:, t0:t0 + 512]
                        S.dma("pool", dst, o_.t[:], reads=[o_.tok], writes=[self.t_g], nowaw=True)
                    for (dstT, c0) in ((bT, 2048), (aT_, 2052)):
                        b = self.psbank()
                        for kc in range(8):
                            S.op("pe", lambda p: p.matmul(self.PS(b), lhsT=win.t[:, kc, c0:c0 + 128], rhs=h_T.t[:, kc, :], start=(kc == 0), stop=(kc == 7)),
                                 reads=[win.tok] + h_T.toks, writes=[self.pst[b]])
                        S.op("dve", lambda v: v.tensor_copy(out=dstT.t[:, t0:t0 + 512], in_=self.ps[0:4, b, :]), reads=[self.pst[b]], writes=[dstT.tok])
                    bq = self.psbank()
                    bk = self.psbank()
                    for which in range(2):
                        bstat = bq if which == 0 else bk
                        base = WQB if which == 0 else WKB
                        for h in range(4):
                            b1 = self.psbank()
                            while b1 in (bq, bk):
                                b1 = self.psbank()
                            b2 = self.psbank()
                            while b2 in (bq, bk):
                                b2 = self.psbank()
                            c1 = base + h * 128
                            c2 = WSW + which * 512 + h * 128
                            for (bb, cc) in ((b1, c1), (b2, c2)):
                                for kc in range(8):
                                    S.op("pe", lambda p: p.matmul(self.PS(bb), lhsT=win.t[:, kc, cc:cc + 128], rhs=h_T.t[:, kc, :], start=(kc == 0), stop=(kc == 7)),
                                         reads=[win.tok] + h_T.toks, writes=[self.pst[bb]])
                            a_ = t1[0]
                            b_ = t2[0]
                            o_ = qo[nq % 2]
                            s_ = sq[nq % 2]
                            nq += 1
                            sc = 0.125 if which == 0 else 1.0
                            S.op("dve", lambda v: v.scalar_tensor_tensor(out=a_.t[:], in0=self.PS(b1), scalar=sc, in1=ctab.t[:], op0=ALU.mult, op1=ALU.mult), reads=[self.pst[b1], ctab.tok], writes=[a_.tok])
                            S.op("dve", lambda v: v.scalar_tensor_tensor(out=b_.t[:], in0=self.PS(b2), scalar=sc, in1=stab.t[:], op0=ALU.mult, op1=ALU.mult), reads=[self.pst[b2], stab.tok], writes=[b_.tok])
                            S.op("pool", lambda g: g.tensor_tensor(out=o_.t[:], in0=a_.t[:], in1=b_.t[:], op=ALU.add), reads=[a_.tok, b_.tok], writes=[o_.tok])
                            for m in range(2):
                                mh = 2 * h + m
                                dst = (self.qTd if which == 0 else self.kTd)[mh][0:64, t0:t0 + 512]
                                S.dma("pool", dst, o_.t[m * 64:(m + 1) * 64, :], reads=[o_.tok], writes=[(self.t_qT if which == 0 else self.t_kT)[mh]])
                            S.op("dve", lambda v: v.tensor_tensor(out=s_.t[:], in0=o_.t[:], in1=o_.t[:], op=ALU.mult), reads=[o_.tok], writes=[s_.tok])
                            S.op("pe", lambda p: p.matmul(self.PS(bstat), lhsT=self.esel2b.t[:, h, :], rhs=s_.t[:], start=(h == 0), stop=(h == 3)),
                                 reads=[self.esel2b.tok, s_.tok], writes=[self.pst[bstat]])
                        dstat = (qsq if which == 0 else ksq)
                        S.op("dve", lambda v: v.tensor_copy(out=dstat.t[:, t0:t0 + 512], in_=self.ps[0:8, bstat, :]), reads=[self.pst[bstat]], writes=[dstat.tok])
                    for sub in range(4):
                        ti = blk * 4 + sub
                        v_ = vt[nv % 2]
                        g_ = gt[nv % 2]
                        nv += 1
                        b = self.psbank()
                        for kc in range(8):
                            S.op("pe", lambda p: p.matmul(self.PS(b), lhsT=h_T.t[:, kc, sub * 128:(sub + 1) * 128], rhs=win.t[:, kc, WVB:WVB + 512], start=(kc == 0), stop=(kc == 7)),
                                 reads=[win.tok, h_T.toks[sub]], writes=[self.pst[b]])
                        S.op("dve", lambda v: v.tensor_copy(out=v_.t[:, 0:4, 0:128], in_=self.PS(b).rearrange("p (h c) -> p h c", c=128)), reads=[self.pst[b]], writes=[v_.tok])
                        S.dma("pool", self.vaug[ti * 128:(ti + 1) * 128, :], v_.t[:].rearrange("p h c -> p (h c)"), reads=[v_.tok], writes=[self.t_vaug], nowaw=True)
                        b = self.psbank()
                        for kc in range(8):
                            S.op("pe", lambda p: p.matmul(self.PS(b), lhsT=h_T.t[:, kc, sub * 128:(sub + 1) * 128], rhs=win.t[:, kc, 1536:2048], start=(kc == 0), stop=(kc == 7)),
                                 reads=[win.tok, h_T.toks[sub]], writes=[self.pst[b]])
                        S.op("act", lambda a: a.activation(out=g_.t[:, 0:512], in_=self.PS(b), func=AF.Silu), reads=[self.pst[b]], writes=[g_.tok])
                        S.dma("pool", self.gbuf[ti * 128:(ti + 1) * 128, :], g_.t[:], reads=[g_.tok], writes=[self.t_gbuf[ti]])
            if STOP == "E1":
                self.stopped = True
            S.barrier()
            with ExitStack() as es:
                if self.stopped:
                    return
                km = self.sb(es, "km", [8, 1], F32)
                ya = self.sb(es, "ya", [8, T], F32)
                b1 = self.sb(es, "b1", [8, T], BF16)
                b2 = self.sb(es, "b2", [8, T], BF16)
                onesb = self.sb(es, "onesb2", [8, T], BF16)
                S.op("pool", lambda g: g.memset(onesb.t[:], 1.0), writes=[onesb.tok])
                S.op("dve", lambda v: v.tensor_reduce(out=km.t[:], in_=ksq.t[:], axis=AX.X, op=ALU.max), reads=[ksq.tok], writes=[km.tok])
                S.op("dve", lambda v: v.tensor_scalar(out=ya.t[:], in0=qsq.t[:], scalar1=64.0, scalar2=km.t[:], op0=ALU.mult, op1=ALU.add), reads=[qsq.tok, km.tok], writes=[ya.tok])
                S.op("dve", lambda v: v.tensor_scalar(out=ya.t[:], in0=ya.t[:], scalar1=-0.5 * 0.125, scalar2=None, op0=ALU.mult), reads=[ya.tok], writes=[ya.tok])
                S.op("dve", lambda v: v.tensor_copy(out=b1.t[:], in_=ya.t[:]), reads=[ya.tok], writes=[b1.tok])
                S.op("dve", lambda v: v.tensor_tensor(out=b2.t[:], in0=ya.t[:], in1=b1.t[:], op=ALU.subtract), reads=[ya.tok, b1.tok], writes=[b2.tok])
                S.dma("sp", self.qTd[:, 64, :], b1.t[:], reads=[b1.tok], writes=self.t_qT)
                S.dma("sp", self.qTd[:, 65, :], b2.t[:], reads=[b2.tok], writes=self.t_qT)
                S.dma("sp", self.kTd[:, 64, :], onesb.t[:], reads=[onesb.tok], writes=self.t_kT)
                S.dma("sp", self.kTd[:, 65, :], onesb.t[:], reads=[onesb.tok], writes=self.t_kT)
                S.barrier()
            esQ.close()
            if STOP == "E2a":
                self.stopped = True
                return
            self.gdn(l, bT, aT_)
        if STOP == "gdnB":
            self.stopped = True
        if self.stopped:
            return
        S.barrier()
        with ExitStack() as es:
            self.attention(es, nheads=8, kdim=64, bias_rows=2, vhead=lambda mh: mh // 2, dest=self.obuf2, dtoks=self.t_obuf2)
        tap = os.environ.get("K_TAP", "")
        if tap:
            S.barrier()
            src = self.obuf if tap == "o" else self.obuf2
            for i in range(0, T, 512):
                S.dma("sp", self.y[i:i + 512, :], src[i:i + 512, :], writes=[self.t_y])
            self.stopped = True

    def gdn(self, l, bT, aT_):
        nc, S, T, NT, NB = self.nc, self.S, self.T, self.NT, self.NB
        j = l // 2
        S.barrier()
        with ExitStack() as es:
            al = self.sb(es, "al", [4, 1], F32)
            dtb = self.sb(es, "dtb", [4, 1], F32)
            S.dma("sp", al.t[:], self.a_log[j].rearrange("(h o) -> h o", o=1), writes=[al.tok])
            S.dma("sp", dtb.t[:], self.dt_bias[j].rearrange("(h o) -> h o", o=1), writes=[dtb.tok])
            cols = self.sb(es, "gcols", [128, NT, 16], F32)
            with ExitStack() as esA:
                cm = self.sb(esA, "cm", [4, T], F32)
                S.dma("sp", cm.t[:], self.c_cmask, writes=[cm.tok])
                xa = aT_
                beta = bT
                ya = self.sb(esA, "gya", [4, T], F32)
                gc = self.sb(esA, "ggc", [4, T], F32)
                bec = self.sb(esA, "gbec", [4, T], F32)
                ekd = self.sb(esA, "gekd", [4, T], F32)
                egc = self.sb(esA, "gegc", [4, T], F32)
                S.op("act", lambda a: a.activation(out=beta.t[:], in_=bT.t[:], func=AF.Sigmoid), reads=[bT.tok], writes=[beta.tok])
                S.op("act", lambda a: a.activation(out=al.t[:], in_=al.t[:], func=AF.Exp), reads=[al.tok], writes=[al.tok])
                S.op("dve", lambda v: v.tensor_scalar(out=al.t[:], in0=al.t[:], scalar1=-1.0, scalar2=None, op0=ALU.mult), reads=[al.tok], writes=[al.tok])
                S.op("dve", lambda v: v.tensor_scalar(out=xa.t[:], in0=aT_.t[:], scalar1=dtb.t[:], scalar2=None, op0=ALU.add), reads=[aT_.tok, dtb.tok], writes=[xa.tok])
                S.op("dve", lambda v: v.scalar_tensor_tensor(out=ya.t[:], in0=xa.t[:], scalar=-1.0, in1=xa.t[:], op0=ALU.mult, op1=ALU.max), reads=[xa.tok], writes=[ya.tok])
                S.op("act", lambda a: a.activation(out=ya.t[:], in_=ya.t[:], func=AF.Exp, scale=-1.0), reads=[ya.tok], writes=[ya.tok])
                S.op("act", lambda a: a.activation(out=ya.t[:], in_=ya.t[:], func=AF.Ln, bias=1.0), reads=[ya.tok], writes=[ya.tok])
                S.op("dve", lambda v: v.scalar_tensor_tensor(out=xa.t[:], in0=xa.t[:], scalar=0.0, in1=ya.t[:], op0=ALU.max, op1=ALU.add), reads=[xa.tok, ya.tok], writes=[xa.tok])
                S.op("dve", lambda v: v.tensor_scalar(out=xa.t[:], in0=xa.t[:], scalar1=al.t[:], scalar2=None, op0=ALU.mult), reads=[xa.tok, al.tok], writes=[xa.tok])
                S.op("dve", lambda v: v.tensor_tensor_scan(out=gc.t[:], data0=cm.t[:], data1=xa.t[:], initial=0.0, op0=ALU.mult, op1=ALU.add), reads=[cm.tok, xa.tok], writes=[gc.tok])
                S.op("act", lambda a: a.activation(out=egc.t[:], in_=gc.t[:], func=AF.Exp), reads=[gc.tok], writes=[egc.tok])
                S.op("dve", lambda v: v.tensor_tensor(out=bec.t[:], in0=beta.t[:], in1=egc.t[:], op=ALU.mult), reads=[beta.tok, egc.tok], writes=[bec.tok])
                gcv = gc.t[:].rearrange("p (n c) -> p n c", c=128)
                S.op("dve", lambda v: v.tensor_tensor(out=ekd.t[:].rearrange("p (n c) -> p n c", c=128), in0=gcv[:, :, 127:128].to_broadcast([4, NT, 128]), in1=gcv, op=ALU.subtract),
                     reads=[gc.tok], writes=[ekd.tok])
                S.op("act", lambda a: a.activation(out=ekd.t[:], in_=ekd.t[:], func=AF.Exp), reads=[ekd.tok], writes=[ekd.tok])
                S.dma("sp", self.gcd, gc.t[:], reads=[gc.tok], writes=[self.t_gcd])
                S.dma("sp", self.egcd, egc.t[:], reads=[egc.tok], writes=[self.t_gcd])
                for ti in range(NT):
                    b = self.psbank()
                    for qi, src in enumerate((gc, beta, bec, ekd)):
                        S.op("pe", lambda p: p.transpose(out=self.ps[:, b, qi * 4:(qi + 1) * 4], in_=src.t[:, ti * 128:(ti + 1) * 128], identity=self.identf.t[0:4, 0:4]),
                             reads=[src.tok, self.identf.tok], writes=[self.pst[b]])
                    S.op("dve", lambda v: v.tensor_copy(out=cols.t[:, ti, :], in_=self.ps[:, b, 0:16]), reads=[self.pst[b]], writes=[cols.tok])
                S.barrier()
            if STOP == "gdnA":
                self.stopped = True
                return
            negup = self.sb(es, "gnegup", [128, 128], F32)
            poslow = self.sb(es, "gposlow", [128, 128], F32)
            S.dma("sp", negup.t[:], self.c_negup, writes=[negup.tok])
            S.dma("sp", poslow.t[:], self.c_poslow, writes=[poslow.tok])
            kT = self.sb(es, "gkT", [128, T], BF16)
            qT = self.sb(es, "gqT", [128, T], BF16)
            vT = self.sb(es, "gvT", [128, T], BF16)
            Rg = self.sb(es, "gRg", [128, T], F32)
            Re = self.sb(es, "gRe", [128, T], F32)
            qd = self.sb(es, "gqd", [128, T], BF16)
            QK = self.sb(es, "gQK", [128, NT, 128], BF16)
            U = self.sb(es, "gU", [128, NT, 128], F32)
            WT = self.sb(es, "gWT", [128, T], BF16)
            KD = self.sb(es, "gKD", [128, NT, 128], BF16)
            egl = self.sb(es, "gegl", [128, NT], F32)
            G = 4
            kbe = self.ring(es, "gkbe", [128, 128], BF16, G)
            vb = self.ring(es, "gvb", [128, 128], BF16, G)
            e1 = self.ring(es, "ge1", [128, 128], F32, G)
            e2 = self.ring(es, "ge2", [128, 128], F32, G)
            Pm = [self.ring(es, f"gP{i}", [128, 128], BF16, G) for i in range(2)]
            PTm = [self.ring(es, f"gPT{i}", [128, 128], BF16, G) for i in range(2)]
            TTm = [self.ring(es, f"gTT{i}", [128, 128], F32, G) for i in range(2)]
            TTs = [self.ring(es, f"gTTs{i}", [128, 128], BF16, G) for i in range(2)]
            Pl = [self.ring(es, f"gPl{i}", [128, 128], BF16, G) for i in range(2)]
            PTl_ = [self.ring(es, f"gPTl{i}", [128, 128], BF16, G) for i in range(2)]
            TSl_ = [self.ring(es, f"gTSl{i}", [128, 128], BF16, G) for i in range(2)]
            xs_ = self.ring(es, "gxs", [128, 128], F32, G)
            xs2_ = self.ring(es, "gxs2", [128, 128], F32, G)

            def split(dh, dl, src):
                S.op("pool", lambda g: g.tensor_copy(out=dh.t[:], in_=src.t[:]), reads=[src.tok], writes=[dh.tok])
                S.op("dve", lambda v: v.tensor_tensor(out=dl.t[:], in0=src.t[:], in1=dh.t[:], op=ALU.subtract), reads=[src.tok, dh.tok], writes=[dl.tok])

            def mm3(bank, *pairs):
                n = len(pairs)
                for i_, (a_, b_) in enumerate(pairs):
                    S.op("pe", lambda p: p.matmul(self.ps[:, bank, 0:128], lhsT=a_.t[:], rhs=b_.t[:], start=(i_ == 0), stop=(i_ == n - 1)),
                         reads=[a_.tok, b_.tok], writes=[self.pst[bank]])
            TTb = self.ring(es, "gTTb", [128, 128], BF16, G)
            Sst = self.sb(es, "gS", [128, 128], F32)
            Sb = self.sb(es, "gSb", [128, 128], BF16)
            Sl = self.sb(es, "gSl", [128, 128], BF16)
            vn = self.ring(es, "gvn", [128, 128], BF16, 2)
            og = self.ring(es, "gog", [128, 4, 128], F32, 2)
            osrc = self.obuf.rearrange("(n p) c -> p n c", p=128)
            for h in range(int(os.environ.get("K_H0", "0")), int(os.environ.get("K_H1", "4"))):
                S.barrier()
                S.dma("sp", kT.t[:], self.gk[h], reads=[self.t_g], writes=[kT.tok])
                S.dma("sp", qT.t[:], self.gq[h], reads=[self.t_g], writes=[qT.tok])
                S.dma("sp", vT.t[:], self.gv[h], reads=[self.t_g], writes=[vT.tok])
                for c0 in range(0, T, 1024):
                    c1 = min(T, c0 + 1024)
                    S.dma("sp", Rg.t[:, c0:c1], self.gcd[h][c0:c1].partition_broadcast(128), reads=[self.t_gcd], writes=[Rg.tok])
                    S.dma("sp", Re.t[:, c0:c1], self.egcd[h][c0:c1].partition_broadcast(128), reads=[self.t_gcd], writes=[Re.tok])
                S.op("pool", lambda g: g.tensor_tensor(out=qd.t[:], in0=qT.t[:], in1=Re.t[:], op=ALU.mult), reads=[qT.tok, Re.tok], writes=[qd.tok])
                S.op("act", lambda a: a.activation(out=egl.t[:], in_=Rg.t[:].rearrange("p (n c) -> p n c", c=128)[:, :, 127], func=AF.Exp), reads=[Rg.tok], writes=[egl.tok])
                for g0 in range(0, NT, G):
                    tiles = list(range(g0, min(NT, g0 + G)))
                    st = {}
                    for ti in tiles:
                        s = ti % G
                        tsl = slice(ti * 128, (ti + 1) * 128)
                        cgc = cols.t[:, ti, 0 + h:1 + h]
                        cbeta = cols.t[:, ti, 4 + h:5 + h]
                        cbec = cols.t[:, ti, 8 + h:9 + h]
                        cekd = cols.t[:, ti, 12 + h:13 + h]
                        b = self.psbank()
                        pv = self.PSB(b)
                        S.op("pe", lambda p: p.transpose(out=pv[:, 0, :], in_=kT.t[:, tsl], identity=self.identb.t[:]), reads=[kT.tok, self.identb.tok], writes=[self.pst[b]])
                        S.op("pe", lambda p: p.transpose(out=pv[:, 1, :], in_=vT.t[:, tsl], identity=self.identb.t[:]), reads=[vT.tok, self.identb.tok], writes=[self.pst[b]])
                        S.op("dve", lambda v: v.tensor_scalar(out=kbe[s].t[:], in0=pv[:, 0, :], scalar1=cbec, scalar2=None, op0=ALU.mult), reads=[self.pst[b], cols.tok], writes=[kbe[s].tok])
                        S.op("dve", lambda v: v.tensor_scalar(out=KD.t[:, ti, :], in0=pv[:, 0, :], scalar1=cekd, scalar2=None, op0=ALU.mult), reads=[self.pst[b], cols.tok], writes=[KD.tok])
                        S.op("dve", lambda v: v.tensor_scalar(out=vb[s].t[:], in0=pv[:, 1, :], scalar1=cbeta, scalar2=None, op0=ALU.mult), reads=[self.pst[b], cols.tok], writes=[vb[s].tok])
                        S.op("dve", lambda v: v.scalar_tensor_tensor(out=e1[s].t[:], in0=Rg.t[:, tsl], scalar=cgc, in1=negup.t[:], op0=ALU.subtract, op1=ALU.add), reads=[Rg.tok, cols.tok, negup.tok], writes=[e1[s].tok])
                        S.op("act", lambda a: a.activation(out=e1[s].t[:], in_=e1[s].t[:], func=AF.Exp), reads=[e1[s].tok], writes=[e1[s].tok])
                        S.op("dve", lambda v: v.scalar_tensor_tensor(out=e2[s].t[:], in0=Rg.t[:, tsl], scalar=cgc, in1=poslow.t[:], op0=ALU.subtract, op1=ALU.add), reads=[Rg.tok, cols.tok, poslow.tok], writes=[e2[s].tok])
                        S.op("act", lambda a: a.activation(out=e2[s].t[:], in_=e2[s].t[:], func=AF.Exp, scale=-1.0), reads=[e2[s].tok], writes=[e2[s].tok])
                        bkk = self.psbank()
                        S.op("pe", lambda p: p.matmul(self.ps[:, bkk, 0:128], lhsT=kT.t[:, tsl], rhs=kT.t[:, tsl], start=True, stop=True), reads=[kT.tok], writes=[self.pst[bkk]])
                        S.op("pe", lambda p: p.matmul(self.ps[:, bkk, 128:256], lhsT=kT.t[:, tsl], rhs=qT.t[:, tsl], start=False, stop=True, skip_group_check=True), reads=[kT.tok, qT.tok], writes=[self.pst[bkk]])
                        Lf = e2[s]
                        S.op("dve", lambda v: v.scalar_tensor_tensor(out=Lf.t[:], in0=self.ps[:, bkk, 0:128], scalar=cbeta, in1=e2[s].t[:], op0=ALU.mult, op1=ALU.mult), reads=[self.pst[bkk], cols.tok, e2[s].tok], writes=[Lf.tok])
                        S.op("dve", lambda v: v.tensor_tensor(out=QK.t[:, ti, :], in0=self.ps[:, bkk, 128:256], in1=e1[s].t[:], op=ALU.mult), reads=[self.pst[bkk], e1[s].tok], writes=[QK.tok])
                        Ph, Pl_, PTh, PTl, TT0, TSh, TSl = Pm[0][s], Pl[0][s], PTm[0][s], PTl_[0][s], TTm[0][s], TTs[0][s], TSl_[0][s]
                        split(Ph, Pl_, Lf)
                        bt = self.psbank()
                        pvt = self.PSB(bt)
                        S.op("pe", lambda p: p.transpose(out=pvt[:, 0, :], in_=Ph.t[:], identity=self.identb.t[:]), reads=[Ph.tok, self.identb.tok], writes=[self.pst[bt]])
                        S.op("pe", lambda p: p.transpose(out=pvt[:, 1, :], in_=Pl_.t[:], identity=self.identb.t[:]), reads=[Pl_.tok, self.identb.tok], writes=[self.pst[bt]])
                        S.op("dve", lambda v: v.tensor_copy(out=PTh.t[:], in_=pvt[:, 0, :]), reads=[self.pst[bt]], writes=[PTh.tok])
                        S.op("dve", lambda v: v.tensor_copy(out=PTl.t[:], in_=pvt[:, 1, :]), reads=[self.pst[bt]], writes=[PTl.tok])
                        S.op("dve", lambda v: v.scalar_tensor_tensor(out=TT0.t[:], in0=PTh.t[:], scalar=-1.0, in1=self.identf.t[:], op0=ALU.mult, op1=ALU.add), reads=[PTh.tok, self.identf.tok], writes=[TT0.tok])
                        S.op("pool", lambda g: g.tensor_tensor(out=TT0.t[:], in0=TT0.t[:], in1=PTl.t[:], op=ALU.subtract), reads=[TT0.tok, PTl.tok], writes=[TT0.tok])
                        split(TSh, TSl, TT0)
                        st[ti] = 0
                    if STOP == "gB1":
                        self.stopped = True
                        return
                    NST = 6
                    for stage in range(NST):
                        lastst = (stage == NST - 1)
                        for ti in tiles:
                            s = ti % G
                            cur = st[ti]
                            nxt = 1 - cur
                            Ph, Pl_, PTh, PTl, TTc, TSh, TSl = Pm[cur][s], Pl[cur][s], PTm[cur][s], PTl_[cur][s], TTm[cur][s], TTs[cur][s], TSl_[cur][s]
                            Pnh, Pnl, PTnh, PTnl, TTn, TSnh, TSnl = Pm[nxt][s], Pl[nxt][s], PTm[nxt][s], PTl_[nxt][s], TTm[nxt][s], TTs[nxt][s], TSl_[nxt][s]
                            X = xs_[s]
                            b = self.psbank()
                            mm3(b, (PTh, Ph), (PTh, Pl_), (PTl, Ph))
                            S.op("dve", lambda v: v.tensor_copy(out=X.t[:], in_=self.ps[:, b, 0:128]), reads=[self.pst[b]], writes=[X.tok])
                            split(Pnh, Pnl, X)
                            if not lastst:
                                b3 = self.psbank()
                                mm3(b3, (Ph, PTh), (Ph, PTl), (Pl_, PTh))
                                X2 = xs2_[s]
                                S.op("dve", lambda v: v.tensor_copy(out=X2.t[:], in_=self.ps[:, b3, 0:128]), reads=[self.pst[b3]], writes=[X2.tok])
                                split(PTnh, PTnl, X2)
                            b2 = self.psbank()
                            mm3(b2, (Pnh, TSh), (Pnh, TSl), (Pnl, TSh))
                            S.op("dve", lambda v: v.tensor_tensor(out=TTn.t[:], in0=self.ps[:, b2, 0:128], in1=TTc.t[:], op=ALU.add), reads=[self.pst[b2], TTc.tok], writes=[TTn.tok])
                            split(TSnh, TSnl, TTn)
                            st[ti] = nxt
                    if STOP == "gB2":
                        self.stopped = True
                        return
                    for ti in tiles:
                        s = ti % G
                        tsl = slice(ti * 128, (ti + 1) * 128)
                        TSh, TSl = TTs[st[ti]][s], TSl_[st[ti]][s]
                        b = self.psbank()
                        S.op("pe", lambda p: p.matmul(self.ps[:, b, 0:128], lhsT=TSh.t[:], rhs=vb[s].t[:], start=True, stop=False), reads=[TSh.tok, vb[s].tok], writes=[self.pst[b]])
                        S.op("pe", lambda p: p.matmul(self.ps[:, b, 0:128], lhsT=TSl.t[:], rhs=vb[s].t[:], start=False, stop=True), reads=[TSl.tok, vb[s].tok], writes=[self.pst[b]])
                        bw = self.psbank()
                        S.op("pe", lambda p: p.matmul(self.ps[:, bw, 0:128], lhsT=kbe[s].t[:], rhs=TSh.t[:], start=True, stop=False), reads=[TSh.tok, kbe[s].tok], writes=[self.pst[bw]])
                        S.op("pe", lambda p: p.matmul(self.ps[:, bw, 0:128], lhsT=kbe[s].t[:], rhs=TSl.t[:], start=False, stop=True), reads=[TSl.tok, kbe[s].tok], writes=[self.pst[bw]])
                        S.op("dve", lambda v: v.tensor_copy(out=U.t[:, ti, :], in_=self.ps[:, b, 0:128]), reads=[self.pst[b]], writes=[U.tok])
                        S.op("dve", lambda v: v.tensor_copy(out=WT.t[:, tsl], in_=self.ps[:, bw, 0:128]), reads=[self.pst[bw]], writes=[WT.tok])
                if STOP == "gB3":
                    self.stopped = True
                    return
                S.op("dve", lambda v: v.memset(Sst.t[:], 0.0), writes=[Sst.tok])
                S.op("pool", lambda g: g.memset(Sb.t[:], 0.0), writes=[Sb.tok])
                S.op("pool", lambda g: g.memset(Sl.t[:], 0.0), writes=[Sl.tok])
                for ti in range(NT):
                    tsl = slice(ti * 128, (ti + 1) * 128)
                    ba = self.psbank()
                    bo = self.psbank()
                    S.op("pe", lambda p: p.matmul(self.ps[:, ba, 0:128], lhsT=WT.t[:, tsl], rhs=Sb.t[:], start=True, stop=False), reads=[WT.tok, Sb.tok], writes=[self.pst[ba]])
                    S.op("pe", lambda p: p.matmul(self.ps[:, ba, 0:128], lhsT=WT.t[:, tsl], rhs=Sl.t[:], start=False, stop=True), reads=[WT.tok, Sl.tok], writes=[self.pst[ba]])
                    S.op("pe", lambda p: p.matmul(self.ps[:, bo, 0:128], lhsT=qd.t[:, tsl], rhs=Sb.t[:], start=True, stop=False), reads=[qd.tok, Sb.tok], writes=[self.pst[bo]])
                    S.op("pe", lambda p: p.matmul(self.ps[:, bo, 0:128], lhsT=qd.t[:, tsl], rhs=Sl.t[:], start=False, stop=False), reads=[qd.tok, Sl.tok], writes=[self.pst[bo]])
                    v_ = vn[ti % 2]
                    S.op("dve", lambda v: v.tensor_tensor(out=v_.t[:], in0=U.t[:, ti, :], in1=self.ps[:, ba, 0:128], op=ALU.subtract), reads=[U.tok, self.pst[ba]], writes=[v_.tok])
                    S.op("pe", lambda p: p.matmul(self.ps[:, bo, 0:128], lhsT=QK.t[:, ti, :], rhs=v_.t[:], start=False, stop=True), reads=[QK.tok, v_.tok], writes=[self.pst[bo]])
                    bd = self.psbank()
                    S.op("pe", lambda p: p.matmul(self.ps[:, bd, 0:128], lhsT=KD.t[:, ti, :], rhs=v_.t[:], start=True, stop=True), reads=[KD.tok, v_.tok], writes=[self.pst[bd]])
                    S.op("dve", lambda v: v.tensor_scalar(out=Sst.t[:], in0=Sst.t[:], scalar1=egl.t[:, ti:ti + 1], scalar2=None, op0=ALU.mult), reads=[Sst.tok, egl.tok], writes=[Sst.tok])
                    S.op("dve", lambda v: v.tensor_tensor(out=Sst.t[:], in0=self.ps[:, bd, 0:128], in1=Sst.t[:], op=ALU.add), reads=[Sst.tok, self.pst[bd]], writes=[Sst.tok])
                    S.op("dve", lambda v: v.tensor_copy(out=Sb.t[:], in_=Sst.t[:]), reads=[Sst.tok], writes=[Sb.tok])
                    S.op("dve", lambda v: v.tensor_tensor(out=Sl.t[:], in0=Sst.t[:], in1=Sb.t[:], op=ALU.subtract), reads=[Sst.tok, Sb.tok], writes=[Sl.tok])
                    o_ = og[(ti // 4) % 2]
                    S.op("dve", lambda v: v.tensor_copy(out=o_.t[:, ti % 4, :], in_=self.ps[:, bo, 0:128]), reads=[self.pst[bo]], writes=[o_.tok])
                    if ti % 4 == 3 and os.environ.get("K_DBG") != "nodma":
                        S.dma("pool", osrc[:, ti - 3:ti + 1, h * 128:(h + 1) * 128], o_.t[:], reads=[o_.tok], writes=self.t_obuf[ti - 3:ti + 1])
                    if os.environ.get("K_SCAN") and ti + 1 >= int(os.environ["K_SCAN"]):
                        self.stopped = True
                        return

    def even_gate(self, l, o_, o2_, g_, m_, wn, nlam, tmp, ss8):
        S = self.S
        o2v = o2_.t[:].rearrange("p (h m c) -> p h m c", m=2, c=128)
        S.op("dve", lambda v: v.scalar_tensor_tensor(out=o_.t[:, 512:1024].rearrange("p (h c) -> p h c", c=128), in0=o2v[:, :, 1, :], scalar=nlam.t[:], in1=o2v[:, :, 0, :], op0=ALU.mult, op1=ALU.add),
             reads=[o2_.tok, nlam.tok, o_.tok], writes=[o_.tok])
        S.op("pool", lambda g: g.tensor_tensor(out=tmp.t[:], in0=o_.t[:], in1=o_.t[:], op=ALU.mult), reads=[o_.tok], writes=[tmp.tok])
        S.op("dve", lambda v: v.tensor_reduce(out=ss8.t[:], in_=tmp.t[:].rearrange("p (g c) -> p g c", c=128), axis=AX.X, op=ALU.add), reads=[tmp.tok], writes=[ss8.tok])
        S.op("act", lambda a: a.activation(out=ss8.t[:], in_=ss8.t[:], func=AF.Sqrt, scale=1.0 / 128, bias=self.epsb.t[:]), reads=[ss8.tok, self.epsb.tok], writes=[ss8.tok])
        S.op("dve", lambda v: v.reciprocal(out=ss8.t[:], in_=ss8.t[:]), reads=[ss8.tok], writes=[ss8.tok])
        S.op("dve", lambda v: v.tensor_tensor(out=tmp.t[:].rearrange("p (g c) -> p g c", c=128), in0=o_.t[:].rearrange("p (g c) -> p g c", c=128), in1=ss8.t[:].unsqueeze(2).to_broadcast([128, 8, 128]), op=ALU.mult),
             reads=[o_.tok, ss8.tok, tmp.tok], writes=[tmp.tok])
        S.op("pool", lambda g: g.tensor_tensor(out=tmp.t[:], in0=tmp.t[:], in1=wn.t[:], op=ALU.mult), reads=[tmp.tok, wn.tok], writes=[tmp.tok])
        S.op("dve", lambda v: v.tensor_tensor(out=m_.t[:], in0=tmp.t[:], in1=g_.t[:], op=ALU.mult), reads=[tmp.tok, g_.tok], writes=[m_.tok])

    def tail(self, l, last):
        nc, S, T, NT, NB = self.nc, self.S, self.T, self.NT, self.NB
        odd = (l % 2 == 1)
        S.barrier()
        with ExitStack() as es:
            wring = self.ring(es, "wr", [128, 8, 512], BF16, 5)
            nwr = [0]

            def wload(src_ap, tok):
                w_ = wring[nwr[0] % 5]
                nwr[0] += 1
                S.dma("sp", w_.t[:], src_ap, reads=[tok], writes=[w_.tok])
                return w_

            wout = self.wb[("out", l)][0].rearrange("(c p) n -> p c n", p=128)
            wup = self.wb[("up", l)][0].rearrange("(c p) n -> p c n", p=128)
            wdn = self.wb[("down", l)][0].rearrange("(c p) n -> p c n", p=128)
            wgt = self.wb[("gate", l)][0].rearrange("(c p) n -> p c n", p=128)
            wple = self.sb(es, "wple", [128, 2, D], BF16)
            S.dma("sp", wple.t[:], self.wb[("ple", l)][0].rearrange("(c p) n -> p c n", p=128), reads=[self.wtok[("ple", l)]], writes=[wple.tok])
            nwm = self.sb(es, "nwm", [128, D], F32)
            S.dma("sp", nwm.t[:], self.norm_mlp[l].partition_broadcast(128), writes=[nwm.tok])
            if last and self.final_norm:
                nwf = self.sb(es, "nwf", [128, D], F32)
                S.dma("sp", nwf.t[:], self.norm_final[0].partition_broadcast(128), writes=[nwf.tok])
            xr = self.ring(es, "txr", [128, D], F32, 8)
            orr = self.ring(es, "tor", [128, D], F32, 2)
            gr = self.ring(es, "tgr", [128, D], BF16, 2)
            mb = self.ring(es, "tmb", [128, D], BF16, 3)
            TT = self.ring(es, "tTT", [128, 8, 512], BF16, 2, n=4)
            aT = self.sb(es, "taT", [128, 32, 512], BF16)
            rr = self.ring(es, "trr", [128, 512], F32, 3)
            pTt = self.ring(es, "tpT", [128, 2, 512], BF16, 2)
            junk = self.sb(es, "tjunk", [128, D], BF16)
            ss = self.ring(es, "tss", [128, 1], F32, 2)
            sd = self.ring(es, "tsd", [128, 1], F32, 2)
            if last and self.final_norm:
                yo = self.ring(es, "tyo", [128, D], F32, 2)
            if not odd:
                jj_ = l // 2
                lam_init = 0.8 - 0.6 * math.exp(-0.3 * l)
                o2r = self.ring(es, "to2", [128, D], F32, 2)
                tmpb = self.sb(es, "ttmp", [128, D], F32)
                ss8 = self.ring(es, "tss8", [128, 8], F32, 2)
                wn = self.sb(es, "twn", [128, D], F32)
                for gi in range(4):
                    S.dma("sp", wn.t[:, gi * 128:(gi + 1) * 128], self.gdn_norm[jj_].partition_broadcast(128), writes=[wn.tok])
                    S.dma("sp", wn.t[:, 512 + gi * 128:512 + (gi + 1) * 128], self.diff_norm[jj_].partition_broadcast(128), writes=[wn.tok])
                S.op("dve", lambda v: v.tensor_scalar(out=wn.t[:, 512:1024], in0=wn.t[:, 512:1024], scalar1=1.0 - lam_init, scalar2=None, op0=ALU.mult), reads=[wn.tok], writes=[wn.tok])
                lt = [self.sb(es, f"tlam{i}", [128, 64], F32) for i in range(4)]
                for i in range(4):
                    S.dma("sp", lt[i].t[:], self.lam[i][jj_].partition_broadcast(128), writes=[lt[i].tok])
                ls = self.sb(es, "tls", [128, 2], F32)
                nlam = self.sb(es, "tnlam", [128, 1], F32)
                for i in range(2):
                    S.op("dve", lambda v: v.tensor_tensor(out=lt[2 * i].t[:], in0=lt[2 * i].t[:], in1=lt[2 * i + 1].t[:], op=ALU.mult), reads=[lt[2 * i].tok, lt[2 * i + 1].tok], writes=[lt[2 * i].tok])
                    S.op("dve", lambda v: v.tensor_reduce(out=ls.t[:, i:i + 1], in_=lt[2 * i].t[:], axis=AX.X, op=ALU.add), reads=[lt[2 * i].tok, ls.tok], writes=[ls.tok])
                S.op("act", lambda a: a.activation(out=ls.t[:], in_=ls.t[:], func=AF.Exp), reads=[ls.tok], writes=[ls.tok])
                S.op("dve", lambda v: v.tensor_tensor(out=nlam.t[:], in0=ls.t[:, 1:2], in1=ls.t[:, 0:1], op=ALU.subtract), reads=[ls.tok], writes=[nlam.tok])
                S.op("dve", lambda v: v.tensor_scalar(out=nlam.t[:], in0=nlam.t[:], scalar1=-lam_init, scalar2=None, op0=ALU.add), reads=[nlam.tok], writes=[nlam.tok])
            nT = 0
            nm = 0
            nr = 0
            for blk in range(NB):
                xs = [xr[(blk % 2) * 4 + s] for s in range(4)]
                p_ = pTt[blk % 2]
                S.dma("pool", p_.t[:], self.pT[l].rearrange("(c p) t -> p c t", p=128)[:, :, blk * 512:(blk + 1) * 512], writes=[p_.tok])
                mT = TT[nT % 2]
                nT += 1
                for sub in range(4):
                    ti = blk * 4 + sub
                    x_ = xs[sub]
                    S.dma("sp", x_.t[:], self.xres[ti * 128:(ti + 1) * 128, :], reads=[self.t_xres[ti]], writes=[x_.tok])
                    o_ = orr[ti % 2]
                    g_ = gr[ti % 2]
                    m_ = mb[nm % 3]
                    nm += 1
                    S.dma("sp", o_.t[:], self.obuf[ti * 128:(ti + 1) * 128, :], reads=[self.t_obuf[ti]], writes=[o_.tok])
                    S.dma("sp", g_.t[:], self.gbuf[ti * 128:(ti + 1) * 128, :], reads=[self.t_gbuf[ti]], writes=[g_.tok])
                    if odd:
                        S.op("pool", lambda g: g.tensor_tensor(out=m_.t[:], in0=o_.t[:], in1=g_.t[:], op=ALU.mult), reads=[o_.tok, g_.tok], writes=[m_.tok])
                    else:
                        o2_ = o2r[ti % 2]
                        S.dma("sp", o2_.t[:], self.obuf2[ti * 128:(ti + 1) * 128, :], reads=[self.t_obuf2[ti]], writes=[o2_.tok])
                        self.even_gate(l, o_, o2_, g_, m_, wn, nlam, tmpb, ss8[ti % 2])
                    self.transpose8(m_.t[:], [m_.tok], mT.t[:, :, sub * 128:(sub + 1) * 128], [mT.toks[sub]], evac=("dve" if sub % 2 == 0 else "act"))
                for nh in range(2):
                    w_ = wload(wout[:, :, nh * 512:(nh + 1) * 512], self.wtok[("out", l)])
                    for sub in range(4):
                        b = self.psbank()
                        for kc in range(8):
                            S.op("pe", lambda p: p.matmul(self.PS(b), lhsT=mT.t[:, kc, sub * 128:(sub + 1) * 128], rhs=w_.t[:, kc, :], start=(kc == 0), stop=(kc == 7)),
                                 reads=[mT.toks[sub], w_.tok], writes=[self.pst[b]])
                        x_ = xs[sub]
                        S.op("dve", lambda v: v.tensor_tensor(out=x_.t[:, nh * 512:(nh + 1) * 512], in0=self.PS(b), in1=x_.t[:, nh * 512:(nh + 1) * 512], op=ALU.add),
                             reads=[self.pst[b], x_.tok], writes=[x_.tok])
                hT = TT[nT % 2]
                nT += 1
                for sub in range(4):
                    m_ = mb[nm % 3]
                    nm += 1
                    self.rmsnorm_h(xs[sub].t[:], xs[sub].tok, nwm, m_, ss[sub % 2], sd[sub % 2], junk)
                    self.transpose8(m_.t[:], [m_.tok], hT.t[:, :, sub * 128:(sub + 1) * 128], [hT.toks[sub]], evac=("dve" if sub % 2 == 0 else "act"))
                for g in range(8):
                    w_ = wload(wup[:, :, g * 512:(g + 1) * 512], self.wtok[("up", l)])
                    for jj in range(4):
                        fc = g * 4 + jj
                        b = self.psbank()
                        for kc in range(8):
                            S.op("pe", lambda p: p.matmul(self.PS(b), lhsT=w_.t[:, kc, jj * 128:(jj + 1) * 128], rhs=hT.t[:, kc, :], start=(kc == 0), stop=(kc == 7)),
                                 reads=[w_.tok] + hT.toks, writes=[self.pst[b]])
                        r_ = rr[nr % 3]
                        nr += 1
                        S.op("act", lambda a: a.activation(out=r_.t[:], in_=self.PS(b), func=AF.Relu), reads=[self.pst[b]], writes=[r_.tok])
                        S.op("dve", lambda v: v.tensor_tensor(out=aT.t[:, fc, :], in0=r_.t[:], in1=self.PS(b), op=ALU.mult),
                             reads=[r_.tok, self.pst[b]], writes=[aT.tok])
                for nh in range(2):
                    accb = [self.psbank() for _ in range(4)]
                    for fg in range(4):
                        w_ = wload(wdn[:, fg * 8:(fg + 1) * 8, nh * 512:(nh + 1) * 512], self.wtok[("down", l)])
                        for jj in range(8):
                            fc = fg * 8 + jj
                            for sub in range(4):
                                b = accb[sub]
                                S.op("pe", lambda p: p.matmul(self.PS(b), lhsT=aT.t[:, fc, sub * 128:(sub + 1) * 128], rhs=w_.t[:, jj, :], start=(fc == 0), stop=(fc == 31)),
                                     reads=[aT.tok, w_.tok], writes=[self.pst[b]])
                    for sub in range(4):
                        x_ = xs[sub]
                        b = accb[sub]
                        S.op("dve", lambda v: v.tensor_tensor(out=x_.t[:, nh * 512:(nh + 1) * 512], in0=self.PS(b), in1=x_.t[:, nh * 512:(nh + 1) * 512], op=ALU.add),
                             reads=[self.pst[b], x_.tok], writes=[x_.tok])
                xT = TT[nT % 2]
                nT += 1
                for sub in range(4):
                    m_ = mb[nm % 3]
                    nm += 1
                    S.op("pool", lambda g: g.tensor_copy(out=m_.t[:], in_=xs[sub].t[:]), reads=[xs[sub].tok], writes=[m_.tok])
                    self.transpose8(m_.t[:], [m_.tok], xT.t[:, :, sub * 128:(sub + 1) * 128], [xT.toks[sub]], evac=("dve" if sub % 2 == 0 else "act"))
                for nh in range(2):
                    w_ = wload(wgt[:, :, nh * 512:(nh + 1) * 512], self.wtok[("gate", l)])
                    for sub in range(4):
                        bg = self.psbank()
                        for kc in range(8):
                            S.op("pe", lambda p: p.matmul(self.PS(bg), lhsT=xT.t[:, kc, sub * 128:(sub + 1) * 128], rhs=w_.t[:, kc, :], start=(kc == 0), stop=(kc == 7)),
                                 reads=[xT.toks[sub], w_.tok], writes=[self.pst[bg]])
                        bp = self.psbank()
                        for kc in range(2):
                            S.op("pe", lambda p: p.matmul(self.PS(bp), lhsT=p_.t[:, kc, sub * 128:(sub + 1) * 128], rhs=wple.t[:, kc, nh * 512:(nh + 1) * 512], start=(kc == 0), stop=(kc == 1)),
                                 reads=[p_.tok, wple.tok], writes=[self.pst[bp]])
                        r_ = rr[nr % 3]
                        nr += 1
                        S.op("act", lambda a: a.activation(out=r_.t[:], in_=self.PS(bg), func=AF.Sigmoid), reads=[self.pst[bg]], writes=[r_.tok])
                        S.op("dve", lambda v: v.tensor_tensor(out=r_.t[:], in0=r_.t[:], in1=self.PS(bp), op=ALU.mult), reads=[r_.tok, self.pst[bp]], writes=[r_.tok])
                        x_ = xs[sub]
                        S.op("pool", lambda g: g.tensor_tensor(out=x_.t[:, nh * 512:(nh + 1) * 512], in0=x_.t[:, nh * 512:(nh + 1) * 512], in1=r_.t[:], op=ALU.add),
                             reads=[r_.tok, x_.tok], writes=[x_.tok])
                for sub in range(4):
                    ti = blk * 4 + sub
                    x_ = xs[sub]
                    if last:
                        if self.final_norm:
                            y_ = yo[sub % 2]
                            self.rmsnorm_h(x_.t[:], x_.tok, nwf, y_, ss[sub % 2], sd[sub % 2], junk)
                            S.dma("pool", self.y[ti * 128:(ti + 1) * 128, :], y_.t[:], reads=[y_.tok], writes=[self.t_y])
                        else:
                            S.dma("pool", self.y[ti * 128:(ti + 1) * 128, :], x_.t[:], reads=[x_.tok], writes=[self.t_y])
                    else:
                        S.dma("pool", self.xres[ti * 128:(ti + 1) * 128, :], x_.t[:], reads=[x_.tok], writes=[self.t_xres[ti]])


_CACHE = {}


def _get_nc(T, layers, final_norm=True):
    key = (T, tuple(layers), final_norm)
    if key not in _CACHE:
        b = Builder(T, list(layers), final_norm)
        nc = b.build()
        print(f"[kernel] built T={T} layers={layers}: {b.S.ninst} instr, {b.S.nwaits} waits; per-engine "
              + str({n: e["cnt"] for n, e in b.S.eng.items()}), flush=True)
        _CACHE[key] = nc
    return _CACHE[key]


def run(inputs, T, layers, ncores, final_norm=True, x_override=None):
    layers = list(layers)
    nc = _get_nc(T, layers, final_norm)
    consts = make_consts(T)
    f32 = lambda a: np.ascontiguousarray(np.asarray(a), dtype=np.float32)
    mall, ev, mev, od, mod = layer_maps(layers)
    shared = {}
    for k in ("norm_mix", "norm_mlp", "w_mlp_up", "w_mlp_down", "w_ple_proj", "w_ple_gate"):
        shared[k] = f32(np.asarray(inputs[k])[layers])
    for k in ("a_log", "dt_bias", "gdn_norm", "lam_q1", "lam_k1", "lam_q2", "lam_k2", "diff_norm", "w_out_even"):
        shared[k] = f32(np.asarray(inputs[k])[ev])
    for k in ("w_in_odd", "b_forget", "w_out_odd"):
        shared[k] = f32(np.asarray(inputs[k])[od])
    shared["norm_final"] = f32(inputs["norm_final"]).reshape(1, D)
    wie = f32(np.asarray(inputs["w_in_even"])[ev])
    idx = []
    for which in range(2):
        for h in range(4):
            for m in range(2):
                for d in range(64):
                    idx.append(2056 + which * 512 + h * 128 + m * 64 + (d + 32) % 64)
    shared["w_in_even"] = np.ascontiguousarray(np.concatenate([wie, wie[:, :, idx]], axis=2))
    cw = f32(np.asarray(inputs["conv_w"])[ev])
    shared["convw"] = np.ascontiguousarray(cw.reshape(len(ev), 4, 12, 128).transpose(0, 3, 2, 1))
    shared.update(consts)
    x = np.asarray(inputs["x"]) if x_override is None else x_override
    p = np.asarray(inputs["p"])
    pos = np.asarray(inputs["positions"])
    in_maps = []
    for b in range(ncores):
        m = dict(shared)
        m["x"] = f32(x[b, :T])
        m["pT"] = f32(np.transpose(p[layers, b, :T, :], (0, 2, 1)))
        m["pos"] = np.ascontiguousarray(pos[b, :T].reshape(1, T).astype(np.int32))
        in_maps.append(m)
    res = run_bass_kernel_spmd(nc, in_maps, core_ids=list(range(ncores)))
    return np.stack([np.asarray(r["y"]) for r in res.results], axis=0)


def kernel(**inputs):
    return run(inputs, 4096, list(range(DEPTH)), 8, final_norm=True).astype(np.float32)
```
